# Optimizing a Trainium2 kernel written in Bass

```python
import math
import jax, jax.numpy as jnp
from jax import lax
import numpy as np

D_MODEL = 1024
BATCH = 8
SEQ = 4096
DEPTH = 2

N_META = 16
EPS = 1e-6
A_HEADS = 4
A_WIDTH = D_MODEL
A_HEAD_DIM = A_WIDTH // A_HEADS
A_CHUNK = 64
B_WIDTH = D_MODEL
B_BLOCKS = 8
B_BLOCK_DIM = B_WIDTH // B_BLOCKS
B_CONV = 4
LRU_C = 8.0
C_HEADS = D_MODEL // 128
C_QK_DIM = D_MODEL // (2 * C_HEADS)
C_V_DIM = 2 * C_QK_DIM
ROT_DIM = C_QK_DIM // 4
ROPE_THETA = 500000.0
Q_BLOCK = 128
D_FF = 4 * D_MODEL
N_AB = (DEPTH + 1) // 2
N_C = DEPTH // 2
AB_SPLITS = [A_WIDTH, A_WIDTH, A_WIDTH, A_WIDTH, A_HEADS, A_HEADS, B_WIDTH, B_WIDTH]
AB_IN = sum(AB_SPLITS)
C_SPLITS = [C_HEADS * 2 * C_QK_DIM, C_HEADS * 2 * C_QK_DIM, C_HEADS * C_V_DIM]
C_IN = sum(C_SPLITS)

kernel_name = "hybrid_mlstm_rglru_diffattn_trunk"


def _rmsnorm(x, g):
    xf = x.astype(jnp.float32)
    y = xf * lax.rsqrt(jnp.mean(xf * xf, axis=-1, keepdims=True) + EPS)
    return (y * g.astype(jnp.float32)).astype(x.dtype)


def _split(a, sizes):
    return jnp.split(a, np.cumsum(sizes)[:-1].tolist(), axis=-1)


def _mlstm_chunk(state, inp):
    c_mem, n_mem, m_prev = state
    q, k, v, log_i, log_f = inp
    L = q.shape[2]
    b = jnp.cumsum(log_f, axis=-1)
    causal = jnp.tril(jnp.ones((L, L), dtype=bool))
    log_d = b[..., :, None] - b[..., None, :] + log_i[..., None, :]
    log_d = jnp.where(causal, log_d, -jnp.inf)
    log_prev = b + m_prev[..., None]
    m_t = jnp.maximum(log_prev, jnp.max(log_d, axis=-1))
    d = jnp.exp(log_d - m_t[..., None])
    w_prev = jnp.exp(log_prev - m_t)
    s = jnp.einsum('bhtd,bhsd->bhts', q, k) * d
    num = (w_prev[..., None] * jnp.einsum('bhvk,bhtk->bhtv', c_mem, q)
           + jnp.einsum('bhts,bhsv->bhtv', s, v))
    den = w_prev * jnp.einsum('bhk,bhtk->bht', n_mem, q) + jnp.sum(s, axis=-1)
    h = num / jnp.maximum(jnp.abs(den), jnp.exp(-m_t))[..., None]
    log_end = b[..., -1:] - b + log_i
    m_new = jnp.maximum(b[..., -1] + m_prev, jnp.max(log_end, axis=-1))
    w_s = jnp.exp(log_end - m_new[..., None])
    decay = jnp.exp(b[..., -1] + m_prev - m_new)
    c_new = decay[..., None, None] * c_mem + jnp.einsum('bhs,bhsv,bhsk->bhvk', w_s, v, k)
    n_new = decay[..., None] * n_mem + jnp.einsum('bhs,bhsk->bhk', w_s, k)
    return (c_new, n_new, m_new), h


def _mlstm(q, k, v, i_pre, f_pre):
    bsz, t_all, nh, dh = q.shape
    seq = t_all - N_META
    nc = seq // A_CHUNK
    q = q.transpose(0, 2, 1, 3)
    k = k.transpose(0, 2, 1, 3) * (dh ** -0.5)
    v = v.transpose(0, 2, 1, 3)
    log_i = i_pre.transpose(0, 2, 1)
    log_f = jax.nn.log_sigmoid(f_pre).transpose(0, 2, 1)
    state = (jnp.zeros((bsz, nh, dh, dh), jnp.float32),
             jnp.zeros((bsz, nh, dh), jnp.float32),
             jnp.zeros((bsz, nh), jnp.float32))
    state, h_meta = _mlstm_chunk(
        state, (q[:, :, :N_META], k[:, :, :N_META], v[:, :, :N_META],
                log_i[:, :, :N_META], log_f[:, :, :N_META]))

    def chunks(a):
        a = a[:, :, N_META:]
        return jnp.moveaxis(a.reshape(bsz, nh, nc, A_CHUNK, *a.shape[3:]), 2, 0)

    _, h_rest = lax.scan(_mlstm_chunk, state,
                         (chunks(q), chunks(k), chunks(v), chunks(log_i), chunks(log_f)))
    h_rest = jnp.moveaxis(h_rest, 0, 2).reshape(bsz, nh, seq, dh)
    h = jnp.concatenate([h_meta, h_rest], axis=2)
    return h.transpose(0, 2, 1, 3)


def _rg_lru_branch(xb, gate, conv_w, conv_b, w_r, b_r, w_i, b_i, lam):
    bsz, t_all, w = xb.shape
    xc = lax.conv_general_dilated(
        xb.astype(jnp.float32), conv_w.astype(jnp.float32)[:, None, :],
        window_strides=(1,), padding=[(B_CONV - 1, 0)],
        dimension_numbers=('NWC', 'WIO', 'NWC'), feature_group_count=w)
    xc = xc + conv_b.astype(jnp.float32)
    xg = xc.reshape(bsz, t_all, B_BLOCKS, B_BLOCK_DIM)
    r = jax.nn.sigmoid(jnp.einsum('btnd,nde->btne', xg, w_r.astype(jnp.float32)).reshape(bsz, t_all, w)
                       + b_r.astype(jnp.float32))
    i = jax.nn.sigmoid(jnp.einsum('btnd,nde->btne', xg, w_i.astype(jnp.float32)).reshape(bsz, t_all, w)
                       + b_i.astype(jnp.float32))
    log_a = -LRU_C * r * jax.nn.softplus(-lam.astype(jnp.float32))
    a = jnp.exp(log_a)
    u = jnp.sqrt(-jnp.expm1(2.0 * log_a)) * (i * xc)

    def combine(left, right):
        a1, b1 = left
        a2, b2 = right
        return a1 * a2, a2 * b1 + b2

    _, h = lax.associative_scan(combine, (a, u), axis=1)
    return h * jax.nn.gelu(gate.astype(jnp.float32))


def _ab_mixer(hn, w_in, if_bias, mlstm_norm, conv_w, conv_b, w_r, b_r, w_i, b_i, lam, w_out):
    bsz, t_all, _ = hn.shape
    proj = hn @ w_in
    q, k, v, o, gi, gf, xb, gate = _split(proj, AB_SPLITS)
    f32 = jnp.float32
    hs = (bsz, t_all, A_HEADS, A_HEAD_DIM)
    ifb = if_bias.astype(f32)
    h_a = _mlstm(q.astype(f32).reshape(hs), k.astype(f32).reshape(hs), v.astype(f32).reshape(hs),
                 gi.astype(f32) + ifb[:A_HEADS], gf.astype(f32) + ifb[A_HEADS:])
    h_a = jax.nn.sigmoid(o.astype(f32)).reshape(hs) * h_a
    h_a = h_a * lax.rsqrt(jnp.mean(h_a * h_a, axis=-1, keepdims=True) + EPS)
    h_a = (h_a * mlstm_norm.astype(f32).reshape(A_HEADS, A_HEAD_DIM)).reshape(bsz, t_all, A_WIDTH)
    h_b = _rg_lru_branch(xb, gate, conv_w, conv_b, w_r, b_r, w_i, b_i, lam)
    y = jnp.concatenate([h_a, h_b], axis=-1).astype(hn.dtype)
    return y @ w_out


def _partial_rope(x, cos, sin):
    half = ROT_DIM // 2
    x1, x2, rest = x[..., :half], x[..., half:ROT_DIM], x[..., ROT_DIM:]
    c = cos[:, None, None, :]
    s = sin[:, None, None, :]
    return jnp.concatenate([x1 * c - x2 * s, x2 * c + x1 * s, rest], axis=-1)


def _diff_attend(qb, q_pos, kh, vh, k_pos, lam):
    s = jnp.einsum('bhcqd,bhckd->bhcqk', qb, kh)
    mask = k_pos[None, :] <= q_pos[:, None]
    p = jax.nn.softmax(jnp.where(mask, s, -jnp.inf), axis=-1)
    pd = p[:, :, 0] - lam * p[:, :, 1]
    return jnp.einsum('bhqk,bhkv->bhqv', pd, vh)


def _diff_attn(hn, w_in, lam_vecs, subln, w_out, lambda_init):
    bsz, t_all, _ = hn.shape
    seq = t_all - N_META
    f32 = jnp.float32
    q, k, v = _split(hn @ w_in, C_SPLITS)
    q = q.astype(f32).reshape(bsz, t_all, C_HEADS, 2, C_QK_DIM)
    k = k.astype(f32).reshape(bsz, t_all, C_HEADS, 2, C_QK_DIM)
    v = v.astype(f32).reshape(bsz, t_all, C_HEADS, C_V_DIM)
    pos = jnp.arange(t_all, dtype=jnp.int32)
    inv_freq = jnp.power(jnp.float32(ROPE_THETA), -jnp.arange(0, ROT_DIM, 2, dtype=f32) / ROT_DIM)
    ang = pos.astype(f32)[:, None] * inv_freq[None, :]
    cos, sin = jnp.cos(ang), jnp.sin(ang)
    q = _partial_rope(q, cos, sin) * (C_QK_DIM ** -0.5)
    k = _partial_rope(k, cos, sin)
    lv = lam_vecs.astype(f32)
    lam = jnp.exp(jnp.sum(lv[0] * lv[1])) - jnp.exp(jnp.sum(lv[2] * lv[3])) + lambda_init
    qh = q.transpose(0, 2, 3, 1, 4)
    kh = k.transpose(0, 2, 3, 1, 4)
    vh = v.transpose(0, 2, 1, 3)
    o_meta = _diff_attend(qh[:, :, :, :N_META], pos[:N_META], kh[:, :, :, :N_META],
                          vh[:, :, :N_META], pos[:N_META], lam)
    nb = seq // Q_BLOCK
    q_blocks = jnp.moveaxis(qh[:, :, :, N_META:].reshape(bsz, C_HEADS, 2, nb, Q_BLOCK, C_QK_DIM), 3, 0)
    pos_blocks = pos[N_META:].reshape(nb, Q_BLOCK)
    o_rest = lax.map(lambda a: _diff_attend(a[0], a[1], kh, vh, pos, lam), (q_blocks, pos_blocks))
    o_rest = jnp.moveaxis(o_rest, 0, 2).reshape(bsz, C_HEADS, seq, C_V_DIM)
    o = jnp.concatenate([o_meta, o_rest], axis=2)
    o = o * lax.rsqrt(jnp.mean(o * o, axis=-1, keepdims=True) + EPS) * subln.astype(f32)
    o = o * (1.0 - lambda_init)
    o = o.transpose(0, 2, 1, 3).reshape(bsz, t_all, C_HEADS * C_V_DIM).astype(hn.dtype)
    return o @ w_out


def _sq_relu_mlp(hn, w1, w2):
    a = jax.nn.relu(hn @ w1)
    return (a * a) @ w2


def setup_inputs(seed: int = 0) -> dict:
    key = jax.random.key(seed)
    ks = jax.random.split(key, 24)
    f32 = jnp.float32

    def nrm(k, shape, scale):
        return jax.random.normal(k, shape, f32) * scale

    x = nrm(ks[0], (BATCH, SEQ, D_MODEL), 1.0)
    meta_tokens = nrm(ks[1], (N_META, D_MODEL), 1.0)
    norm_mix = 1.0 + nrm(ks[2], (DEPTH, D_MODEL), 0.05)
    norm_mlp = 1.0 + nrm(ks[3], (DEPTH, D_MODEL), 0.05)
    norm_final = 1.0 + nrm(ks[4], (D_MODEL,), 0.05)
    ab_w_in = nrm(ks[5], (N_AB, D_MODEL, AB_IN), D_MODEL ** -0.5)
    i_b = -1.0 + nrm(ks[6], (N_AB, A_HEADS), 0.1)
    f_b = jnp.linspace(3.0, 6.0, A_HEADS, dtype=f32)[None, :] + nrm(ks[7], (N_AB, A_HEADS), 0.1)
    ab_if_bias = jnp.concatenate([i_b, f_b], axis=-1)
    mlstm_norm = 1.0 + nrm(ks[8], (N_AB, A_WIDTH), 0.05)
    lru_conv_w = nrm(ks[9], (N_AB, B_CONV, B_WIDTH), B_CONV ** -0.5)
    lru_conv_b = nrm(ks[10], (N_AB, B_WIDTH), 0.01)
    lru_w_r = nrm(ks[11], (N_AB, B_BLOCKS, B_BLOCK_DIM, B_BLOCK_DIM), B_BLOCK_DIM ** -0.5)
    lru_b_r = nrm(ks[12], (N_AB, B_WIDTH), 0.01)
    lru_w_i = nrm(ks[13], (N_AB, B_BLOCKS, B_BLOCK_DIM, B_BLOCK_DIM), B_BLOCK_DIM ** -0.5)
    lru_b_i = nrm(ks[14], (N_AB, B_WIDTH), 0.01)
    u = jax.random.uniform(ks[15], (N_AB, B_WIDTH), f32, 0.9, 0.999)
    p = u ** (1.0 / LRU_C)
    lru_lambda = jnp.log(p) - jnp.log1p(-p)
    ab_w_out = nrm(ks[16], (N_AB, A_WIDTH + B_WIDTH, D_MODEL), (A_WIDTH + B_WIDTH) ** -0.5)
    c_w_in = nrm(ks[17], (N_C, D_MODEL, C_IN), D_MODEL ** -0.5)
    c_lambda = nrm(ks[18], (N_C, 4, C_QK_DIM), 0.1)
    c_subln = 1.0 + nrm(ks[19], (N_C, C_V_DIM), 0.05)
    c_w_out = nrm(ks[20], (N_C, C_HEADS * C_V_DIM, D_MODEL), (C_HEADS * C_V_DIM) ** -0.5)
    mlp_w1 = nrm(ks[21], (DEPTH, D_MODEL, D_FF), D_MODEL ** -0.5)
    mlp_w2 = nrm(ks[22], (DEPTH, D_FF, D_MODEL), D_FF ** -0.5)
    return {"x": x, "meta_tokens": meta_tokens, "norm_mix": norm_mix, "norm_mlp": norm_mlp,
            "norm_final": norm_final, "ab_w_in": ab_w_in, "ab_if_bias": ab_if_bias,
            "mlstm_norm": mlstm_norm, "lru_conv_w": lru_conv_w, "lru_conv_b": lru_conv_b,
            "lru_w_r": lru_w_r, "lru_b_r": lru_b_r, "lru_w_i": lru_w_i, "lru_b_i": lru_b_i,
            "lru_lambda": lru_lambda, "ab_w_out": ab_w_out, "c_w_in": c_w_in,
            "c_lambda": c_lambda, "c_subln": c_subln, "c_w_out": c_w_out,
            "mlp_w1": mlp_w1, "mlp_w2": mlp_w2}


def reference(x, meta_tokens, norm_mix, norm_mlp, norm_final, ab_w_in, ab_if_bias,
              mlstm_norm, lru_conv_w, lru_conv_b, lru_w_r, lru_b_r, lru_w_i, lru_b_i,
              lru_lambda, ab_w_out, c_w_in, c_lambda, c_subln, c_w_out, mlp_w1, mlp_w2):
    bsz = x.shape[0]
    meta = jnp.broadcast_to(meta_tokens[None].astype(x.dtype), (bsz, N_META, x.shape[-1]))
    h = jnp.concatenate([meta, x], axis=1)
    for layer in range(DEPTH):
        j = layer // 2
        hn = _rmsnorm(h, norm_mix[layer])
        if layer % 2 == 0:
            h = h + _ab_mixer(hn, ab_w_in[j], ab_if_bias[j], mlstm_norm[j], lru_conv_w[j],
                              lru_conv_b[j], lru_w_r[j], lru_b_r[j], lru_w_i[j], lru_b_i[j],
                              lru_lambda[j], ab_w_out[j])
        else:
            lambda_init = 0.8 - 0.6 * math.exp(-0.3 * layer)
            h = h + _diff_attn(hn, c_w_in[j], c_lambda[j], c_subln[j], c_w_out[j], lambda_init)
        h = h + _sq_relu_mlp(_rmsnorm(h, norm_mlp[layer]), mlp_w1[layer], mlp_w2[layer])
    h = _rmsnorm(h, norm_final)
    return h[:, N_META:, :]
```

```python
import math
import contextlib
import numpy as np
import concourse.bass as bass
import concourse.mybir as mybir
from concourse.bass_utils import run_bass_kernel_spmd

F32 = mybir.dt.float32
BF16 = mybir.dt.bfloat16
AF = mybir.ActivationFunctionType
ALU = mybir.AluOpType
AX = mybir.AxisListType

T = 4112
NM = 16
D = 1024
EPS = 1e-6
TG = [(0, 16)] + [(16 + 512 * i, 512) for i in range(8)]
TT = [(0, 16)] + [(16 + 128 * i, 128) for i in range(32)]
TG2 = [(0, 16)] + [(16 + 256 * i, 256) for i in range(16)]
LAMBDA_INIT = 0.8 - 0.6 * math.exp(-0.3 * 1)
SEM_LIMIT = 24000
DUMMY_MM = 1
ARENA_BYTES = 192 * 1024

ENGS = ("pe", "act", "dve", "pool", "sp")


class Buf:
    __slots__ = ("w", "r")

    def __init__(self):
        self.w = None
        self.r = {}


class Sched:
    def __init__(self, nc, n_dma_sems=24):
        self.nc = nc
        self.ops = {e: [] for e in ENGS}
        self.cnt = {e: 0 for e in ENGS if e != "sp"}
        self.epoch = {e: 0 for e in ENGS if e != "sp"}
        self.seen = {e: {} for e in ENGS}
        self.n_dma = n_dma_sems
        self.dma_val = [0] * n_dma_sems
        self.dma_epoch = [0] * n_dma_sems
        self.dma_rr = 0
        self.keys = set()
        self.last = {}

    def _need(self, eng, tok, waits):
        if tok is None:
            return
        k, v = tok
        if k[0] == "pe" and eng == "pe":
            return
        if self.seen[eng].get(k, 0) >= v:
            return
        if waits.get(k, 0) < v:
            waits[k] = v

    def _deps(self, eng, reads, writes):
        waits = {}
        for b in reads:
            self._need(eng, b.w, waits)
        for b in writes:
            self._need(eng, b.w, waits)
            for t in b.r.items():
                self._need(eng, t, waits)
        for k, v in waits.items():
            self.seen[eng][k] = v
        return list(waits.items())

    def _mark(self, tok, reads, writes):
        for b in reads:
            if b.r.get(tok[0], 0) < tok[1]:
                b.r[tok[0]] = tok[1]
        for b in writes:
            b.w = tok
            b.r = {}
        self.last[tok[0]] = max(self.last.get(tok[0], 0), tok[1])

    def _next_tok(self, eng, advance):
        c, ep = self.cnt[eng], self.epoch[eng]
        if c >= SEM_LIMIT:
            c, ep = 0, ep + 1
        if advance:
            self.cnt[eng], self.epoch[eng] = c + 1, ep
        key = (eng, ep)
        self.keys.add(key)
        return (key, c + 1)

    def op(self, eng, meth, *args, reads=(), writes=(), inc=True, **kw):
        waits = self._deps(eng, reads, writes)
        tok = self._next_tok(eng, inc)
        self.ops[eng].append((waits, (meth, args, kw), tok[0] if inc else None, 1))
        self._mark(tok, reads, writes)
        return tok

    def dma(self, out, in_, reads=(), writes=(), q="sp", **kw):
        kw = dict(kw)
        kw["out"] = out
        kw["in_"] = in_
        i = self.dma_rr
        self.dma_rr = (i + 1) % self.n_dma
        waits = dict(self._deps(q, reads, writes))
        key = ("dma", i, self.dma_epoch[i])
        prev = self.dma_val[i]
        if prev and self.seen[q].get(key, 0) < prev:
            waits[key] = max(waits.get(key, 0), prev)
            self.seen[q][key] = prev
        if prev + 16 > SEM_LIMIT:
            self.dma_epoch[i] += 1
            self.dma_val[i] = 0
            key = ("dma", i, self.dma_epoch[i])
        self.dma_val[i] += 16
        self.keys.add(key)
        tok = (key, self.dma_val[i])
        self.ops[q].append((list(waits.items()), ("dma_start", (), kw), key, 16))
        self._mark(tok, reads, writes)
        return tok

    def barrier(self):
        for e in ENGS:
            waits = {}
            for k, v in self.last.items():
                self._need(e, (k, v), waits)
            for k, v in waits.items():
                self.seen[e][k] = v
            if waits:
                self.ops[e].append((list(waits.items()), None, None, 0))

    def emit(self):
        nc = self.nc
        with contextlib.ExitStack() as st:
            sems = {}
            for k in sorted(self.keys, key=str):
                sems[k] = st.enter_context(nc.semaphore("s_" + "_".join(str(x) for x in k)))
            block = st.enter_context(nc.Block())

            def run(engname):
                def body(e):
                    for waits, fn, post, amt in self.ops[engname]:
                        for k, v in waits:
                            e.wait_ge(sems[k], v)
                        if fn is None:
                            continue
                        ins = getattr(e, fn[0])(*fn[1], **fn[2])
                        if post is not None:
                            ins.then_inc(sems[post], amt)
                return body

            block.tensor(run("pe"))
            block.scalar(run("act"))
            block.vector(run("dve"))
            block.gpsimd(run("pool"))
            block.sync(run("sp"))


class Tl:
    __slots__ = ("ap", "b")

    def __init__(self, ap, b=None):
        self.ap = ap
        self.b = b if b is not None else Buf()

    def __getitem__(self, k):
        return self.ap[k]


class KB:
    def __init__(self, phases, ext_out=()):
        self.phases = set(phases)
        self.ext_out = set(ext_out)
        nc = self.nc = bass.Bass("TRN2", target_bir_lowering=False)
        self.S = Sched(nc)
        self.dram = {}
        self.meta = {}
        self.dbuf = {}
        self.in_names = []
        self.out_names = []

    def dt(self, name, shape, dtype, producer=None):
        if name in self.dram:
            return self.dram[name]
        if producer is None or producer not in self.phases:
            kind = "ExternalInput"
            self.in_names.append(name)
        elif name in self.ext_out:
            kind = "ExternalOutput"
            self.out_names.append(name)
        else:
            kind = "Internal"
        t = self.nc.dram_tensor(name, list(shape), dtype, kind=kind).ap()
        self.meta[name] = (list(shape), dtype)
        self.dram[name] = t
        self.dbuf[name] = Buf()
        return t

    def setup_sbuf(self):
        nc = self.nc
        self.persist = nc.alloc_sbuf_tensor("persist", [128, 3072], F32)
        self.poff = 0
        self.arena = nc.alloc_sbuf_tensor("arena", [128, ARENA_BYTES // 4], F32)
        self.aoff = 0
        self.ps = [Tl(nc.alloc_psum_tensor("ps%d" % i, [128, 512], F32)[:]) for i in range(8)]
        self.psi = 0
        self.psc = {}

    def _carve(self, base, off, shape, dtype):
        n = int(np.prod(shape[1:]))
        nb = n * (4 if dtype == F32 else 2)
        nb = (nb + 63) // 64 * 64
        if dtype == F32:
            v = base[0:shape[0], off // 4: off // 4 + n]
        else:
            v = base[0:shape[0], off // 4: off // 4 + (n + 1) // 2].bitcast(BF16)[:, 0:n]
        if len(shape) == 3:
            v = v.rearrange("p (a b) -> p a b", a=shape[1])
        elif len(shape) == 4:
            v = v.rearrange("p (a b c) -> p a b c", a=shape[1], b=shape[2])
        return v, nb

    def P(self, shape, dtype=F32):
        v, nb = self._carve(self.persist, self.poff, shape, dtype)
        self.poff += nb
        assert self.poff <= 3072 * 4, self.poff
        return Tl(v)

    def A(self, shape, dtype=F32):
        v, nb = self._carve(self.arena, self.aoff, shape, dtype)
        self.aoff += nb
        assert self.aoff <= ARENA_BYTES, (self.aoff, shape)
        return Tl(v)

    def reset_arena(self):
        self.S.barrier()
        self.aoff = 0

    def psum(self):
        p = self.ps[self.psi]
        self.psi = (self.psi + 1) % 8
        return p

    def psum_from(self, key, banks):
        c = self.psc.get(key, 0)
        self.psc[key] = c + 1
        return self.ps[banks[c % len(banks)]]

    def cast(self, i, out, in_, r, w):
        if i % 2 == 0:
            self.op("act", "activation", out=out, in_=in_, func=AF.Copy, r=r, w=w)
        else:
            self.op("pool", "tensor_copy", out, in_, r=r, w=w)

    def op(self, eng, meth, *args, r=(), w=(), inc=True, **kw):
        return self.S.op(eng, meth, *args, reads=[x.b if isinstance(x, Tl) else x for x in r],
                         writes=[x.b if isinstance(x, Tl) else x for x in w], inc=inc, **kw)

    def dma(self, out, in_, r=(), w=(), **kw):
        return self.S.dma(out, in_, reads=[x.b if isinstance(x, Tl) else x for x in r],
                          writes=[x.b if isinstance(x, Tl) else x for x in w], **kw)

    def mm(self, ps, pairs, r, n_out=None, out=None):
        o = out if out is not None else ps.ap
        for i, (l, rh) in enumerate(pairs):
            last = i == len(pairs) - 1
            self.op("pe", "matmul", o, l, rh, start=(i == 0), stop=last, r=r, w=[ps], inc=last)


def build(phases=None, ext_out=("out",)):
    ALL = ["A", "C", "D", "E", "F", "G", "H", "I", "J0", "J", "K2", "M"]
    if phases is None:
        phases = ALL
    kb = KB(phases, ext_out)
    nc, S = kb.nc, kb.S
    dt = kb.dt
    op, dma, mm = kb.op, kb.dma, kb.mm

    x = dt("x", [4096, D], F32)
    meta = dt("meta_tokens", [NM, D], F32)
    norm_mix = dt("norm_mix", [2, D], F32)
    norm_mlp = dt("norm_mlp", [2, D], F32)
    norm_final = dt("norm_final", [1, D], F32)
    ab_w_in = dt("ab_w_in", [D, 6152], F32)
    ab_if_bias = dt("ab_if_bias", [8, 1], F32)
    mlstm_norm = dt("mlstm_norm", [1, D], F32)
    lru_conv_w = dt("lru_conv_w", [4, D], F32)
    lru_conv_b = dt("lru_conv_b", [1, D], F32)
    lru_w_r = dt("lru_w_r", [8, 128, 128], F32)
    lru_b_r = dt("lru_b_r", [1, D], F32)
    lru_w_i = dt("lru_w_i", [8, 128, 128], F32)
    lru_b_i = dt("lru_b_i", [1, D], F32)
    lru_lambda = dt("lru_lambda", [1, D], F32)
    ab_w_out = dt("ab_w_out", [2048, D], F32)
    c_w_in = dt("c_w_in", [D, 3072], F32)
    c_lambda = dt("c_lambda", [1, 256], F32)
    c_subln = dt("c_subln", [1, 128], F32)
    c_w_out = dt("c_w_out", [D, D], F32)
    mlp_w1 = dt("mlp_w1", [2, D, 4096], F32)
    mlp_w2 = dt("mlp_w2", [2, 4096, D], F32)
    c_ident = dt("c_ident", [128, 128], F32)
    c_tri = dt("c_tri", [128, 128], F32)
    c_cos = dt("c_cos", [128, 33, 8], F32)
    c_sin = dt("c_sin", [128, 33, 8], F32)

    hT = [dt("hT%d" % i, [D, T], F32, p) for i, p in enumerate(["A", "G", "H", "J", "K2"])]
    qT = dt("qT", [D, T], BF16, "C")
    kT = dt("kT", [D, T], BF16, "C")
    ktok = dt("ktok", [T, D], BF16, "C")
    vtok = dt("vtok", [T, D], BF16, "C")
    otok = dt("otok", [T, D], BF16, "C")
    giT = dt("giT", [4, T], F32, "C")
    gfT = dt("gfT", [4, T], F32, "C")
    xbT = dt("xbT", [D, T], F32, "C")
    ggT = dt("ggT", [D, T], F32, "C")
    gprep = dt("gprep", [128, 33 * 8 + 33 * 4], F32, "D")
    yTa = dt("yTa", [D, T], BF16, "E")
    yTb = dt("yTb", [D, T], BF16, "F")
    qkT = dt("qkT", [2048, T], BF16, "I")
    vtok2 = dt("vtok2", [T, D], BF16, "I")
    oT = dt("oT", [D, T], BF16, "J0")
    w1s = [dt("w1s%d" % l, [D, 4096], BF16, p) for l, p in enumerate(["C", "J0"])]
    w2s = [dt("w2s%d" % l, [4096, D], BF16, p) for l, p in enumerate(["C", "J0"])]
    out = dt("out", [4096, D], F32, "M")
    DB = kb.dbuf

    kb.setup_sbuf()
    P, A = kb.P, kb.A

    def gen_precast(layer):
        stg = [A([128, 2048], F32) for _ in range(2)]
        stb = [A([128, 2048], BF16) for _ in range(2)]
        w1v = mlp_w1[layer].rearrange("(c p) n -> p c n", p=128)
        w2v = mlp_w2[layer].rearrange("(c p) n -> p c n", p=128)
        d1 = w1s[layer].rearrange("(c p) n -> p c n", p=128)
        d2 = w2s[layer].rearrange("(c p) n -> p c n", p=128)
        jobs = []
        for kc in range(8):
            for hf in range(2):
                jobs.append((w1v[:, kc, hf * 2048:(hf + 1) * 2048], d1[:, kc, hf * 2048:(hf + 1) * 2048], None, "w1s%d" % layer))
        for c2 in range(16):
            jobs.append((w2v[:, 2 * c2:2 * c2 + 2, :], d2[:, 2 * c2:2 * c2 + 2, :], 2, "w2s%d" % layer))
        prev = None
        for k, (src, dst, a3, dname) in enumerate(jobs):
            sg, sb_ = stg[k % 2], stb[k % 2]
            sv = sg.ap if a3 is None else sg.ap.rearrange("p (a b) -> p a b", a=a3)
            bv = sb_.ap if a3 is None else sb_.ap.rearrange("p (a b) -> p a b", a=a3)
            dma(sv, src, w=[sg])
            yield
            op("pool", "tensor_copy", sb_.ap, sg.ap, r=[sg], w=[sb_])
            yield
            if prev is not None:
                dma(prev[0], prev[1], r=[prev[2]], w=[DB[prev[3]]])
            prev = (dst, bv, sb_, dname)
            yield
        dma(prev[0], prev[1], r=[prev[2]], w=[DB[prev[3]]])
        yield

    def fm(t):
        return t.rearrange("(c p) t -> p c t", p=128)

    ident = P([128, 128], F32)
    identb = P([128, 128], BF16)
    tri = P([128, 128], F32)
    ones_b = P([128, 128], BF16)
    ones_f = P([128, 128], F32)
    epsc = P([128, 1], F32)
    gam = P([128, 8, 8], F32)
    lrup = P([128, 8, 8], F32)
    clam = P([128, 8], F32)
    dma(ident.ap, c_ident, w=[ident])
    dma(tri.ap, c_tri, w=[tri])
    op("dve", "tensor_copy", identb.ap, ident.ap, r=[ident], w=[identb])
    op("pool", "memset", ones_b.ap, 1.0 / 1024, w=[ones_b])
    op("pool", "memset", ones_f.ap, 1.0, w=[ones_f])
    op("pool", "memset", epsc.ap, EPS, w=[epsc])

    def load_colvecs(dst, rows):
        st = A([16, 1024], F32)
        for i, rr in enumerate(rows):
            dma(st.ap[i:i + 1, :], rr, w=[st])
        for kc in range(8):
            ps = kb.psum()
            op("pe", "transpose", ps.ap[:, 0:len(rows)], st.ap[0:len(rows), kc * 128:(kc + 1) * 128],
               ident.ap[0:len(rows), 0:len(rows)], r=[st, ident], w=[ps])
            op("dve", "tensor_copy", dst.ap[:, kc, 0:len(rows)], ps.ap[:, 0:len(rows)], r=[ps], w=[dst])

    load_colvecs(gam, [norm_mix[0:1, :], norm_mix[1:2, :], norm_mlp[0:1, :], norm_mlp[1:2, :], norm_final])
    load_colvecs(lrup, [lru_conv_w[j:j + 1, :] for j in range(4)] + [lru_conv_b, lru_b_r, lru_b_i, lru_lambda])
    kb.reset_arena()

    def norm_group(xt, n, gidx, hn_out, sq, rstd, out_dtype_f32=False):
        op("act", "activation", out=sq.ap[:, :, 0:n], in_=xt.ap[:, :, 0:n], func=AF.Square, r=[xt], w=[sq])
        ps = kb.psum()
        mm(ps, [(ones_b.ap, sq.ap[:, kc, 0:n]) for kc in range(8)], r=[ones_b, sq], out=ps.ap[:, 0:n])
        op("act", "activation", out=rstd.ap[:, 0:n], in_=ps.ap[:, 0:n], func=AF.Sqrt, bias=epsc.ap, r=[ps, epsc], w=[rstd])
        op("dve", "reciprocal", rstd.ap[:, 0:n], rstd.ap[:, 0:n], r=[rstd], w=[rstd])
        for kc in range(8):
            op("dve", "scalar_tensor_tensor", hn_out(kc), xt.ap[:, kc, 0:n], gam.ap[:, kc, gidx:gidx + 1],
               rstd.ap[:, 0:n], ALU.mult, ALU.mult, r=[xt, gam, rstd], w=[hn_out.tl])

    class HnOut:
        def __init__(self, tl, fn):
            self.tl, self.fn = tl, fn

        def __call__(self, kc):
            return self.fn(kc)

    if "A" in kb.phases:
        xin = [A([128, 4, D], F32) for _ in range(2)]
        stg = [A([128, 8, 512], F32) for _ in range(2)]
        for gi, (p0, n) in enumerate(TG):
            xi, sg = xin[gi % 2], stg[gi % 2]
            nt = max(1, n // 128)
            if gi == 0:
                dma(xi.ap[0:16, 0, :], meta, w=[xi])
            else:
                r0 = p0 - NM
                dma(xi.ap, x[r0:r0 + 512, :].rearrange("(j p) d -> p j d", p=128), w=[xi])
            for kc in range(8):
                ps = kb.psum()
                for j in range(nt):
                    m = min(n, 128)
                    op("pe", "transpose", ps.ap[:, j * 128:j * 128 + m], xi.ap[0:m, j, kc * 128:(kc + 1) * 128],
                       ident.ap[0:m, 0:m], r=[xi, ident], w=[ps], inc=(j == nt - 1))
                if kc % 2 == 0:
                    op("act", "activation", out=sg.ap[:, kc, 0:n], in_=ps.ap[:, 0:n], func=AF.Copy, r=[ps], w=[sg])
                else:
                    op("dve", "tensor_copy", sg.ap[:, kc, 0:n], ps.ap[:, 0:n], r=[ps], w=[sg])
            dma(fm(hT[0])[:, :, p0:p0 + n], sg.ap[:, :, 0:n], r=[sg], w=[DB["hT0"]])
        kb.reset_arena()

    def norm_full(src_name, gidx):
        hn = A([128, 8, T], BF16)
        hnb = [Buf() for _ in TG]
        mark = kb.aoff
        xts = [A([128, 8, 512], F32) for _ in range(2)]
        sqs = [A([128, 8, 512], BF16) for _ in range(2)]
        rs = [A([128, 512], F32) for _ in range(2)]
        src = fm(kb.dram[src_name])
        for gi, (p0, n) in enumerate(TG):
            xt = xts[gi % 2]
            dma(xt.ap[:, :, 0:n], src[:, :, p0:p0 + n], r=[DB[src_name]], w=[xt])
            ho = HnOut(Tl(hn.ap, hnb[gi]), lambda kc, p0=p0, n=n: hn.ap[:, kc, p0:p0 + n])
            norm_group(xt, n, gidx, ho, sqs[gi % 2], rs[gi % 2])
        S.barrier()
        kb.aoff = mark
        return hn, hnb

    if "C" in kb.phases:
        hn, hnb = norm_full("hT0", 0)
        wst = [A([128, 8, 512], F32) for _ in range(2)]
        wbf = [A([128, 8, 512], BF16) for _ in range(2)]
        ostg = [A([128, 4, 512], F32) for _ in range(2)]
        ostg_b = [Tl(o.ap.rearrange("p a b -> p (a b)").bitcast(BF16)[:, 0:2048].rearrange("p (a b) -> p a b", a=4), o.b) for o in ostg]
        tstg = [A([128, 512], BF16) for _ in range(3)]
        win = ab_w_in.rearrange("(c p) n -> p c n", p=128)
        pcast = gen_precast(0)
        jobs = []
        for name, c0 in (("q", 0), ("k", 1024), ("v", 2048), ("o", 3072), ("xb", 4104), ("gate", 5128)):
            for hb in range(2):
                jobs.append((name, c0 + 512 * hb, 512, hb))
        jobs.append(("gates", 4096, 8, 0))
        cnt = 0
        tcnt = 0
        for ji, (name, c0, ncol, hb) in enumerate(jobs):
            ws, wb = wst[ji % 2], wbf[ji % 2]
            dma(ws.ap[:, :, 0:ncol], win[:, :, c0:c0 + ncol], w=[ws])
            kb.cast(ji, wb.ap[:, :, 0:ncol], ws.ap[:, :, 0:ncol], [ws], [wb])
            if name == "gates":
                gstgs = [A([4, 2, 512], F32) for _ in range(2)]
                for gi, (p0, n) in enumerate(TG):
                    gstg = gstgs[gi % 2]
                    for half in range(2):
                        ps = kb.psum()
                        mm(ps, [(wb.ap[:, kc, 4 * half:4 * half + 4], hn.ap[:, kc, p0:p0 + n]) for kc in range(8)],
                           r=[wb, hnb[gi]], out=ps.ap[0:4, 0:n])
                        op("dve", "tensor_copy", gstg.ap[:, half, 0:n], ps.ap[0:4, 0:n], r=[ps], w=[gstg])
                    dma(giT[:, p0:p0 + n], gstg.ap[:, 0, 0:n], r=[gstg], w=[DB["giT"]])
                    dma(gfT[:, p0:p0 + n], gstg.ap[:, 1, 0:n], r=[gstg], w=[DB["gfT"]])
                continue
            if name in ("q", "k", "xb", "gate"):
                dst_name = {"q": "qT", "k": "kT", "xb": "xbT", "gate": "ggT"}[name]
                isb = name in ("q", "k")
                for gi, (p0, n) in enumerate(TG):
                    next(pcast, None)
                    og = (ostg_b if isb else ostg)[cnt % 2]
                    cnt += 1
                    for oc in range(4):
                        ps = kb.psum()
                        mm(ps, [(wb.ap[:, kc, oc * 128:(oc + 1) * 128], hn.ap[:, kc, p0:p0 + n]) for kc in range(8)],
                           r=[wb, hnb[gi]], out=ps.ap[:, 0:n])
                        if name == "gate":
                            op("act", "activation", out=og.ap[:, oc, 0:n], in_=ps.ap[:, 0:n], func=AF.Gelu_apprx_tanh, r=[ps], w=[og])
                        elif oc % 2 == 0:
                            op("act", "activation", out=og.ap[:, oc, 0:n], in_=ps.ap[:, 0:n], func=AF.Copy, r=[ps], w=[og])
                        else:
                            op("dve", "tensor_copy", og.ap[:, oc, 0:n], ps.ap[:, 0:n], r=[ps], w=[og])
                    dma(fm(kb.dram[dst_name])[:, 4 * hb:4 * hb + 4, p0:p0 + n], og.ap[:, :, 0:n], r=[og], w=[DB[dst_name]])
            if name in ("k", "v", "o"):
                dst_name = {"k": "ktok", "v": "vtok", "o": "otok"}[name]
                for ti, (p0, n) in enumerate(TT):
                    if ti % 4 == 0:
                        next(pcast, None)
                    tg_i = 0 if ti == 0 else 1 + (ti - 1) // 4
                    ts_ = tstg[tcnt % 3]
                    tcnt += 1
                    ps = kb.psum()
                    mm(ps, [(hn.ap[:, kc, p0:p0 + n], wb.ap[:, kc, 0:512]) for kc in range(8)],
                       r=[wb, hnb[tg_i]], out=ps.ap[0:n, :])
                    if name == "o":
                        op("act", "activation", out=ts_.ap[0:n, :], in_=ps.ap[0:n, :], func=AF.Sigmoid, r=[ps], w=[ts_])
                    elif ti % 2 == 0:
                        op("act", "activation", out=ts_.ap[0:n, :], in_=ps.ap[0:n, :], func=AF.Copy, r=[ps], w=[ts_])
                    else:
                        op("dve", "tensor_copy", ts_.ap[0:n, :], ps.ap[0:n, :], r=[ps], w=[ts_])
                    dma(kb.dram[dst_name][p0:p0 + n, 512 * hb:512 * hb + 512], ts_.ap[0:n, :], r=[ts_], w=[DB[dst_name]])
        for _ in pcast:
            pass
        kb.reset_arena()

    NB = 33
    if "D" in kb.phases:
        gi_ = A([4, T], F32); gf_ = A([4, T], F32)
        t2 = A([4, T], F32); t3 = A([4, T], F32); Bc = A([4, T], F32)
        onesr = A([4, 1], F32)
        bi = A([4, 1], F32); bfb = A([4, 1], F32)
        Rb = A([4, NB + 1], F32)
        dec = A([4, NB], F32); decd = A([4, 4, NB], F32)
        gout = A([128, NB * 8 + NB * 4], F32)
        dma(gi_.ap, giT, r=[DB["giT"]], w=[gi_])
        dma(gf_.ap, gfT, r=[DB["gfT"]], w=[gf_])
        dma(bi.ap, ab_if_bias[0:4, :], w=[bi])
        dma(bfb.ap, ab_if_bias[4:8, :], w=[bfb])
        op("pool", "memset", onesr.ap, 1.0, w=[onesr])
        op("pool", "memset", Rb.ap, 0.0, w=[Rb])
        op("pool", "memset", gout.ap, 0.0, w=[gout])
        op("dve", "tensor_scalar", gf_.ap, gf_.ap, bfb.ap, None, ALU.add, r=[gf_, bfb], w=[gf_])
        op("act", "activation", out=t2.ap, in_=gf_.ap, func=AF.Abs, r=[gf_], w=[t2])
        op("act", "activation", out=t2.ap, in_=t2.ap, func=AF.Exp, scale=-1.0, r=[t2], w=[t2])
        op("act", "activation", out=t2.ap, in_=t2.ap, func=AF.Ln, bias=1.0, r=[t2], w=[t2])
        op("dve", "tensor_scalar", t3.ap, gf_.ap, 0.0, None, ALU.min, r=[gf_], w=[t3])
        op("dve", "tensor_tensor", t3.ap, t3.ap, t2.ap, ALU.subtract, r=[t3, t2], w=[t3])
        op("dve", "tensor_tensor_scan", Bc.ap, onesr.ap.to_broadcast([4, T]), t3.ap, 0.0, ALU.mult, ALU.add, r=[onesr, t3], w=[Bc])
        G = gi_; Mx = t2; Rt = t3; beta = gf_; flo = Bc
        op("dve", "scalar_tensor_tensor", G.ap, gi_.ap, bi.ap, Bc.ap, ALU.add, ALU.subtract, r=[gi_, bi, Bc], w=[G])
        op("dve", "tensor_tensor_scan", Mx.ap, G.ap, G.ap, 0.0, ALU.max, ALU.max, r=[G], w=[Mx])
        op("dve", "tensor_copy", Rb.ap[:, 1:2], Mx.ap[:, 15:16], r=[Mx], w=[Rb])
        op("dve", "tensor_copy", Rb.ap[:, 2:NB + 1], Mx.ap[:, 16:T].rearrange("p (b s) -> p b s", s=128)[:, :, 127], r=[Mx], w=[Rb])
        op("dve", "tensor_copy", Rt.ap[:, 0:16], Rb.ap[:, 1:2].to_broadcast([4, 16]), r=[Rb], w=[Rt])
        op("dve", "tensor_copy", Rt.ap[:, 16:T].rearrange("p (b s) -> p b s", s=128),
           Rb.ap[:, 2:NB + 1].unsqueeze(2).to_broadcast([4, 32, 128]), r=[Rb], w=[Rt])
        op("dve", "tensor_tensor", beta.ap, G.ap, Rt.ap, ALU.subtract, r=[G, Rt], w=[beta])
        op("act", "activation", out=beta.ap, in_=beta.ap, func=AF.Exp, r=[beta], w=[beta])
        op("dve", "tensor_scalar", beta.ap, beta.ap, 1.0 / 16, None, ALU.mult, r=[beta], w=[beta])
        op("dve", "tensor_tensor", flo.ap, Bc.ap, Rt.ap, ALU.add, r=[Bc, Rt], w=[flo])
        op("act", "activation", out=flo.ap, in_=flo.ap, func=AF.Exp, scale=-1.0, r=[flo], w=[flo])
        op("dve", "tensor_tensor", dec.ap, Rb.ap[:, 0:NB], Rb.ap[:, 1:NB + 1], ALU.subtract, r=[Rb], w=[dec])
        op("act", "activation", out=dec.ap, in_=dec.ap, func=AF.Exp, r=[dec], w=[dec])
        op("dve", "tensor_tensor", decd.ap, dec.ap.unsqueeze(1).to_broadcast([4, 4, NB]),
           ident.ap[0:4, 0:4].unsqueeze(2).to_broadcast([4, 4, NB]), ALU.mult, r=[dec, ident], w=[decd])
        ps = kb.psum()
        for ti, (p0, n) in enumerate(TT):
            op("pe", "transpose", ps.ap[0:n, ti * 8:ti * 8 + 4], beta.ap[:, p0:p0 + n], ident.ap[0:4, 0:4], r=[beta, ident], w=[ps], inc=False)
            op("pe", "transpose", ps.ap[0:n, ti * 8 + 4:ti * 8 + 8], flo.ap[:, p0:p0 + n], ident.ap[0:4, 0:4], r=[flo, ident], w=[ps], inc=(ti == NB - 1))
        op("dve", "tensor_copy", gout.ap[:, 8:NB * 8], ps.ap[:, 8:NB * 8], r=[ps], w=[gout])
        op("dve", "tensor_copy", gout.ap[0:16, 0:8], ps.ap[0:16, 0:8], r=[ps], w=[gout])
        ps2 = kb.psum()
        mm(ps2, [(ones_f.ap[0:4, :], decd.ap.rearrange("p a b -> p (a b)"))], r=[ones_f, decd], out=ps2.ap[:, 0:4 * NB])
        op("dve", "tensor_copy", gout.ap[:, NB * 8:NB * 12], ps2.ap[:, 0:4 * NB], r=[ps2], w=[gout])
        dma(gprep, gout.ap, r=[gout], w=[DB["gprep"]])
        kb.reset_arena()

    genE = genF = None
    if "E" in kb.phases:
        gp = A([128, NB * 12], F32)
        dma(gp.ap, gprep, r=[DB["gprep"]], w=[gp])
        bfv = gp.ap[:, 0:NB * 8].rearrange("p (b e) -> p b e", e=8)
        decv = gp.ap[:, NB * 8:NB * 12].rearrange("p (h b) -> p h b", h=4)
        gnb = A([128, D], F32)
        dma(gnb.ap, mlstm_norm.partition_broadcast(128), w=[gnb])
        Cst = A([128, 4, 2, 257], F32)
        Cd = A([128, 4, 2, 257], BF16)
        CstB = [Buf() for _ in range(4)]
        CdB = [Buf() for _ in range(4)]
        NBUF = 2
        qTb = [A([128, 8, 128], BF16) for _ in range(NBUF)]
        kTb = [A([128, 8, 128], BF16) for _ in range(NBUF)]
        ktb = [A([128, D], BF16) for _ in range(NBUF)]
        otb = [A([128, D], BF16) for _ in range(NBUF)]
        vab = [A([128, 4, 257], BF16) for _ in range(NBUF)]
        for v_ in vab:
            op("pool", "memset", v_.ap[:, :, 256:257], 1.0, w=[v_])
        Sm = [A([128, 128], BF16) for _ in range(4)]
        ktl = [A([128, 256], BF16) for _ in range(4)]
        ha = [A([128, 256], F32) for _ in range(2)]
        junk = A([128, 256], F32)
        sm1 = [A([128, 4], F32) for _ in range(2)]
        yst = [A([128, 8, 128], BF16) for _ in range(2)]
        yrow = fm(yTa)
        it = 0

        def loads(b):
            p0, n = TT[b]
            s = b % NBUF
            dma(qTb[s].ap[:, :, 0:n], fm(qT)[:, :, p0:p0 + n], r=[DB["qT"]], w=[qTb[s]])
            dma(kTb[s].ap[:, :, 0:n], fm(kT)[:, :, p0:p0 + n], r=[DB["kT"]], w=[kTb[s]])
            dma(ktb[s].ap[0:n, :], ktok[p0:p0 + n, :], r=[DB["ktok"]], w=[ktb[s]])
            dma(otb[s].ap[0:n, :], otok[p0:p0 + n, :], r=[DB["otok"]], w=[otb[s]])
            dma(vab[s].ap[0:n, :, 0:256], vtok[p0:p0 + n, :].rearrange("t (h v) -> t h v", h=4), r=[DB["vtok"]], w=[vab[s]])

        loads(0)
        han = [A([128, 256], BF16) for _ in range(4)]
        flat = [(b, h) for b in range(NB) for h in range(4)]
        pS_slots = [kb.ps[0]] * 4

        def s1(i):
            b, h = flat[i]
            p0, n = TT[b]
            s = b % NBUF
            i2 = i % 4
            beta_c = bfv[0:n, b, h:h + 1]
            pS = pS_slots[i % 4]
            c0 = (i % 4) * 128
            mm(pS, [(kTb[s].ap[:, 2 * h + dc, 0:n], qTb[s].ap[:, 2 * h + dc, 0:n]) for dc in range(2)],
               r=[kTb[s], qTb[s]], out=pS.ap[0:n, c0:c0 + n])
            op("dve", "scalar_tensor_tensor", Sm[i2].ap[0:n, 0:n], pS.ap[0:n, c0:c0 + n], beta_c, tri.ap[0:n, 0:n],
               ALU.mult, ALU.mult, r=[pS, gp, tri], w=[Sm[i2]])
            op("act", "activation", out=ktl[i2].ap[0:n, :], in_=ktb[s].ap[0:n, h * 256:(h + 1) * 256], func=AF.Copy,
               scale=beta_c, r=[ktb[s], gp], w=[ktl[i2]])

        def s2(i):
            b, h = flat[i]
            p0, n = TT[b]
            s = b % NBUF
            i2 = i % 2
            i4 = i % 4
            cstT = Tl(Cst.ap, CstB[h]); cdT = Tl(Cd.ap, CdB[h])
            floor_c = bfv[0:n, b, 4 + h:5 + h]
            pN = kb.ps[1 + i % 2]
            pairs = []
            if b > 0:
                pairs += [(qTb[s].ap[:, 2 * h + dc, 0:n], Cd.ap[:, h, dc, :]) for dc in range(2)]
            pairs.append((Sm[i4].ap[0:n, 0:n], vab[s].ap[0:n, h, :]))
            mm(pN, pairs, r=[qTb[s], cdT, Sm[i4], vab[s]], out=pN.ap[0:n, 0:257])
            pC = [kb.ps[3 + (i % 2) * 2 + dc] for dc in range(2)]
            for dc in range(2):
                mm(pC[dc], [(ktl[i4].ap[0:n, dc * 128:(dc + 1) * 128], vab[s].ap[0:n, h, :])], r=[ktl[i4], vab[s]],
                   out=pC[dc].ap[:, 0:257])
            yield
            for dc in range(2):
                if b == 0:
                    op("dve", "tensor_copy", Cst.ap[:, h, dc, :], pC[dc].ap[:, 0:257], r=[pC[dc]], w=[cstT])
                else:
                    op("dve", "scalar_tensor_tensor", Cst.ap[:, h, dc, :], Cst.ap[:, h, dc, :], decv[:, h, b:b + 1],
                       pC[dc].ap[:, 0:257], ALU.mult, ALU.add, r=[pC[dc], gp, cstT], w=[cstT])
            if b + 1 < NB:
                op("act", "activation", out=Cd.ap[:, h, :, :], in_=Cst.ap[:, h, :, :], func=AF.Copy,
                   scale=decv[:, h, b + 1:b + 2], r=[cstT, gp], w=[cdT])
            sm = sm1[i2]
            yield
            op("act", "activation", out=sm.ap[0:n, 0:1], in_=pN.ap[0:n, 256:257], func=AF.Abs, r=[pN], w=[sm])
            yield
            op("dve", "tensor_scalar", sm.ap[0:n, 0:1], sm.ap[0:n, 0:1], floor_c, None, ALU.max, r=[sm, gp], w=[sm])
            op("dve", "reciprocal", sm.ap[0:n, 0:1], sm.ap[0:n, 0:1], r=[sm], w=[sm])
            yield
            op("dve", "scalar_tensor_tensor", ha[i2].ap[0:n, :], pN.ap[0:n, 0:256], sm.ap[0:n, 0:1],
               otb[s].ap[0:n, h * 256:(h + 1) * 256], ALU.mult, ALU.mult, r=[pN, sm, otb[s]], w=[ha[i2]])
            yield
            op("act", "activation", out=junk.ap[0:n, :], in_=ha[i2].ap[0:n, :], func=AF.Square, accum_out=sm.ap[0:n, 1:2],
               r=[ha[i2]], w=[sm])
            op("act", "activation", out=sm.ap[0:n, 2:3], in_=sm.ap[0:n, 1:2], func=AF.Sqrt, scale=1.0 / 256, bias=epsc.ap[0:n, :],
               r=[sm, epsc], w=[sm])
            yield
            op("dve", "reciprocal", sm.ap[0:n, 2:3], sm.ap[0:n, 2:3], r=[sm], w=[sm])
            hn_ = han[i % 4]
            op("dve", "scalar_tensor_tensor", hn_.ap[0:n, :], ha[i2].ap[0:n, :], sm.ap[0:n, 2:3],
               gnb.ap[0:n, h * 256:(h + 1) * 256], ALU.mult, ALU.mult, r=[ha[i2], sm, gnb], w=[hn_])

        def s4(i):
            b, h = flat[i]
            p0, n = TT[b]
            ys = yst[b % 2]
            hn_ = han[i % 4]
            pT = kb.ps[7]
            pTb = pT.ap.bitcast(BF16)
            for vc in range(2):
                op("pe", "transpose", pTb[:, vc * 128:vc * 128 + n], hn_.ap[0:n, vc * 128:(vc + 1) * 128],
                   identb.ap[0:n, 0:n], r=[hn_, identb], w=[pT], inc=(vc == 1))
            op("act", "activation", out=ys.ap[:, 2 * h:2 * h + 2, 0:n],
               in_=pTb[:, 0:256].rearrange("p (a b) -> p a b", a=2)[:, :, 0:n], func=AF.Copy, r=[pT], w=[ys])
            if h == 3:
                dma(yrow[:, 0:8, p0:p0 + n], ys.ap[:, :, 0:n], r=[ys], w=[DB["yTa"]])

        NF = len(flat)

        def genE_():
            s1(0)
            s1(1)
            for i0_ in range(0, NF, 2):
                b, h = flat[i0_]
                if h == 0 and b + 1 < NB:
                    loads(b + 1)
                for j in (i0_ + 2, i0_ + 3):
                    if j < NF:
                        s1(j)
                gs = [s2(i0_), s2(i0_ + 1)]
                while gs:
                    for g_ in list(gs):
                        try:
                            next(g_)
                        except StopIteration:
                            gs.remove(g_)
                for j in (i0_ - 2, i0_ - 1):
                    if j >= 0:
                        s4(j)
                yield
            s4(NF - 2)
            s4(NF - 1)
            yield
        genE = genE_()

    if "F" in kb.phases:
        t8 = A([128, 8], F32); t8b = A([128, 8], F32)
        lam_v = lrup.ap[:, :, 7]
        op("act", "activation", out=t8.ap, in_=lam_v, func=AF.Abs, r=[lrup], w=[t8])
        op("act", "activation", out=t8.ap, in_=t8.ap, func=AF.Exp, scale=-1.0, r=[t8], w=[t8])
        op("act", "activation", out=t8.ap, in_=t8.ap, func=AF.Ln, bias=1.0, r=[t8], w=[t8])
        op("dve", "tensor_scalar", t8b.ap, lam_v, 0.0, None, ALU.min, r=[lrup], w=[t8b])
        op("dve", "tensor_tensor", t8b.ap, t8b.ap, t8.ap, ALU.subtract, r=[t8, t8b], w=[t8b])
        op("dve", "tensor_scalar", clam.ap, t8b.ap, 8.0, None, ALU.mult, r=[t8b], w=[clam])
        wri_s = A([128, 16, 128], F32)
        wri = A([128, 16, 128], BF16)
        dma(wri_s.ap[:, 0:8, :], lru_w_r.rearrange("n d e -> d n e"), w=[wri_s])
        dma(wri_s.ap[:, 8:16, :], lru_w_i.rearrange("n d e -> d n e"), w=[wri_s])
        op("pool", "tensor_copy", wri.ap, wri_s.ap, r=[wri_s], w=[wri])
        xb = A([128, T], F32); gg = A([128, T], F32)
        xc = A([128, T], F32); xcb = A([128, T], BF16)
        r_ = A([128, T], F32); i_ = A([128, T], F32); s_ = A([128, T], F32)
        hb_ = [A([128, T], BF16) for _ in range(2)]

        clam2 = A([128, 8], F32)
        op("dve", "tensor_scalar", clam2.ap, clam.ap, 2.0, None, ALU.mult, r=[clam], w=[clam2])
        NP = len(TG)
        pb = {nm: [Buf() for _ in range(NP)] for nm in ("xc", "xcb", "r", "i", "s", "hb0", "hb1")}

        def genF_():
            dma(xb.ap, xbT[0:128, :], r=[DB["xbT"]], w=[xb])
            dma(gg.ap, ggT[0:128, :], r=[DB["ggT"]], w=[gg])

            def tl(n_, gi):
                hb = hb_[n_ % 2]
                return (Tl(xc.ap, pb["xc"][gi]), Tl(xcb.ap, pb["xcb"][gi]), Tl(r_.ap, pb["r"][gi]),
                        Tl(i_.ap, pb["i"][gi]), Tl(s_.ap, pb["s"][gi]), Tl(hb.ap, pb["hb%d" % (n_ % 2)][gi]))

            def stage1(n_, gi):
                p0, n = TG[gi]
                pv = lambda i: lrup.ap[:, n_, i:i + 1]
                sl = slice(p0, p0 + n)
                xcT, xcbT, rT, iT, sT, hbT = tl(n_, gi)
                op("dve", "tensor_scalar", xc.ap[:, sl], xb.ap[:, sl], pv(3), pv(4), ALU.mult, ALU.add, r=[xb, lrup], w=[xcT])
                for j in (1, 2, 3):
                    lo = max(p0, j)
                    op("dve", "scalar_tensor_tensor", xc.ap[:, lo:p0 + n], xb.ap[:, lo - j:p0 + n - j], pv(3 - j), xc.ap[:, lo:p0 + n],
                       ALU.mult, ALU.add, r=[xb, lrup, xcT], w=[xcT])
                op("act", "activation", out=xcb.ap[:, sl], in_=xc.ap[:, sl], func=AF.Copy, r=[xcT], w=[xcbT])
                for which, dst, dT, bidx in ((0, r_, rT, 5), (1, i_, iT, 6)):
                    ps = kb.psum_from("F", [5, 6])
                    mm(ps, [(wri.ap[:, which * 8 + n_, :], xcb.ap[:, sl])], r=[wri, xcbT], out=ps.ap[:, 0:n])
                    op("act", "activation", out=dst.ap[:, sl], in_=ps.ap[:, 0:n], func=AF.Sigmoid, bias=pv(bidx),
                       r=[ps, lrup], w=[dT])

            def stage4(n_, gi):
                p0, n = TG[gi]
                sl = slice(p0, p0 + n)
                hb = hb_[n_ % 2]
                xcT, xcbT, rT, iT, sT, hbT = tl(n_, gi)
                op("pool", "tensor_tensor", i_.ap[:, sl], i_.ap[:, sl], xc.ap[:, sl], ALU.mult, r=[iT, xcT], w=[iT])
                op("pool", "tensor_tensor", i_.ap[:, sl], i_.ap[:, sl], s_.ap[:, sl], ALU.mult, r=[iT, sT], w=[iT])
                init = 0.0 if gi == 0 else s_.ap[:, p0 - 1:p0]
                rd = [rT, iT] + ([Tl(s_.ap, pb["s"][gi - 1])] if gi > 0 else [])
                op("dve", "tensor_tensor_scan", s_.ap[:, sl], r_.ap[:, sl], i_.ap[:, sl], init, ALU.mult, ALU.add, r=rd, w=[sT])
                op("pool", "tensor_tensor", hb.ap[:, sl], s_.ap[:, sl], gg.ap[:, sl], ALU.mult, r=[sT, gg], w=[hbT])

            for gi in range(NP):
                stage1(0, gi)
            yield
            for n_ in range(8):
                if n_ + 1 < 8:
                    dma(xb.ap, xbT[(n_ + 1) * 128:(n_ + 2) * 128, :], r=[DB["xbT"]], w=[xb])
                for gi, (p0, n) in enumerate(TG):
                    sl = slice(p0, p0 + n)
                    xcT, xcbT, rT, iT, sT, hbT = tl(n_, gi)
                    op("act", "activation", out=s_.ap[:, sl], in_=r_.ap[:, sl], func=AF.Exp, scale=clam2.ap[:, n_:n_ + 1], r=[rT, clam2], w=[sT])
                    op("act", "activation", out=r_.ap[:, sl], in_=r_.ap[:, sl], func=AF.Exp, scale=clam.ap[:, n_:n_ + 1], r=[rT, clam], w=[rT])
                for gi, (p0, n) in enumerate(TG):
                    sl = slice(p0, p0 + n)
                    xcT, xcbT, rT, iT, sT, hbT = tl(n_, gi)
                    op("act", "activation", out=s_.ap[:, sl], in_=s_.ap[:, sl], func=AF.Sqrt, scale=-1.0, bias=1.0, r=[sT], w=[sT])
                yield
                for gi in range(NP):
                    stage4(n_, gi)
                    if n_ + 1 < 8 and gi >= 1:
                        stage1(n_ + 1, gi - 1)
                if n_ + 1 < 8:
                    stage1(n_ + 1, NP - 1)
                    dma(gg.ap, ggT[(n_ + 1) * 128:(n_ + 2) * 128, :], r=[DB["ggT"]], w=[gg])
                dma(yTb[n_ * 128:(n_ + 1) * 128, :], hb_[n_ % 2].ap, r=pb["hb%d" % (n_ % 2)], w=[DB["yTb"]])
                yield
        genF = genF_()

    if genE is not None or genF is not None:
        for g_ in (genE, genF):
            if g_ is not None:
                for _ in g_:
                    pass
        kb.reset_arena()

    def out_proj(y_names, KC, w_dram, src_name, dst_name):
        wb = A([128, KC, D], BF16)
        wst = [A([128, 4, D], F32) for _ in range(2)]
        wv = w_dram.rearrange("(c p) n -> p c n", p=128)
        for c4 in range(KC // 4):
            ws = wst[c4 % 2]
            dma(ws.ap, wv[:, 4 * c4:4 * c4 + 4, :], w=[ws])
            kb.cast(c4, wb.ap[:, 4 * c4:4 * c4 + 4, :], ws.ap, [ws], [wb])
        yb = [A([128, KC, 512], BF16) for _ in range(2)]
        xt = [A([128, 8, 512], F32) for _ in range(2)]
        hs = fm(kb.dram[src_name]); hd = fm(kb.dram[dst_name])

        def ld(gi):
            p0, n = TG[gi]
            for yi, yn in enumerate(y_names):
                dma(yb[gi % 2].ap[:, 8 * yi:8 * yi + 8, 0:n], fm(kb.dram[yn])[:, :, p0:p0 + n], r=[DB[yn]], w=[yb[gi % 2]])
            dma(xt[gi % 2].ap[:, :, 0:n], hs[:, :, p0:p0 + n], r=[DB[src_name]], w=[xt[gi % 2]])

        ld(0)
        for gi, (p0, n) in enumerate(TG):
            if gi + 1 < len(TG):
                ld(gi + 1)
            y_, x_ = yb[gi % 2], xt[gi % 2]
            for oc in range(8):
                ps = kb.psum()
                mm(ps, [(wb.ap[:, kc, oc * 128:(oc + 1) * 128], y_.ap[:, kc, 0:n]) for kc in range(KC)], r=[wb, y_], out=ps.ap[:, 0:n])
                op("dve", "tensor_tensor", x_.ap[:, oc, 0:n], ps.ap[:, 0:n], x_.ap[:, oc, 0:n], ALU.add, r=[ps, x_], w=[x_])
            dma(hd[:, :, p0:p0 + n], x_.ap[:, :, 0:n], r=[x_], w=[DB[dst_name]])
        kb.reset_arena()

    if "G" in kb.phases:
        out_proj(["yTa", "yTb"], 16, ab_w_out, "hT0", "hT1")

    def mlp(layer, src_name, dst_name):
        w1b = A([128, 8, 4096], BF16)
        w2b = A([128, 32, D], BF16)
        wst = [A([128, 2048], F32) for _ in range(2)]
        w1v = mlp_w1[layer].rearrange("(c p) n -> p c n", p=128)
        w2v = mlp_w2[layer].rearrange("(c p) n -> p c n", p=128)
        w1B = [Buf() for _ in range(8)]
        w2B = [Buf() for _ in range(16)]
        pre = ("C" if layer == 0 else "J0") in kb.phases
        if pre:
            n1, n2 = "w1s%d" % layer, "w2s%d" % layer
            s1v = w1s[layer].rearrange("(c p) n -> p c n", p=128)
            s2v = w2s[layer].rearrange("(c p) n -> p c n", p=128)
            for kc in range(8):
                dma(w1b.ap[:, kc, :], s1v[:, kc, :], r=[DB[n1]], w=[w1B[kc]])
            for c2 in range(8):
                dma(w2b.ap[:, 4 * c2:4 * c2 + 4, :], s2v[:, 4 * c2:4 * c2 + 4, :], r=[DB[n2]], w=[w2B[2 * c2], w2B[2 * c2 + 1]])
        k = 0
        for kc in range(8 if not pre else 0):
            for hf in range(2):
                ws = wst[k % 2]
                dma(ws.ap, w1v[:, kc, hf * 2048:(hf + 1) * 2048], w=[ws])
                kb.cast(k, w1b.ap[:, kc, hf * 2048:(hf + 1) * 2048], ws.ap, [ws], [w1B[kc]])
                k += 1
        for c2 in range(16 if not pre else 0):
            ws = wst[k % 2]
            wsv = ws.ap.rearrange("p (a b) -> p a b", a=2)
            dma(wsv, w2v[:, 2 * c2:2 * c2 + 2, :], w=[ws])
            kb.cast(k, w2b.ap[:, 2 * c2:2 * c2 + 2, :], wsv, [ws], [w2B[c2]])
            k += 1
        NG = 256
        xts = [A([128, 8, NG], F32) for _ in range(2)]
        aT = A([128, 32, NG], BF16)
        sq = A([128, 8, NG], BF16)
        hns = [A([128, 8, NG], BF16) for _ in range(2)]
        rstds = [A([128, NG], F32) for _ in range(2)]
        tmp = [A([128, NG], F32) for _ in range(2)]
        hs = fm(kb.dram[src_name]); hd = fm(kb.dram[dst_name])
        gidx = 2 + layer
        NGR = len(TG2)

        def ld(gi):
            p0, n = TG2[gi]
            dma(xts[gi % 2].ap[:, :, 0:n], hs[:, :, p0:p0 + n], r=[DB[src_name]], w=[xts[gi % 2]])

        def nrm(gi):
            p0, n = TG2[gi]
            hn = hns[gi % 2]
            norm_group(xts[gi % 2], n, gidx, HnOut(hn, lambda kc, n=n, hn=hn: hn.ap[:, kc, 0:n]), sq, rstds[gi % 2])

        ld(0)
        ld(1)
        nrm(0)
        tc_ = 0
        for gi, (p0, n) in enumerate(TG2):
            xt = xts[gi % 2]
            hn = hns[gi % 2]
            for fc in range(32):
                ps = kb.psum()
                mm(ps, [(w1b.ap[:, kc, fc * 128:(fc + 1) * 128], hn.ap[:, kc, 0:n]) for kc in range(8)], r=w1B + [hn], out=ps.ap[:, 0:n])
                tm_ = tmp[tc_ % 2]
                op("act", "activation", out=tm_.ap[:, 0:n], in_=ps.ap[:, 0:n], func=AF.Relu, r=[ps], w=[tm_])
                op("pool" if tc_ % 2 == 0 else "dve", "tensor_tensor", aT.ap[:, fc, 0:n], tm_.ap[:, 0:n], tm_.ap[:, 0:n], ALU.mult, r=[tm_], w=[aT])
                tc_ += 1
            if gi + 1 < NGR:
                nrm(gi + 1)
            for oc in range(8):
                ps = kb.psum()
                mm(ps, [(w2b.ap[:, fc, oc * 128:(oc + 1) * 128], aT.ap[:, fc, 0:n]) for fc in range(32)], r=w2B + [aT], out=ps.ap[:, 0:n])
                op("dve", "tensor_tensor", xt.ap[:, oc, 0:n], ps.ap[:, 0:n], xt.ap[:, oc, 0:n], ALU.add, r=[ps, xt], w=[xt])
            dma(hd[:, :, p0:p0 + n], xt.ap[:, :, 0:n], r=[xt], w=[DB[dst_name]])
            if gi + 2 < NGR:
                ld(gi + 2)
        kb.reset_arena()

    if "H" in kb.phases:
        mlp(0, "hT1", "hT2")

    if "I" in kb.phases:
        hn, hnb = norm_full("hT2", 1)
        wb = A([128, 8, 3072], BF16)
        wst = [A([128, 4, 1024], F32) for _ in range(2)]
        cwv = c_w_in.rearrange("(c p) n -> p c n", p=128)
        wB = [Buf() for _ in range(6)]
        k = 0
        for cb in range(3):
            for c4 in range(2):
                ws = wst[k % 2]
                dma(ws.ap, cwv[:, 4 * c4:4 * c4 + 4, cb * 1024:(cb + 1) * 1024], w=[ws])
                kb.cast(k, wb.ap[:, 4 * c4:4 * c4 + 4, cb * 1024:(cb + 1) * 1024], ws.ap, [ws], [wB[k]])
                k += 1
        cosT = A([128, 33, 8], F32); sinT = A([128, 33, 8], F32)
        dma(cosT.ap, c_cos, w=[cosT]); dma(sinT.ap, c_sin, w=[sinT])
        qk = [A([128, 2048], F32) for _ in range(2)]
        qkb = [A([128, 2048], BF16) for _ in range(2)]
        rt = [A([128, 32, 8], F32) for _ in range(4)]
        vst = [A([128, 1024], BF16) for _ in range(2)]
        qst = [A([128, 16, 128], BF16) for _ in range(2)]
        pend_tr = []
        for ti, (p0, n) in enumerate(TT):
            tg_i = 0 if ti == 0 else 1 + (ti - 1) // 4
            q_, qb_, v_, qs_ = qk[ti % 2], qkb[ti % 2], vst[ti % 2], qst[ti % 2]
            for blk in range(6):
                ps = kb.psum()
                mm(ps, [(hn.ap[:, kc, p0:p0 + n], wb.ap[:, kc, blk * 512:(blk + 1) * 512]) for kc in range(8)], r=wB + [hnb[tg_i]], out=ps.ap[0:n, :])
                if blk < 2:
                    op("act", "activation", out=q_.ap[0:n, blk * 512:(blk + 1) * 512], in_=ps.ap[0:n, :], func=AF.Copy, scale=0.125, r=[ps], w=[q_])
                elif blk < 4:
                    op("dve", "tensor_copy", q_.ap[0:n, blk * 512:(blk + 1) * 512], ps.ap[0:n, :], r=[ps], w=[q_])
                else:
                    op("act", "activation", out=v_.ap[0:n, (blk - 4) * 512:(blk - 3) * 512], in_=ps.ap[0:n, :], func=AF.Copy, r=[ps], w=[v_])
            dma(vtok2[p0:p0 + n, :], v_.ap[0:n, :], r=[v_], w=[DB["vtok2"]])
            qv = q_.ap.rearrange("p (g d) -> p g d", d=64)
            x1 = qv[0:n, :, 0:8]; x2 = qv[0:n, :, 8:16]
            cb_ = cosT.ap[0:n, ti, :].unsqueeze(1).to_broadcast([n, 32, 8])
            sb_ = sinT.ap[0:n, ti, :].unsqueeze(1).to_broadcast([n, 32, 8])
            a1, a2, a3, a4 = [t_.ap[0:n] for t_ in rt]
            op("pool", "tensor_tensor", a1, x1, cb_, ALU.mult, r=[q_, cosT], w=[rt[0]])
            op("pool", "tensor_tensor", a2, x2, sb_, ALU.mult, r=[q_, sinT], w=[rt[1]])
            op("dve", "tensor_tensor", a3, x2, cb_, ALU.mult, r=[q_, cosT], w=[rt[2]])
            op("dve", "tensor_tensor", a4, x1, sb_, ALU.mult, r=[q_, sinT], w=[rt[3]])
            op("pool", "tensor_tensor", x1, a1, a2, ALU.subtract, r=[rt[0], rt[1]], w=[q_])
            op("dve", "tensor_tensor", x2, a3, a4, ALU.add, r=[rt[2], rt[3]], w=[q_])
            op("act", "activation", out=qb_.ap[0:n, :], in_=q_.ap[0:n, :], func=AF.Copy, r=[q_], w=[qb_])

            def trn(ti=ti, p0=p0, n=n, qb_=qb_, qs_=qs_):
                for c4 in range(4):
                    pT = kb.psum()
                    pTb = pT.ap.bitcast(BF16)
                    for j in range(4):
                        c = c4 * 4 + j
                        op("pe", "transpose", pTb[:, j * 128:j * 128 + n], qb_.ap[0:n, c * 128:(c + 1) * 128], identb.ap[0:n, 0:n],
                           r=[qb_, identb], w=[pT], inc=(j == 3))
                    src = pTb[:, 0:512].rearrange("p (a b) -> p a b", a=4)[:, :, 0:n]
                    if c4 % 2 == 0:
                        op("act", "activation", out=qs_.ap[:, 4 * c4:4 * c4 + 4, 0:n], in_=src, func=AF.Copy, r=[pT], w=[qs_])
                    else:
                        op("dve", "tensor_copy", qs_.ap[:, 4 * c4:4 * c4 + 4, 0:n], src, r=[pT], w=[qs_])
                dma(fm(qkT)[:, :, p0:p0 + n], qs_.ap[:, :, 0:n], r=[qs_], w=[DB["qkT"]])
            pend_tr.append(trn)
            if len(pend_tr) > 1:
                pend_tr.pop(0)()
        while pend_tr:
            pend_tr.pop(0)()
        kb.reset_arena()

    if "J0" in kb.phases:
        lv = A([1, 256], F32); l2 = A([1, 4], F32); lam_bc = A([128, 1], F32)
        dma(lv.ap, c_lambda, w=[lv])
        op("dve", "tensor_tensor", lv.ap[:, 0:64], lv.ap[:, 0:64], lv.ap[:, 64:128], ALU.mult, r=[lv], w=[lv])
        op("dve", "tensor_tensor", lv.ap[:, 128:192], lv.ap[:, 128:192], lv.ap[:, 192:256], ALU.mult, r=[lv], w=[lv])
        op("dve", "reduce_sum", l2.ap[:, 0:1], lv.ap[:, 0:64], AX.X, r=[lv], w=[l2])
        op("dve", "reduce_sum", l2.ap[:, 1:2], lv.ap[:, 128:192], AX.X, r=[lv], w=[l2])
        op("act", "activation", out=l2.ap[:, 0:2], in_=l2.ap[:, 0:2], func=AF.Exp, r=[l2], w=[l2])
        op("dve", "tensor_tensor", l2.ap[:, 2:3], l2.ap[:, 1:2], l2.ap[:, 0:1], ALU.subtract, r=[l2], w=[l2])
        op("dve", "tensor_scalar", l2.ap[:, 2:3], l2.ap[:, 2:3], -LAMBDA_INIT, None, ALU.add, r=[l2], w=[l2])
        psl = kb.psum()
        mm(psl, [(ones_f.ap[0:1, :], l2.ap[0:1, 2:3])], r=[ones_f, l2], out=psl.ap[:, 0:1])
        op("dve", "tensor_copy", lam_bc.ap, psl.ap[:, 0:1], r=[psl], w=[lam_bc])
        sln = A([128, 128], F32)
        dma(sln.ap, c_subln.partition_broadcast(128), w=[sln])
        op("dve", "tensor_scalar", sln.ap, sln.ap, 1.0 - LAMBDA_INIT, None, ALU.mult, r=[sln], w=[sln])
        sel = A([128, 2], BF16)
        op("pool", "memset", sel.ap, 0.0, w=[sel])
        op("pool", "memset", sel.ap[0:64, 0:1], 1.0, w=[sel])
        op("pool", "memset", sel.ap[64:128, 1:2], 1.0, w=[sel])
        qTh = [A([128, T], BF16) for _ in range(2)]
        kTh = [A([128, T], BF16) for _ in range(2)]
        Vh = [A([128, 33, 129], BF16) for _ in range(2)]
        for v_ in Vh:
            op("pool", "memset", v_.ap[:, :, 128:129], 1.0, w=[v_])
        sqt = A([128, T], BF16)
        ssq = A([2, 2, T], F32)
        mx = A([2, 4], F32)
        shift = A([128, 2], F32)
        mxd = A([2, 2], F32)
        PT = [A([128, 512], BF16) for _ in range(4)]
        gbuf = []
        for _ in range(2):
            gbuf.append({"Os": [A([128, 512], F32) for _ in range(2)], "rbc": [A([128, 512], F32) for _ in range(2)],
                         "sq": A([128, 512], BF16), "rstd": A([128, 512], F32), "on": A([128, 512], BF16)})
        accs = [A([128, 512], F32) for _ in range(2)]
        accbs = [A([128, 512], BF16) for _ in range(2)]
        onesq = A([128, 128], BF16)
        op("pool", "memset", onesq.ap, 1.0, w=[onesq])
        qz = [[A([128, T], BF16) for _ in range(2)] for _ in range(2)]
        for qq in qz:
            op("pool", "memset", qq[0].ap[64:128, :], 0.0, w=[qq[0]])
            op("pool", "memset", qq[1].ap[0:64, :], 0.0, w=[qq[1]])
        one1 = A([128, 1], BF16)
        op("pool", "memset", one1.ap, 1.0, w=[one1])
        o128 = A([128, 128], BF16)
        op("pool", "memset", o128.ap, 1.0 / 128, w=[o128])
        slc = A([128, 1], F32)
        dma(slc.ap, c_subln.rearrange("o (p u) -> p (o u)", u=1), w=[slc])
        op("dve", "tensor_scalar", slc.ap, slc.ap, 1.0 - LAMBDA_INIT, None, ALU.mult, r=[slc], w=[slc])
        qkv = fm(qkT)

        def aload(h):
            s = h % 2
            dma(qTh[s].ap, qkT[h * 128:(h + 1) * 128, :], r=[DB["qkT"]], w=[qTh[s]])
            dma(kTh[s].ap, qkT[1024 + h * 128:1024 + (h + 1) * 128, :], r=[DB["qkT"]], w=[kTh[s]])
            dma(qz[s][0].ap[0:64, :], qkT[h * 128:h * 128 + 64, :], r=[DB["qkT"]], w=[qz[s][0]])
            dma(qz[s][1].ap[64:128, :], qkT[h * 128 + 64:(h + 1) * 128, :], r=[DB["qkT"]], w=[qz[s][1]])
            dma(Vh[s].ap[0:16, 0, 0:128], vtok2[0:16, h * 128:(h + 1) * 128], r=[DB["vtok2"]], w=[Vh[s]])
            dma(Vh[s].ap[:, 1:33, 0:128], vtok2[16:T, h * 128:(h + 1) * 128].rearrange("(j p) c -> p j c", p=128), r=[DB["vtok2"]], w=[Vh[s]])

        aload(0)
        pcast = gen_precast(1)
        pti = 0
        oi = 0
        for h in range(8):
            if h + 1 < 8:
                aload(h + 1)
            s = h % 2
            qt, kt, vh = qTh[s], kTh[s], Vh[s]
            for which, src in ((0, qt), (1, kt)):
                op("pool", "tensor_tensor", sqt.ap, src.ap, src.ap, ALU.mult, r=[src], w=[sqt])
                for gi, (p0, n) in enumerate(TG):
                    ps = kb.psum()
                    mm(ps, [(sel.ap, sqt.ap[:, p0:p0 + n])], r=[sel, sqt], out=ps.ap[0:2, 0:n])
                    op("dve", "tensor_copy", ssq.ap[:, which, p0:p0 + n], ps.ap[0:2, 0:n], r=[ps], w=[ssq])
                op("dve", "reduce_max", mx.ap[:, which:which + 1], ssq.ap[:, which, :], AX.X, r=[ssq], w=[mx])
            op("dve", "tensor_tensor", mx.ap[:, 2:3], mx.ap[:, 0:1], mx.ap[:, 1:2], ALU.mult, r=[mx], w=[mx])
            op("act", "activation", out=mx.ap[:, 3:4], in_=mx.ap[:, 2:3], func=AF.Ln, scale=1.05, r=[mx], w=[mx])
            op("act", "activation", out=mx.ap[:, 3:4], in_=mx.ap[:, 3:4], func=AF.Exp, scale=0.5, r=[mx], w=[mx])
            op("dve", "tensor_tensor", mxd.ap, mx.ap[:, 3:4].to_broadcast([2, 2]), ident.ap[0:2, 0:2], ALU.mult, r=[mx, ident], w=[mxd])
            ps = kb.psum()
            mm(ps, [(ones_f.ap[0:2, :], mxd.ap)], r=[ones_f, mxd], out=ps.ap[:, 0:2])
            op("dve", "tensor_scalar", shift.ap, ps.ap[:, 0:2], -1.0, None, ALU.mult, r=[ps], w=[shift])

            tasks = []
            for g in range(-1, 8):
                if g < 0:
                    q0, nq_tot, nblk = 0, 16, 1
                else:
                    q0, nq_tot, nblk = 16 + 512 * g, 512, 4
                for c in range(2):
                    kbl = [(0, 0, 16, 0, g < 0)]
                    if g >= 0:
                        for j in range(4 * g):
                            kbl.append((1 + j, 16 + 128 * j, 128, 0, False))
                        for i in range(4):
                            kbl.append((1 + 4 * g + i, 16 + 128 * (4 * g + i), 128, i, True))
                    for bi_, e in enumerate(kbl):
                        tasks.append(dict(g=g, c=c, q0=q0, nq_tot=nq_tot, nblk=nblk, kb=e, first=(bi_ == 0), last=(bi_ == len(kbl) - 1)))
            LA = 2
            cur = {}
            deferred = []

            def stage1(t):
                kt_i, k0, nk, qb0, masked = t["kb"]
                c = t["c"]
                qzc = qz[s][c]
                qs = t["q0"] + qb0 * 128
                ncol = t["nq_tot"] - qb0 * 128
                pS = kb.psum_from("pS", [4, 5, 6])
                mm(pS, [(kt.ap[:, k0:k0 + nk], qzc.ap[:, qs:qs + ncol])], r=[kt, qzc], out=pS.ap[0:nk, 0:ncol])
                pt = PT[cur.setdefault("pti", 0) % len(PT)]
                cur["pti"] += 1
                op("act", "activation", out=pt.ap[0:nk, 0:ncol], in_=pS.ap[0:nk, 0:ncol], func=AF.Exp, bias=shift.ap[0:nk, c:c + 1],
                   r=[pS, shift], w=[pt])
                if masked:
                    m = min(128, ncol)
                    op("pool", "tensor_tensor", pt.ap[0:nk, 0:m], pt.ap[0:nk, 0:m], tri.ap[0:nk, 0:m], ALU.mult, r=[pt, tri], w=[pt])
                t["pt"] = pt

            def stage2(t, idx):
                kt_i, k0, nk, qb0, masked = t["kb"]
                g, c, nq_tot, q0 = t["g"], t["c"], t["nq_tot"], t["q0"]
                pt = t["pt"]
                ncol = nq_tot - qb0 * 128
                qoff = qb0 * 128
                if t["first"]:
                    cur["pOT"] = kb.psum_from("pOT", [0, 1])
                    cur["pL"] = kb.psum_from("pL", [2, 3])
                    if c == 0:
                        cur["gp"] = cur.setdefault("gcount", 0) % 2
                        cur["gcount"] += 1
                pOT, pL = cur["pOT"], cur["pL"]
                op("pe", "matmul", pOT.ap[:, qoff:qoff + ncol], vh.ap[0:nk, kt_i, 0:128], pt.ap[0:nk, 0:ncol], start=t["first"], stop=t["last"],
                   r=[pt, vh], w=[pOT], inc=False)
                op("pe", "matmul", pL.ap[:, qoff:qoff + ncol], onesq.ap[0:nk, :], pt.ap[0:nk, 0:ncol], start=t["first"], stop=t["last"],
                   r=[pt, onesq], w=[pL], inc=True)
                if not t["last"]:
                    return
                B = gbuf[cur["gp"]]
                n_ = nq_tot
                Os0, rbc = B["Os"][0], B["rbc"][c]
                sq, rstd, onb_ = B["sq"], B["rstd"], B["on"]

                def stepA(n_=n_, Os0=Os0, rbc=rbc, pOT=pOT, pL=pL, c=c, sq=sq):
                    op("dve", "reciprocal", rbc.ap[:, 0:n_], pL.ap[:, 0:n_], r=[pL], w=[rbc])
                    if c == 0:
                        op("dve", "tensor_tensor", Os0.ap[:, 0:n_], pOT.ap[:, 0:n_], rbc.ap[:, 0:n_], ALU.mult, r=[pOT, rbc], w=[Os0])
                    else:
                        op("dve", "tensor_tensor", rbc.ap[:, 0:n_], pOT.ap[:, 0:n_], rbc.ap[:, 0:n_], ALU.mult, r=[pOT, rbc], w=[rbc])
                        op("dve", "scalar_tensor_tensor", Os0.ap[:, 0:n_], rbc.ap[:, 0:n_], lam_bc.ap[:, 0:1], Os0.ap[:, 0:n_], ALU.mult, ALU.add,
                           r=[rbc, lam_bc, Os0], w=[Os0])
                        op("pool", "tensor_tensor", sq.ap[:, 0:n_], Os0.ap[:, 0:n_], Os0.ap[:, 0:n_], ALU.mult, r=[Os0], w=[sq])

                deferred.append((idx + 2, stepA))
                if c == 1:
                    def stepC(n_=n_, Os0=Os0, sq=sq, rstd=rstd, onb_=onb_, q0=q0):
                        pSS = kb.ps[7]
                        mm(pSS, [(o128.ap, sq.ap[:, 0:n_])], r=[o128, sq], out=pSS.ap[:, 0:n_])
                        op("act", "activation", out=rstd.ap[:, 0:n_], in_=pSS.ap[:, 0:n_], func=AF.Ln, bias=epsc.ap, r=[pSS, epsc], w=[rstd])
                        op("act", "activation", out=rstd.ap[:, 0:n_], in_=rstd.ap[:, 0:n_], func=AF.Exp, scale=-0.5, r=[rstd], w=[rstd])
                        op("dve", "scalar_tensor_tensor", onb_.ap[:, 0:n_], Os0.ap[:, 0:n_], slc.ap[:, 0:1], rstd.ap[:, 0:n_], ALU.mult, ALU.mult,
                           r=[Os0, slc, rstd], w=[onb_])
                        dma(oT[h * 128:(h + 1) * 128, q0:q0 + n_], onb_.ap[:, 0:n_], r=[onb_], w=[DB["oT"]])
                    deferred.append((idx + 10, stepC))
                deferred.sort(key=lambda x: x[0])

            NTK = len(tasks)
            for i in range(NTK + LA):
                if i % 20 == 0:
                    next(pcast, None)
                if i < NTK:
                    stage1(tasks[i])
                j = i - LA
                if j >= 0:
                    while deferred and deferred[0][0] <= j:
                        deferred.pop(0)[1]()
                    stage2(tasks[j], j)
            while deferred:
                deferred.pop(0)[1]()
        for _ in pcast:
            pass
        kb.reset_arena()

    if "J" in kb.phases:
        out_proj(["oT"], 8, c_w_out, "hT2", "hT3")
    if "K2" in kb.phases:
        mlp(1, "hT3", "hT4")

    if "M" in kb.phases:
        xts = [A([128, 8, 512], F32) for _ in range(2)]
        sqs = [A([128, 8, 512], BF16) for _ in range(2)]
        rs = [A([128, 512], F32) for _ in range(2)]
        xn = [A([128, 8, 512], F32) for _ in range(2)]
        ostg = [A([128, D], F32) for _ in range(2)]
        src = fm(hT[4])
        oc_ = 0
        for gi, (p0, n) in enumerate(TG[1:]):
            xt = xts[gi % 2]
            dma(xt.ap, src[:, :, p0:p0 + n], r=[DB["hT4"]], w=[xt])
            xn_ = xn[gi % 2]
            norm_group(xt, n, 4, HnOut(xn_, lambda kc, xn_=xn_: xn_.ap[:, kc, :]), sqs[gi % 2], rs[gi % 2])
            for j in range(4):
                os_ = ostg[oc_ % 2]
                oc_ += 1
                for half in range(2):
                    ps = kb.psum()
                    for kk in range(4):
                        kc = half * 4 + kk
                        op("pe", "transpose", ps.ap[:, kk * 128:(kk + 1) * 128], xn_.ap[:, kc, j * 128:(j + 1) * 128], ident.ap,
                           r=[xn_, ident], w=[ps], inc=(kk == 3))
                    if half == 0:
                        op("act", "activation", out=os_.ap[:, 0:512], in_=ps.ap, func=AF.Copy, r=[ps], w=[os_])
                    else:
                        op("dve", "tensor_copy", os_.ap[:, 512:1024], ps.ap, r=[ps], w=[os_])
                r0 = p0 - NM + j * 128
                dma(out[r0:r0 + 128, :], os_.ap, r=[os_], w=[DB["out"]])
        kb.reset_arena()

    S.barrier()
    S.emit()
    return kb


def _consts():
    ident = np.eye(128, dtype=np.float32)
    tri = np.triu(np.ones((128, 128), dtype=np.float32))
    pos = np.arange(T, dtype=np.float32)
    inv_freq = np.power(np.float32(500000.0), -np.arange(0, 16, 2, dtype=np.float32) / np.float32(16)).astype(np.float32)
    ang = (pos[:, None] * inv_freq[None, :]).astype(np.float32)
    cos = np.cos(ang).astype(np.float32)
    sin = np.sin(ang).astype(np.float32)

    def tile(a):
        o = np.zeros((128, 33, 8), dtype=np.float32)
        o[0:16, 0] = a[0:16]
        o[:, 1:] = a[16:].reshape(32, 128, 8).transpose(1, 0, 2)
        return o
    return {"c_ident": ident, "c_tri": tri, "c_cos": tile(cos), "c_sin": tile(sin)}


def _core_inputs(inputs, b):
    f = lambda a: np.ascontiguousarray(np.asarray(a, dtype=np.float32))
    m = {
        "x": f(inputs["x"][b]),
        "meta_tokens": f(inputs["meta_tokens"]),
        "norm_mix": f(inputs["norm_mix"]),
        "norm_mlp": f(inputs["norm_mlp"]),
        "norm_final": f(inputs["norm_final"]).reshape(1, D),
        "ab_w_in": f(inputs["ab_w_in"][0]),
        "ab_if_bias": f(inputs["ab_if_bias"][0]).reshape(8, 1),
        "mlstm_norm": f(inputs["mlstm_norm"][0]).reshape(1, D),
        "lru_conv_w": f(inputs["lru_conv_w"][0]),
        "lru_conv_b": f(inputs["lru_conv_b"][0]).reshape(1, D),
        "lru_w_r": f(inputs["lru_w_r"][0]),
        "lru_b_r": f(inputs["lru_b_r"][0]).reshape(1, D),
        "lru_w_i": f(inputs["lru_w_i"][0]),
        "lru_b_i": f(inputs["lru_b_i"][0]).reshape(1, D),
        "lru_lambda": f(inputs["lru_lambda"][0]).reshape(1, D),
        "ab_w_out": f(inputs["ab_w_out"][0]),
        "c_w_in": f(inputs["c_w_in"][0]),
        "c_lambda": f(inputs["c_lambda"][0]).reshape(1, 256),
        "c_subln": f(inputs["c_subln"][0]).reshape(1, 128),
        "c_w_out": f(inputs["c_w_out"][0]),
        "mlp_w1": f(inputs["mlp_w1"]),
        "mlp_w2": f(inputs["mlp_w2"]),
    }
    m.update(_consts())
    return m


def kernel(**inputs):
    kb = build()
    maps = []
    for b in range(8):
        m = _core_inputs(inputs, b)
        maps.append({k: m[k] for k in kb.in_names})
    res = run_bass_kernel_spmd(kb.nc, maps, core_ids=list(range(8)))
    return np.stack([np.asarray(r["out"], dtype=np.float32) for r in res.results], axis=0)
```

```python
import math
import contextlib
import numpy as np
import concourse.bass as bass
import concourse.mybir as mybir
from concourse.bass_utils import run_bass_kernel_spmd

F32 = mybir.dt.float32
BF16 = mybir.dt.bfloat16
AF = mybir.ActivationFunctionType
ALU = mybir.AluOpType
AX = mybir.AxisListType

T = 4112
NM = 16
D = 1024
EPS = 1e-6
TG = [(0, 16)] + [(16 + 512 * i, 512) for i in range(8)]
TT = [(0, 16)] + [(16 + 128 * i, 128) for i in range(32)]
TG2 = [(0, 16)] + [(16 + 256 * i, 256) for i in range(16)]
LAMBDA_INIT = 0.8 - 0.6 * math.exp(-0.3 * 1)
SEM_LIMIT = 24000
DUMMY_MM = 1
ARENA_BYTES = 192 * 1024

ENGS = ("pe", "act", "dve", "pool", "sp")


class Buf:
    __slots__ = ("w", "r")

    def __init__(self):
        self.w = None
        self.r = {}


class Sched:
    def __init__(self, nc, n_dma_sems=24):
        self.nc = nc
        self.ops = {e: [] for e in ENGS}
        self.cnt = {e: 0 for e in ENGS if e != "sp"}
        self.epoch = {e: 0 for e in ENGS if e != "sp"}
        self.seen = {e: {} for e in ENGS}
        self.n_dma = n_dma_sems
        self.dma_val = [0] * n_dma_sems
        self.dma_epoch = [0] * n_dma_sems
        self.dma_rr = 0
        self.keys = set()
        self.last = {}

    def _need(self, eng, tok, waits):
        if tok is None:
            return
        k, v = tok
        if k[0] == "pe" and eng == "pe":
            return
        if self.seen[eng].get(k, 0) >= v:
            return
        if waits.get(k, 0) < v:
            waits[k] = v

    def _deps(self, eng, reads, writes):
        waits = {}
        for b in reads:
            self._need(eng, b.w, waits)
        for b in writes:
            self._need(eng, b.w, waits)
            for t in b.r.items():
                self._need(eng, t, waits)
        for k, v in waits.items():
            self.seen[eng][k] = v
        return list(waits.items())

    def _mark(self, tok, reads, writes):
        for b in reads:
            if b.r.get(tok[0], 0) < tok[1]:
                b.r[tok[0]] = tok[1]
        for b in writes:
            b.w = tok
            b.r = {}
        self.last[tok[0]] = max(self.last.get(tok[0], 0), tok[1])

    def _next_tok(self, eng, advance):
        c, ep = self.cnt[eng], self.epoch[eng]
        if c >= SEM_LIMIT:
            c, ep = 0, ep + 1
        if advance:
            self.cnt[eng], self.epoch[eng] = c + 1, ep
        key = (eng, ep)
        self.keys.add(key)
        return (key, c + 1)

    def op(self, eng, meth, *args, reads=(), writes=(), inc=True, **kw):
        waits = self._deps(eng, reads, writes)
        tok = self._next_tok(eng, inc)
        self.ops[eng].append((waits, (meth, args, kw), tok[0] if inc else None, 1))
        self._mark(tok, reads, writes)
        return tok

    def dma(self, out, in_, reads=(), writes=(), q="sp", **kw):
        kw = dict(kw)
        kw["out"] = out
        kw["in_"] = in_
        i = self.dma_rr
        self.dma_rr = (i + 1) % self.n_dma
        waits = dict(self._deps(q, reads, writes))
        key = ("dma", i, self.dma_epoch[i])
        prev = self.dma_val[i]
        if prev and self.seen[q].get(key, 0) < prev:
            waits[key] = max(waits.get(key, 0), prev)
            self.seen[q][key] = prev
        if prev + 16 > SEM_LIMIT:
            self.dma_epoch[i] += 1
            self.dma_val[i] = 0
            key = ("dma", i, self.dma_epoch[i])
        self.dma_val[i] += 16
        self.keys.add(key)
        tok = (key, self.dma_val[i])
        self.ops[q].append((list(waits.items()), ("dma_start", (), kw), key, 16))
        self._mark(tok, reads, writes)
        return tok

    def barrier(self):
        for e in ENGS:
            waits = {}
            for k, v in self.last.items():
                self._need(e, (k, v), waits)
            for k, v in waits.items():
                self.seen[e][k] = v
            if waits:
                self.ops[e].append((list(waits.items()), None, None, 0))

    def emit(self):
        nc = self.nc
        with contextlib.ExitStack() as st:
            sems = {}
            for k in sorted(self.keys, key=str):
                sems[k] = st.enter_context(nc.semaphore("s_" + "_".join(str(x) for x in k)))
            block = st.enter_context(nc.Block())

            def run(engname):
                def body(e):
                    for waits, fn, post, amt in self.ops[engname]:
                        for k, v in waits:
                            e.wait_ge(sems[k], v)
                        if fn is None:
                            continue
                        ins = getattr(e, fn[0])(*fn[1], **fn[2])
                        if post is not None:
                            ins.then_inc(sems[post], amt)
                return body

            block.tensor(run("pe"))
            block.scalar(run("act"))
            block.vector(run("dve"))
            block.gpsimd(run("pool"))
            block.sync(run("sp"))


class Tl:
    __slots__ = ("ap", "b")

    def __init__(self, ap, b=None):
        self.ap = ap
        self.b = b if b is not None else Buf()

    def __getitem__(self, k):
        return self.ap[k]


class KB:
    def __init__(self, phases, ext_out=()):
        self.phases = set(phases)
        self.ext_out = set(ext_out)
        nc = self.nc = bass.Bass("TRN2", target_bir_lowering=False)
        self.S = Sched(nc)
        self.dram = {}
        self.meta = {}
        self.dbuf = {}
        self.in_names = []
        self.out_names = []

    def dt(self, name, shape, dtype, producer=None):
        if name in self.dram:
            return self.dram[name]
        if producer is None or producer not in self.phases:
            kind = "ExternalInput"
            self.in_names.append(name)
        elif name in self.ext_out:
            kind = "ExternalOutput"
            self.out_names.append(name)
        else:
            kind = "Internal"
        t = self.nc.dram_tensor(name, list(shape), dtype, kind=kind).ap()
        self.meta[name] = (list(shape), dtype)
        self.dram[name] = t
        self.dbuf[name] = Buf()
        return t

    def setup_sbuf(self):
        nc = self.nc
        self.persist = nc.alloc_sbuf_tensor("persist", [128, 3072], F32)
        self.poff = 0
        self.arena = nc.alloc_sbuf_tensor("arena", [128, ARENA_BYTES // 4], F32)
        self.aoff = 0
        self.ps = [Tl(nc.alloc_psum_tensor("ps%d" % i, [128, 512], F32)[:]) for i in range(8)]
        self.psi = 0
        self.psc = {}

    def _carve(self, base, off, shape, dtype):
        n = int(np.prod(shape[1:]))
        nb = n * (4 if dtype == F32 else 2)
        nb = (nb + 63) // 64 * 64
        if dtype == F32:
            v = base[0:shape[0], off // 4: off // 4 + n]
        else:
            v = base[0:shape[0], off // 4: off // 4 + (n + 1) // 2].bitcast(BF16)[:, 0:n]
        if len(shape) == 3:
            v = v.rearrange("p (a b) -> p a b", a=shape[1])
        elif len(shape) == 4:
            v = v.rearrange("p (a b c) -> p a b c", a=shape[1], b=shape[2])
        return v, nb

    def P(self, shape, dtype=F32):
        v, nb = self._carve(self.persist, self.poff, shape, dtype)
        self.poff += nb
        assert self.poff <= 3072 * 4, self.poff
        return Tl(v)

    def A(self, shape, dtype=F32):
        v, nb = self._carve(self.arena, self.aoff, shape, dtype)
        self.aoff += nb
        assert self.aoff <= ARENA_BYTES, (self.aoff, shape)
        return Tl(v)

    def reset_arena(self):
        self.S.barrier()
        self.aoff = 0

    def psum(self):
        p = self.ps[self.psi]
        self.psi = (self.psi + 1) % 8
        return p

    def psum_from(self, key, banks):
        c = self.psc.get(key, 0)
        self.psc[key] = c + 1
        return self.ps[banks[c % len(banks)]]

    def cast(self, i, out, in_, r, w):
        if i % 2 == 0:
            self.op("act", "activation", out=out, in_=in_, func=AF.Copy, r=r, w=w)
        else:
            self.op("pool", "tensor_copy", out, in_, r=r, w=w)

    def op(self, eng, meth, *args, r=(), w=(), inc=True, **kw):
        return self.S.op(eng, meth, *args, reads=[x.b if isinstance(x, Tl) else x for x in r],
                         writes=[x.b if isinstance(x, Tl) else x for x in w], inc=inc, **kw)

    def dma(self, out, in_, r=(), w=(), **kw):
        return self.S.dma(out, in_, reads=[x.b if isinstance(x, Tl) else x for x in r],
                          writes=[x.b if isinstance(x, Tl) else x for x in w], **kw)

    def mm(self, ps, pairs, r, n_out=None, out=None):
        o = out if out is not None else ps.ap
        for i, (l, rh) in enumerate(pairs):
            last = i == len(pairs) - 1
            self.op("pe", "matmul", o, l, rh, start=(i == 0), stop=last, r=r, w=[ps], inc=last)


def build(phases=None, ext_out=("out",)):
    ALL = ["A", "C", "D", "E", "F", "G", "H", "I", "J0", "J", "K2", "M"]
    if phases is None:
        phases = ALL
    kb = KB(phases, ext_out)
    nc, S = kb.nc, kb.S
    dt = kb.dt
    op, dma, mm = kb.op, kb.dma, kb.mm

    x = dt("x", [4096, D], F32)
    meta = dt("meta_tokens", [NM, D], F32)
    norm_mix = dt("norm_mix", [2, D], F32)
    norm_mlp = dt("norm_mlp", [2, D], F32)
    norm_final = dt("norm_final", [1, D], F32)
    ab_w_in = dt("ab_w_in", [D, 6152], F32)
    ab_if_bias = dt("ab_if_bias", [8, 1], F32)
    mlstm_norm = dt("mlstm_norm", [1, D], F32)
    lru_conv_w = dt("lru_conv_w", [4, D], F32)
    lru_conv_b = dt("lru_conv_b", [1, D], F32)
    lru_w_r = dt("lru_w_r", [8, 128, 128], F32)
    lru_b_r = dt("lru_b_r", [1, D], F32)
    lru_w_i = dt("lru_w_i", [8, 128, 128], F32)
    lru_b_i = dt("lru_b_i", [1, D], F32)
    lru_lambda = dt("lru_lambda", [1, D], F32)
    ab_w_out = dt("ab_w_out", [2048, D], F32)
    c_w_in = dt("c_w_in", [D, 3072], F32)
    c_lambda = dt("c_lambda", [1, 256], F32)
    c_subln = dt("c_subln", [1, 128], F32)
    c_w_out = dt("c_w_out", [D, D], F32)
    mlp_w1 = dt("mlp_w1", [2, D, 4096], F32)
    mlp_w2 = dt("mlp_w2", [2, 4096, D], F32)
    c_ident = dt("c_ident", [128, 128], F32)
    c_tri = dt("c_tri", [128, 128], F32)
    c_cos = dt("c_cos", [128, 33, 8], F32)
    c_sin = dt("c_sin", [128, 33, 8], F32)

    hT = [dt("hT%d" % i, [D, T], F32, p) for i, p in enumerate(["A", "G", "H", "J", "K2"])]
    qT = dt("qT", [D, T], BF16, "C")
    kT = dt("kT", [D, T], BF16, "C")
    ktok = dt("ktok", [T, D], BF16, "C")
    vtok = dt("vtok", [T, D], BF16, "C")
    otok = dt("otok", [T, D], BF16, "C")
    giT = dt("giT", [4, T], F32, "C")
    gfT = dt("gfT", [4, T], F32, "C")
    xbT = dt("xbT", [D, T], F32, "C")
    ggT = dt("ggT", [D, T], F32, "C")
    gprep = dt("gprep", [128, 33 * 8 + 33 * 4], F32, "D")
    yTa = dt("yTa", [D, T], BF16, "E")
    yTb = dt("yTb", [D, T], BF16, "F")
    qkT = dt("qkT", [2048, T], BF16, "I")
    vtok2 = dt("vtok2", [T, D], BF16, "I")
    oT = dt("oT", [D, T], BF16, "J0")
    w1s = [dt("w1s%d" % l, [D, 4096], BF16, p) for l, p in enumerate(["C", "J0"])]
    w2s = [dt("w2s%d" % l, [4096, D], BF16, p) for l, p in enumerate(["C", "J0"])]
    out = dt("out", [4096, D], F32, "M")
    DB = kb.dbuf

    kb.setup_sbuf()
    P, A = kb.P, kb.A

    def gen_precast(layer):
        stg = [A([128, 2048], F32) for _ in range(2)]
        stb = [A([128, 2048], BF16) for _ in range(2)]
        w1v = mlp_w1[layer].rearrange("(c p) n -> p c n", p=128)
        w2v = mlp_w2[layer].rearrange("(c p) n -> p c n", p=128)
        d1 = w1s[layer].rearrange("(c p) n -> p c n", p=128)
        d2 = w2s[layer].rearrange("(c p) n -> p c n", p=128)
        jobs = []
        for kc in range(8):
            for hf in range(2):
                jobs.append((w1v[:, kc, hf * 2048:(hf + 1) * 2048], d1[:, kc, hf * 2048:(hf + 1) * 2048], None, "w1s%d" % layer))
        for c2 in range(16):
            jobs.append((w2v[:, 2 * c2:2 * c2 + 2, :], d2[:, 2 * c2:2 * c2 + 2, :], 2, "w2s%d" % layer))
        prev = None
        for k, (src, dst, a3, dname) in enumerate(jobs):
            sg, sb_ = stg[k % 2], stb[k % 2]
            sv = sg.ap if a3 is None else sg.ap.rearrange("p (a b) -> p a b", a=a3)
            bv = sb_.ap if a3 is None else sb_.ap.rearrange("p (a b) -> p a b", a=a3)
            dma(sv, src, w=[sg])
            yield
            op("pool", "tensor_copy", sb_.ap, sg.ap, r=[sg], w=[sb_])
            yield
            if prev is not None:
                dma(prev[0], prev[1], r=[prev[2]], w=[DB[prev[3]]])
            prev = (dst, bv, sb_, dname)
            yield
        dma(prev[0], prev[1], r=[prev[2]], w=[DB[prev[3]]])
        yield

    def fm(t):
        return t.rearrange("(c p) t -> p c t", p=128)

    ident = P([128, 128], F32)
    identb = P([128, 128], BF16)
    tri = P([128, 128], F32)
    ones_b = P([128, 128], BF16)
    ones_f = P([128, 128], F32)
    epsc = P([128, 1], F32)
    gam = P([128, 8, 8], F32)
    lrup = P([128, 8, 8], F32)
    clam = P([128, 8], F32)
    dma(ident.ap, c_ident, w=[ident])
    dma(tri.ap, c_tri, w=[tri])
    op("dve", "tensor_copy", identb.ap, ident.ap, r=[ident], w=[identb])
    op("pool", "memset", ones_b.ap, 1.0 / 1024, w=[ones_b])
    op("pool", "memset", ones_f.ap, 1.0, w=[ones_f])
    op("pool", "memset", epsc.ap, EPS, w=[epsc])

    def load_colvecs(dst, rows):
        st = A([16, 1024], F32)
        for i, rr in enumerate(rows):
            dma(st.ap[i:i + 1, :], rr, w=[st])
        for kc in range(8):
            ps = kb.psum()
            op("pe", "transpose", ps.ap[:, 0:len(rows)], st.ap[0:len(rows), kc * 128:(kc + 1) * 128],
               ident.ap[0:len(rows), 0:len(rows)], r=[st, ident], w=[ps])
            op("dve", "tensor_copy", dst.ap[:, kc, 0:len(rows)], ps.ap[:, 0:len(rows)], r=[ps], w=[dst])

    load_colvecs(gam, [norm_mix[0:1, :], norm_mix[1:2, :], norm_mlp[0:1, :], norm_mlp[1:2, :], norm_final])
    load_colvecs(lrup, [lru_conv_w[j:j + 1, :] for j in range(4)] + [lru_conv_b, lru_b_r, lru_b_i, lru_lambda])
    kb.reset_arena()

    def norm_group(xt, n, gidx, hn_out, sq, rstd, out_dtype_f32=False):
        op("act", "activation", out=sq.ap[:, :, 0:n], in_=xt.ap[:, :, 0:n], func=AF.Square, r=[xt], w=[sq])
        ps = kb.psum()
        mm(ps, [(ones_b.ap, sq.ap[:, kc, 0:n]) for kc in range(8)], r=[ones_b, sq], out=ps.ap[:, 0:n])
        op("act", "activation", out=rstd.ap[:, 0:n], in_=ps.ap[:, 0:n], func=AF.Sqrt, bias=epsc.ap, r=[ps, epsc], w=[rstd])
        op("dve", "reciprocal", rstd.ap[:, 0:n], rstd.ap[:, 0:n], r=[rstd], w=[rstd])
        for kc in range(8):
            op("dve", "scalar_tensor_tensor", hn_out(kc), xt.ap[:, kc, 0:n], gam.ap[:, kc, gidx:gidx + 1],
               rstd.ap[:, 0:n], ALU.mult, ALU.mult, r=[xt, gam, rstd], w=[hn_out.tl])

    class HnOut:
        def __init__(self, tl, fn):
            self.tl, self.fn = tl, fn

        def __call__(self, kc):
            return self.fn(kc)

    if "A" in kb.phases:
        xin = [A([128, 4, D], F32) for _ in range(2)]
        stg = [A([128, 8, 512], F32) for _ in range(2)]
        for gi, (p0, n) in enumerate(TG):
            xi, sg = xin[gi % 2], stg[gi % 2]
            nt = max(1, n // 128)
            if gi == 0:
                dma(xi.ap[0:16, 0, :], meta, w=[xi])
            else:
                r0 = p0 - NM
                dma(xi.ap, x[r0:r0 + 512, :].rearrange("(j p) d -> p j d", p=128), w=[xi])
            for kc in range(8):
                ps = kb.psum()
                for j in range(nt):
                    m = min(n, 128)
                    op("pe", "transpose", ps.ap[:, j * 128:j * 128 + m], xi.ap[0:m, j, kc * 128:(kc + 1) * 128],
                       ident.ap[0:m, 0:m], r=[xi, ident], w=[ps], inc=(j == nt - 1))
                if kc % 2 == 0:
                    op("act", "activation", out=sg.ap[:, kc, 0:n], in_=ps.ap[:, 0:n], func=AF.Copy, r=[ps], w=[sg])
                else:
                    op("dve", "tensor_copy", sg.ap[:, kc, 0:n], ps.ap[:, 0:n], r=[ps], w=[sg])
            dma(fm(hT[0])[:, :, p0:p0 + n], sg.ap[:, :, 0:n], r=[sg], w=[DB["hT0"]])
        kb.reset_arena()

    def norm_full(src_name, gidx):
        hn = A([128, 8, T], BF16)
        hnb = [Buf() for _ in TG]
        mark = kb.aoff
        xts = [A([128, 8, 512], F32) for _ in range(2)]
        sqs = [A([128, 8, 512], BF16) for _ in range(2)]
        rs = [A([128, 512], F32) for _ in range(2)]
        src = fm(kb.dram[src_name])
        for gi, (p0, n) in enumerate(TG):
            xt = xts[gi % 2]
            dma(xt.ap[:, :, 0:n], src[:, :, p0:p0 + n], r=[DB[src_name]], w=[xt])
            ho = HnOut(Tl(hn.ap, hnb[gi]), lambda kc, p0=p0, n=n: hn.ap[:, kc, p0:p0 + n])
            norm_group(xt, n, gidx, ho, sqs[gi % 2], rs[gi % 2])
        S.barrier()
        kb.aoff = mark
        return hn, hnb

    if "C" in kb.phases:
        hn, hnb = norm_full("hT0", 0)
        wst = [A([128, 8, 512], F32) for _ in range(2)]
        wbf = [A([128, 8, 512], BF16) for _ in range(2)]
        ostg = [A([128, 4, 512], F32) for _ in range(2)]
        ostg_b = [Tl(o.ap.rearrange("p a b -> p (a b)").bitcast(BF16)[:, 0:2048].rearrange("p (a b) -> p a b", a=4), o.b) for o in ostg]
        tstg = [A([128, 512], BF16) for _ in range(3)]
        win = ab_w_in.rearrange("(c p) n -> p c n", p=128)
        pcast = gen_precast(0)
        jobs = []
        for name, c0 in (("q", 0), ("k", 1024), ("v", 2048), ("o", 3072), ("xb", 4104), ("gate", 5128)):
            for hb in range(2):
                jobs.append((name, c0 + 512 * hb, 512, hb))
        jobs.append(("gates", 4096, 8, 0))
        cnt = 0
        tcnt = 0
        for ji, (name, c0, ncol, hb) in enumerate(jobs):
            ws, wb = wst[ji % 2], wbf[ji % 2]
            dma(ws.ap[:, :, 0:ncol], win[:, :, c0:c0 + ncol], w=[ws])
            kb.cast(ji, wb.ap[:, :, 0:ncol], ws.ap[:, :, 0:ncol], [ws], [wb])
            if name == "gates":
                gstgs = [A([4, 2, 512], F32) for _ in range(2)]
                for gi, (p0, n) in enumerate(TG):
                    gstg = gstgs[gi % 2]
                    for half in range(2):
                        ps = kb.psum()
                        mm(ps, [(wb.ap[:, kc, 4 * half:4 * half + 4], hn.ap[:, kc, p0:p0 + n]) for kc in range(8)],
                           r=[wb, hnb[gi]], out=ps.ap[0:4, 0:n])
                        op("dve", "tensor_copy", gstg.ap[:, half, 0:n], ps.ap[0:4, 0:n], r=[ps], w=[gstg])
                    dma(giT[:, p0:p0 + n], gstg.ap[:, 0, 0:n], r=[gstg], w=[DB["giT"]])
                    dma(gfT[:, p0:p0 + n], gstg.ap[:, 1, 0:n], r=[gstg], w=[DB["gfT"]])
                continue
            if name in ("q", "k", "xb", "gate"):
                dst_name = {"q": "qT", "k": "kT", "xb": "xbT", "gate": "ggT"}[name]
                isb = name in ("q", "k")
                for gi, (p0, n) in enumerate(TG):
                    next(pcast, None)
                    og = (ostg_b if isb else ostg)[cnt % 2]
                    cnt += 1
                    for oc in range(4):
                        ps = kb.psum()
                        mm(ps, [(wb.ap[:, kc, oc * 128:(oc + 1) * 128], hn.ap[:, kc, p0:p0 + n]) for kc in range(8)],
                           r=[wb, hnb[gi]], out=ps.ap[:, 0:n])
                        if name == "gate":
                            op("act", "activation", out=og.ap[:, oc, 0:n], in_=ps.ap[:, 0:n], func=AF.Gelu_apprx_tanh, r=[ps], w=[og])
                        elif oc % 2 == 0:
                            op("act", "activation", out=og.ap[:, oc, 0:n], in_=ps.ap[:, 0:n], func=AF.Copy, r=[ps], w=[og])
                        else:
                            op("dve", "tensor_copy", og.ap[:, oc, 0:n], ps.ap[:, 0:n], r=[ps], w=[og])
                    dma(fm(kb.dram[dst_name])[:, 4 * hb:4 * hb + 4, p0:p0 + n], og.ap[:, :, 0:n], r=[og], w=[DB[dst_name]])
            if name in ("k", "v", "o"):
                dst_name = {"k": "ktok", "v": "vtok", "o": "otok"}[name]
                for ti, (p0, n) in enumerate(TT):
                    if ti % 4 == 0:
                        next(pcast, None)
                    tg_i = 0 if ti == 0 else 1 + (ti - 1) // 4
                    ts_ = tstg[tcnt % 3]
                    tcnt += 1
                    ps = kb.psum()
                    mm(ps, [(hn.ap[:, kc, p0:p0 + n], wb.ap[:, kc, 0:512]) for kc in range(8)],
                       r=[wb, hnb[tg_i]], out=ps.ap[0:n, :])
                    if name == "o":
                        op("act", "activation", out=ts_.ap[0:n, :], in_=ps.ap[0:n, :], func=AF.Sigmoid, r=[ps], w=[ts_])
                    elif ti % 2 == 0:
                        op("act", "activation", out=ts_.ap[0:n, :], in_=ps.ap[0:n, :], func=AF.Copy, r=[ps], w=[ts_])
                    else:
                        op("dve", "tensor_copy", ts_.ap[0:n, :], ps.ap[0:n, :], r=[ps], w=[ts_])
                    dma(kb.dram[dst_name][p0:p0 + n, 512 * hb:512 * hb + 512], ts_.ap[0:n, :], r=[ts_], w=[DB[dst_name]])
        for _ in pcast:
            pass
        kb.reset_arena()

    NB = 33
    if "D" in kb.phases:
        gi_ = A([4, T], F32); gf_ = A([4, T], F32)
        t2 = A([4, T], F32); t3 = A([4, T], F32); Bc = A([4, T], F32)
        onesr = A([4, 1], F32)
        bi = A([4, 1], F32); bfb = A([4, 1], F32)
        Rb = A([4, NB + 1], F32)
        dec = A([4, NB], F32); decd = A([4, 4, NB], F32)
        gout = A([128, NB * 8 + NB * 4], F32)
        dma(gi_.ap, giT, r=[DB["giT"]], w=[gi_])
        dma(gf_.ap, gfT, r=[DB["gfT"]], w=[gf_])
        dma(bi.ap, ab_if_bias[0:4, :], w=[bi])
        dma(bfb.ap, ab_if_bias[4:8, :], w=[bfb])
        op("pool", "memset", onesr.ap, 1.0, w=[onesr])
        op("pool", "memset", Rb.ap, 0.0, w=[Rb])
        op("pool", "memset", gout.ap, 0.0, w=[gout])
        op("dve", "tensor_scalar", gf_.ap, gf_.ap, bfb.ap, None, ALU.add, r=[gf_, bfb], w=[gf_])
        op("act", "activation", out=t2.ap, in_=gf_.ap, func=AF.Abs, r=[gf_], w=[t2])
        op("act", "activation", out=t2.ap, in_=t2.ap, func=AF.Exp, scale=-1.0, r=[t2], w=[t2])
        op("act", "activation", out=t2.ap, in_=t2.ap, func=AF.Ln, bias=1.0, r=[t2], w=[t2])
        op("dve", "tensor_scalar", t3.ap, gf_.ap, 0.0, None, ALU.min, r=[gf_], w=[t3])
        op("dve", "tensor_tensor", t3.ap, t3.ap, t2.ap, ALU.subtract, r=[t3, t2], w=[t3])
        op("dve", "tensor_tensor_scan", Bc.ap, onesr.ap.to_broadcast([4, T]), t3.ap, 0.0, ALU.mult, ALU.add, r=[onesr, t3], w=[Bc])
        G = gi_; Mx = t2; Rt = t3; beta = gf_; flo = Bc
        op("dve", "scalar_tensor_tensor", G.ap, gi_.ap, bi.ap, Bc.ap, ALU.add, ALU.subtract, r=[gi_, bi, Bc], w=[G])
        op("dve", "tensor_tensor_scan", Mx.ap, G.ap, G.ap, 0.0, ALU.max, ALU.max, r=[G], w=[Mx])
        op("dve", "tensor_copy", Rb.ap[:, 1:2], Mx.ap[:, 15:16], r=[Mx], w=[Rb])
        op("dve", "tensor_copy", Rb.ap[:, 2:NB + 1], Mx.ap[:, 16:T].rearrange("p (b s) -> p b s", s=128)[:, :, 127], r=[Mx], w=[Rb])
        op("dve", "tensor_copy", Rt.ap[:, 0:16], Rb.ap[:, 1:2].to_broadcast([4, 16]), r=[Rb], w=[Rt])
        op("dve", "tensor_copy", Rt.ap[:, 16:T].rearrange("p (b s) -> p b s", s=128),
           Rb.ap[:, 2:NB + 1].unsqueeze(2).to_broadcast([4, 32, 128]), r=[Rb], w=[Rt])
        op("dve", "tensor_tensor", beta.ap, G.ap, Rt.ap, ALU.subtract, r=[G, Rt], w=[beta])
        op("act", "activation", out=beta.ap, in_=beta.ap, func=AF.Exp, r=[beta], w=[beta])
        op("dve", "tensor_scalar", beta.ap, beta.ap, 1.0 / 16, None, ALU.mult, r=[beta], w=[beta])
        op("dve", "tensor_tensor", flo.ap, Bc.ap, Rt.ap, ALU.add, r=[Bc, Rt], w=[flo])
        op("act", "activation", out=flo.ap, in_=flo.ap, func=AF.Exp, scale=-1.0, r=[flo], w=[flo])
        op("dve", "tensor_tensor", dec.ap, Rb.ap[:, 0:NB], Rb.ap[:, 1:NB + 1], ALU.subtract, r=[Rb], w=[dec])
        op("act", "activation", out=dec.ap, in_=dec.ap, func=AF.Exp, r=[dec], w=[dec])
        op("dve", "tensor_tensor", decd.ap, dec.ap.unsqueeze(1).to_broadcast([4, 4, NB]),
           ident.ap[0:4, 0:4].unsqueeze(2).to_broadcast([4, 4, NB]), ALU.mult, r=[dec, ident], w=[decd])
        ps = kb.psum()
        for ti, (p0, n) in enumerate(TT):
            op("pe", "transpose", ps.ap[0:n, ti * 8:ti * 8 + 4], beta.ap[:, p0:p0 + n], ident.ap[0:4, 0:4], r=[beta, ident], w=[ps], inc=False)
            op("pe", "transpose", ps.ap[0:n, ti * 8 + 4:ti * 8 + 8], flo.ap[:, p0:p0 + n], ident.ap[0:4, 0:4], r=[flo, ident], w=[ps], inc=(ti == NB - 1))
        op("dve", "tensor_copy", gout.ap[:, 8:NB * 8], ps.ap[:, 8:NB * 8], r=[ps], w=[gout])
        op("dve", "tensor_copy", gout.ap[0:16, 0:8], ps.ap[0:16, 0:8], r=[ps], w=[gout])
        ps2 = kb.psum()
        mm(ps2, [(ones_f.ap[0:4, :], decd.ap.rearrange("p a b -> p (a b)"))], r=[ones_f, decd], out=ps2.ap[:, 0:4 * NB])
        op("dve", "tensor_copy", gout.ap[:, NB * 8:NB * 12], ps2.ap[:, 0:4 * NB], r=[ps2], w=[gout])
        dma(gprep, gout.ap, r=[gout], w=[DB["gprep"]])
        kb.reset_arena()

    genE = genF = None
    if "E" in kb.phases:
        gp = A([128, NB * 12], F32)
        dma(gp.ap, gprep, r=[DB["gprep"]], w=[gp])
        bfv = gp.ap[:, 0:NB * 8].rearrange("p (b e) -> p b e", e=8)
        decv = gp.ap[:, NB * 8:NB * 12].rearrange("p (h b) -> p h b", h=4)
        gnb = A([128, D], F32)
        dma(gnb.ap, mlstm_norm.partition_broadcast(128), w=[gnb])
        Cst = A([128, 4, 2, 257], F32)
        Cd = A([128, 4, 2, 257], BF16)
        CstB = [Buf() for _ in range(4)]
        CdB = [Buf() for _ in range(4)]
        NBUF = 2
        qTb = [A([128, 8, 128], BF16) for _ in range(NBUF)]
        kTb = [A([128, 8, 128], BF16) for _ in range(NBUF)]
        ktb = [A([128, D], BF16) for _ in range(NBUF)]
        otb = [A([128, D], BF16) for _ in range(NBUF)]
        vab = [A([128, 4, 257], BF16) for _ in range(NBUF)]
        for v_ in vab:
            op("pool", "memset", v_.ap[:, :, 256:257], 1.0, w=[v_])
        Sm = [A([128, 128], BF16) for _ in range(4)]
        ktl = [A([128, 256], BF16) for _ in range(4)]
        ha = [A([128, 256], F32) for _ in range(2)]
        junk = A([128, 256], F32)
        sm1 = [A([128, 4], F32) for _ in range(2)]
        yst = [A([128, 8, 128], BF16) for _ in range(2)]
        yrow = fm(yTa)
        it = 0

        def loads(b):
            p0, n = TT[b]
            s = b % NBUF
            dma(qTb[s].ap[:, :, 0:n], fm(qT)[:, :, p0:p0 + n], r=[DB["qT"]], w=[qTb[s]])
            dma(kTb[s].ap[:, :, 0:n], fm(kT)[:, :, p0:p0 + n], r=[DB["kT"]], w=[kTb[s]])
            dma(ktb[s].ap[0:n, :], ktok[p0:p0 + n, :], r=[DB["ktok"]], w=[ktb[s]])
            dma(otb[s].ap[0:n, :], otok[p0:p0 + n, :], r=[DB["otok"]], w=[otb[s]])
            dma(vab[s].ap[0:n, :, 0:256], vtok[p0:p0 + n, :].rearrange("t (h v) -> t h v", h=4), r=[DB["vtok"]], w=[vab[s]])

        loads(0)
        han = [A([128, 256], BF16) for _ in range(4)]
        flat = [(b, h) for b in range(NB) for h in range(4)]
        pS_slots = [kb.ps[0]] * 4

        def s1(i):
            b, h = flat[i]
            p0, n = TT[b]
            s = b % NBUF
            i2 = i % 4
            beta_c = bfv[0:n, b, h:h + 1]
            pS = pS_slots[i % 4]
            c0 = (i % 4) * 128
            mm(pS, [(kTb[s].ap[:, 2 * h + dc, 0:n], qTb[s].ap[:, 2 * h + dc, 0:n]) for dc in range(2)],
               r=[kTb[s], qTb[s]], out=pS.ap[0:n, c0:c0 + n])
            op("dve", "scalar_tensor_tensor", Sm[i2].ap[0:n, 0:n], pS.ap[0:n, c0:c0 + n], beta_c, tri.ap[0:n, 0:n],
               ALU.mult, ALU.mult, r=[pS, gp, tri], w=[Sm[i2]])
            op("act", "activation", out=ktl[i2].ap[0:n, :], in_=ktb[s].ap[0:n, h * 256:(h + 1) * 256], func=AF.Copy,
               scale=beta_c, r=[ktb[s], gp], w=[ktl[i2]])

        def s2(i):
            b, h = flat[i]
            p0, n = TT[b]
            s = b % NBUF
            i2 = i % 2
            i4 = i % 4
            cstT = Tl(Cst.ap, CstB[h]); cdT = Tl(Cd.ap, CdB[h])
            floor_c = bfv[0:n, b, 4 + h:5 + h]
            pN = kb.ps[1 + i % 2]
            pairs = []
            if b > 0:
                pairs += [(qTb[s].ap[:, 2 * h + dc, 0:n], Cd.ap[:, h, dc, :]) for dc in range(2)]
            pairs.append((Sm[i4].ap[0:n, 0:n], vab[s].ap[0:n, h, :]))
            mm(pN, pairs, r=[qTb[s], cdT, Sm[i4], vab[s]], out=pN.ap[0:n, 0:257])
            pC = [kb.ps[3 + (i % 2) * 2 + dc] for dc in range(2)]
            for dc in range(2):
                mm(pC[dc], [(ktl[i4].ap[0:n, dc * 128:(dc + 1) * 128], vab[s].ap[0:n, h, :])], r=[ktl[i4], vab[s]],
                   out=pC[dc].ap[:, 0:257])
            yield
            for dc in range(2):
                if b == 0:
                    op("dve", "tensor_copy", Cst.ap[:, h, dc, :], pC[dc].ap[:, 0:257], r=[pC[dc]], w=[cstT])
                else:
                    op("dve", "scalar_tensor_tensor", Cst.ap[:, h, dc, :], Cst.ap[:, h, dc, :], decv[:, h, b:b + 1],
                       pC[dc].ap[:, 0:257], ALU.mult, ALU.add, r=[pC[dc], gp, cstT], w=[cstT])
            if b + 1 < NB:
                op("act", "activation", out=Cd.ap[:, h, :, :], in_=Cst.ap[:, h, :, :], func=AF.Copy,
                   scale=decv[:, h, b + 1:b + 2], r=[cstT, gp], w=[cdT])
            sm = sm1[i2]
            yield
            op("act", "activation", out=sm.ap[0:n, 0:1], in_=pN.ap[0:n, 256:257], func=AF.Abs, r=[pN], w=[sm])
            yield
            op("dve", "tensor_scalar", sm.ap[0:n, 0:1], sm.ap[0:n, 0:1], floor_c, None, ALU.max, r=[sm, gp], w=[sm])
            op("dve", "reciprocal", sm.ap[0:n, 0:1], sm.ap[0:n, 0:1], r=[sm], w=[sm])
            yield
            op("dve", "scalar_tensor_tensor", ha[i2].ap[0:n, :], pN.ap[0:n, 0:256], sm.ap[0:n, 0:1],
               otb[s].ap[0:n, h * 256:(h + 1) * 256], ALU.mult, ALU.mult, r=[pN, sm, otb[s]], w=[ha[i2]])
            yield
            op("act", "activation", out=junk.ap[0:n, :], in_=ha[i2].ap[0:n, :], func=AF.Square, accum_out=sm.ap[0:n, 1:2],
               r=[ha[i2]], w=[sm])
            op("act", "activation", out=sm.ap[0:n, 2:3], in_=sm.ap[0:n, 1:2], func=AF.Sqrt, scale=1.0 / 256, bias=epsc.ap[0:n, :],
               r=[sm, epsc], w=[sm])
            yield
            op("dve", "reciprocal", sm.ap[0:n, 2:3], sm.ap[0:n, 2:3], r=[sm], w=[sm])
            hn_ = han[i % 4]
            op("dve", "scalar_tensor_tensor", hn_.ap[0:n, :], ha[i2].ap[0:n, :], sm.ap[0:n, 2:3],
               gnb.ap[0:n, h * 256:(h + 1) * 256], ALU.mult, ALU.mult, r=[ha[i2], sm, gnb], w=[hn_])

        def s4(i):
            b, h = flat[i]
            p0, n = TT[b]
            ys = yst[b % 2]
            hn_ = han[i % 4]
            pT = kb.ps[7]
            pTb = pT.ap.bitcast(BF16)
            for vc in range(2):
                op("pe", "transpose", pTb[:, vc * 128:vc * 128 + n], hn_.ap[0:n, vc * 128:(vc + 1) * 128],
                   identb.ap[0:n, 0:n], r=[hn_, identb], w=[pT], inc=(vc == 1))
            op("act", "activation", out=ys.ap[:, 2 * h:2 * h + 2, 0:n],
               in_=pTb[:, 0:256].rearrange("p (a b) -> p a b", a=2)[:, :, 0:n], func=AF.Copy, r=[pT], w=[ys])
            if h == 3:
                dma(yrow[:, 0:8, p0:p0 + n], ys.ap[:, :, 0:n], r=[ys], w=[DB["yTa"]])

        NF = len(flat)

        def genE_():
            s1(0)
            s1(1)
            for i0_ in range(0, NF, 2):
                b, h = flat[i0_]
                if h == 0 and b + 1 < NB:
                    loads(b + 1)
                for j in (i0_ + 2, i0_ + 3):
                    if j < NF:
                        s1(j)
                gs = [s2(i0_), s2(i0_ + 1)]
                while gs:
                    for g_ in list(gs):
                        try:
                            next(g_)
                        except StopIteration:
                            gs.remove(g_)
                for j in (i0_ - 2, i0_ - 1):
                    if j >= 0:
                        s4(j)
                yield
            s4(NF - 2)
            s4(NF - 1)
            yield
        genE = genE_()

    if "F" in kb.phases:
        t8 = A([128, 8], F32); t8b = A([128, 8], F32)
        lam_v = lrup.ap[:, :, 7]
        op("act", "activation", out=t8.ap, in_=lam_v, func=AF.Abs, r=[lrup], w=[t8])
        op("act", "activation", out=t8.ap, in_=t8.ap, func=AF.Exp, scale=-1.0, r=[t8], w=[t8])
        op("act", "activation", out=t8.ap, in_=t8.ap, func=AF.Ln, bias=1.0, r=[t8], w=[t8])
        op("dve", "tensor_scalar", t8b.ap, lam_v, 0.0, None, ALU.min, r=[lrup], w=[t8b])
        op("dve", "tensor_tensor", t8b.ap, t8b.ap, t8.ap, ALU.subtract, r=[t8, t8b], w=[t8b])
        op("dve", "tensor_scalar", clam.ap, t8b.ap, 8.0, None, ALU.mult, r=[t8b], w=[clam])
        wri_s = A([128, 16, 128], F32)
        wri = A([128, 16, 128], BF16)
        dma(wri_s.ap[:, 0:8, :], lru_w_r.rearrange("n d e -> d n e"), w=[wri_s])
        dma(wri_s.ap[:, 8:16, :], lru_w_i.rearrange("n d e -> d n e"), w=[wri_s])
        op("pool", "tensor_copy", wri.ap, wri_s.ap, r=[wri_s], w=[wri])
        xb = A([128, T], F32); gg = A([128, T], F32)
        xc = A([128, T], F32); xcb = A([128, T], BF16)
        r_ = A([128, T], F32); i_ = A([128, T], F32); s_ = A([128, T], F32)
        hb_ = [A([128, T], BF16) for _ in range(2)]

        clam2 = A([128, 8], F32)
        op("dve", "tensor_scalar", clam2.ap, clam.ap, 2.0, None, ALU.mult, r=[clam], w=[clam2])
        NP = len(TG)
        pb = {nm: [Buf() for _ in range(NP)] for nm in ("xc", "xcb", "r", "i", "s", "hb0", "hb1")}

        def genF_():
            dma(xb.ap, xbT[0:128, :], r=[DB["xbT"]], w=[xb])
            dma(gg.ap, ggT[0:128, :], r=[DB["ggT"]], w=[gg])

            def tl(n_, gi):
                hb = hb_[n_ % 2]
                return (Tl(xc.ap, pb["xc"][gi]), Tl(xcb.ap, pb["xcb"][gi]), Tl(r_.ap, pb["r"][gi]),
                        Tl(i_.ap, pb["i"][gi]), Tl(s_.ap, pb["s"][gi]), Tl(hb.ap, pb["hb%d" % (n_ % 2)][gi]))

            def stage1(n_, gi):
                p0, n = TG[gi]
                pv = lambda i: lrup.ap[:, n_, i:i + 1]
                sl = slice(p0, p0 + n)
                xcT, xcbT, rT, iT, sT, hbT = tl(n_, gi)
                op("dve", "tensor_scalar", xc.ap[:, sl], xb.ap[:, sl], pv(3), pv(4), ALU.mult, ALU.add, r=[xb, lrup], w=[xcT])
                for j in (1, 2, 3):
                    lo = max(p0, j)
                    op("dve", "scalar_tensor_tensor", xc.ap[:, lo:p0 + n], xb.ap[:, lo - j:p0 + n - j], pv(3 - j), xc.ap[:, lo:p0 + n],
                       ALU.mult, ALU.add, r=[xb, lrup, xcT], w=[xcT])
                op("act", "activation", out=xcb.ap[:, sl], in_=xc.ap[:, sl], func=AF.Copy, r=[xcT], w=[xcbT])
                for which, dst, dT, bidx in ((0, r_, rT, 5), (1, i_, iT, 6)):
                    ps = kb.psum_from("F", [5, 6])
                    mm(ps, [(wri.ap[:, which * 8 + n_, :], xcb.ap[:, sl])], r=[wri, xcbT], out=ps.ap[:, 0:n])
                    op("act", "activation", out=dst.ap[:, sl], in_=ps.ap[:, 0:n], func=AF.Sigmoid, bias=pv(bidx),
                       r=[ps, lrup], w=[dT])

            def stage4(n_, gi):
                p0, n = TG[gi]
                sl = slice(p0, p0 + n)
                hb = hb_[n_ % 2]
                xcT, xcbT, rT, iT, sT, hbT = tl(n_, gi)
                op("pool", "tensor_tensor", i_.ap[:, sl], i_.ap[:, sl], xc.ap[:, sl], ALU.mult, r=[iT, xcT], w=[iT])
                op("pool", "tensor_tensor", i_.ap[:, sl], i_.ap[:, sl], s_.ap[:, sl], ALU.mult, r=[iT, sT], w=[iT])
                init = 0.0 if gi == 0 else s_.ap[:, p0 - 1:p0]
                rd = [rT, iT] + ([Tl(s_.ap, pb["s"][gi - 1])] if gi > 0 else [])
                op("dve", "tensor_tensor_scan", s_.ap[:, sl], r_.ap[:, sl], i_.ap[:, sl], init, ALU.mult, ALU.add, r=rd, w=[sT])
                op("pool", "tensor_tensor", hb.ap[:, sl], s_.ap[:, sl], gg.ap[:, sl], ALU.mult, r=[sT, gg], w=[hbT])

            for gi in range(NP):
                stage1(0, gi)
            yield
            for n_ in range(8):
                if n_ + 1 < 8:
                    dma(xb.ap, xbT[(n_ + 1) * 128:(n_ + 2) * 128, :], r=[DB["xbT"]], w=[xb])
                for gi, (p0, n) in enumerate(TG):
                    sl = slice(p0, p0 + n)
                    xcT, xcbT, rT, iT, sT, hbT = tl(n_, gi)
                    op("act", "activation", out=s_.ap[:, sl], in_=r_.ap[:, sl], func=AF.Exp, scale=clam2.ap[:, n_:n_ + 1], r=[rT, clam2], w=[sT])
                    op("act", "activation", out=r_.ap[:, sl], in_=r_.ap[:, sl], func=AF.Exp, scale=clam.ap[:, n_:n_ + 1], r=[rT, clam], w=[rT])
                for gi, (p0, n) in enumerate(TG):
                    sl = slice(p0, p0 + n)
                    xcT, xcbT, rT, iT, sT, hbT = tl(n_, gi)
                    op("act", "activation", out=s_.ap[:, sl], in_=s_.ap[:, sl], func=AF.Sqrt, scale=-1.0, bias=1.0, r=[sT], w=[sT])
                yield
                for gi in range(NP):
                    stage4(n_, gi)
                    if n_ + 1 < 8 and gi >= 1:
                        stage1(n_ + 1, gi - 1)
                if n_ + 1 < 8:
                    stage1(n_ + 1, NP - 1)
                    dma(gg.ap, ggT[(n_ + 1) * 128:(n_ + 2) * 128, :], r=[DB["ggT"]], w=[gg])
                dma(yTb[n_ * 128:(n_ + 1) * 128, :], hb_[n_ % 2].ap, r=pb["hb%d" % (n_ % 2)], w=[DB["yTb"]])
                yield
        genF = genF_()

    if genE is not None or genF is not None:
        for g_ in (genE, genF):
            if g_ is not None:
                for _ in g_:
                    pass
        kb.reset_arena()

    def out_proj(y_names, KC, w_dram, src_name, dst_name):
        wb = A([128, KC, D], BF16)
        wst = [A([128, 4, D], F32) for _ in range(2)]
        wv = w_dram.rearrange("(c p) n -> p c n", p=128)
        for c4 in range(KC // 4):
            ws = wst[c4 % 2]
            dma(ws.ap, wv[:, 4 * c4:4 * c4 + 4, :], w=[ws])
            kb.cast(c4, wb.ap[:, 4 * c4:4 * c4 + 4, :], ws.ap, [ws], [wb])
        yb = [A([128, KC, 512], BF16) for _ in range(2)]
        xt = [A([128, 8, 512], F32) for _ in range(2)]
        hs = fm(kb.dram[src_name]); hd = fm(kb.dram[dst_name])

        def ld(gi):
            p0, n = TG[gi]
            for yi, yn in enumerate(y_names):
                dma(yb[gi % 2].ap[:, 8 * yi:8 * yi + 8, 0:n], fm(kb.dram[yn])[:, :, p0:p0 + n], r=[DB[yn]], w=[yb[gi % 2]])
            dma(xt[gi % 2].ap[:, :, 0:n], hs[:, :, p0:p0 + n], r=[DB[src_name]], w=[xt[gi % 2]])

        ld(0)
        for gi, (p0, n) in enumerate(TG):
            if gi + 1 < len(TG):
                ld(gi + 1)
            y_, x_ = yb[gi % 2], xt[gi % 2]
            for oc in range(8):
                ps = kb.psum()
                mm(ps, [(wb.ap[:, kc, oc * 128:(oc + 1) * 128], y_.ap[:, kc, 0:n]) for kc in range(KC)], r=[wb, y_], out=ps.ap[:, 0:n])
                op("dve", "tensor_tensor", x_.ap[:, oc, 0:n], ps.ap[:, 0:n], x_.ap[:, oc, 0:n], ALU.add, r=[ps, x_], w=[x_])
            dma(hd[:, :, p0:p0 + n], x_.ap[:, :, 0:n], r=[x_], w=[DB[dst_name]])
        kb.reset_arena()

    if "G" in kb.phases:
        out_proj(["yTa", "yTb"], 16, ab_w_out, "hT0", "hT1")

    def mlp(layer, src_name, dst_name):
        w1b = A([128, 8, 4096], BF16)
        w2b = A([128, 32, D], BF16)
        wst = [A([128, 2048], F32) for _ in range(2)]
        w1v = mlp_w1[layer].rearrange("(c p) n -> p c n", p=128)
        w2v = mlp_w2[layer].rearrange("(c p) n -> p c n", p=128)
        w1B = [Buf() for _ in range(8)]
        w2B = [Buf() for _ in range(16)]
        pre = ("C" if layer == 0 else "J0") in kb.phases
        if pre:
            n1, n2 = "w1s%d" % layer, "w2s%d" % layer
            s1v = w1s[layer].rearrange("(c p) n -> p c n", p=128)
            s2v = w2s[layer].rearrange("(c p) n -> p c n", p=128)
            for kc in range(8):
                dma(w1b.ap[:, kc, :], s1v[:, kc, :], r=[DB[n1]], w=[w1B[kc]])
            for c2 in range(8):
                dma(w2b.ap[:, 4 * c2:4 * c2 + 4, :], s2v[:, 4 * c2:4 * c2 + 4, :], r=[DB[n2]], w=[w2B[2 * c2], w2B[2 * c2 + 1]])
        k = 0
        for kc in range(8 if not pre else 0):
            for hf in range(2):
                ws = wst[k % 2]
                dma(ws.ap, w1v[:, kc, hf * 2048:(hf + 1) * 2048], w=[ws])
                kb.cast(k, w1b.ap[:, kc, hf * 2048:(hf + 1) * 2048], ws.ap, [ws], [w1B[kc]])
                k += 1
        for c2 in range(16 if not pre else 0):
            ws = wst[k % 2]
            wsv = ws.ap.rearrange("p (a b) -> p a b", a=2)
            dma(wsv, w2v[:, 2 * c2:2 * c2 + 2, :], w=[ws])
            kb.cast(k, w2b.ap[:, 2 * c2:2 * c2 + 2, :], wsv, [ws], [w2B[c2]])
            k += 1
        NG = 256
        xts = [A([128, 8, NG], F32) for _ in range(2)]
        aT = A([128, 32, NG], BF16)
        sq = A([128, 8, NG], BF16)
        hns = [A([128, 8, NG], BF16) for _ in range(2)]
        rstds = [A([128, NG], F32) for _ in range(2)]
        tmp = [A([128, NG], F32) for _ in range(2)]
        hs = fm(kb.dram[src_name]); hd = fm(kb.dram[dst_name])
        gidx = 2 + layer
        NGR = len(TG2)

        def ld(gi):
            p0, n = TG2[gi]
            dma(xts[gi % 2].ap[:, :, 0:n], hs[:, :, p0:p0 + n], r=[DB[src_name]], w=[xts[gi % 2]])

        def nrm(gi):
            p0, n = TG2[gi]
            hn = hns[gi % 2]
            norm_group(xts[gi % 2], n, gidx, HnOut(hn, lambda kc, n=n, hn=hn: hn.ap[:, kc, 0:n]), sq, rstds[gi % 2])

        ld(0)
        ld(1)
        nrm(0)
        tc_ = 0
        for gi, (p0, n) in enumerate(TG2):
            xt = xts[gi % 2]
            hn = hns[gi % 2]
            for fc in range(32):
                ps = kb.psum()
                mm(ps, [(w1b.ap[:, kc, fc * 128:(fc + 1) * 128], hn.ap[:, kc, 0:n]) for kc in range(8)], r=w1B + [hn], out=ps.ap[:, 0:n])
                tm_ = tmp[tc_ % 2]
                op("act", "activation", out=tm_.ap[:, 0:n], in_=ps.ap[:, 0:n], func=AF.Relu, r=[ps], w=[tm_])
                op("pool" if tc_ % 2 == 0 else "dve", "tensor_tensor", aT.ap[:, fc, 0:n], tm_.ap[:, 0:n], tm_.ap[:, 0:n], ALU.mult, r=[tm_], w=[aT])
                tc_ += 1
            if gi + 1 < NGR:
                nrm(gi + 1)
            for oc in range(8):
                ps = kb.psum()
                mm(ps, [(w2b.ap[:, fc, oc * 128:(oc + 1) * 128], aT.ap[:, fc, 0:n]) for fc in range(32)], r=w2B + [aT], out=ps.ap[:, 0:n])
                op("dve", "tensor_tensor", xt.ap[:, oc, 0:n], ps.ap[:, 0:n], xt.ap[:, oc, 0:n], ALU.add, r=[ps, xt], w=[xt])
            dma(hd[:, :, p0:p0 + n], xt.ap[:, :, 0:n], r=[xt], w=[DB[dst_name]])
            if gi + 2 < NGR:
                ld(gi + 2)
        kb.reset_arena()

    if "H" in kb.phases:
        mlp(0, "hT1", "hT2")

    if "I" in kb.phases:
        hn, hnb = norm_full("hT2", 1)
        wb = A([128, 8, 3072], BF16)
        wst = [A([128, 4, 1024], F32) for _ in range(2)]
        cwv = c_w_in.rearrange("(c p) n -> p c n", p=128)
        wB = [Buf() for _ in range(6)]
        k = 0
        for cb in range(3):
            for c4 in range(2):
                ws = wst[k % 2]
                dma(ws.ap, cwv[:, 4 * c4:4 * c4 + 4, cb * 1024:(cb + 1) * 1024], w=[ws])
                kb.cast(k, wb.ap[:, 4 * c4:4 * c4 + 4, cb * 1024:(cb + 1) * 1024], ws.ap, [ws], [wB[k]])
                k += 1
        cosT = A([128, 33, 8], F32); sinT = A([128, 33, 8], F32)
        dma(cosT.ap, c_cos, w=[cosT]); dma(sinT.ap, c_sin, w=[sinT])
        qk = [A([128, 2048], F32) for _ in range(2)]
        qkb = [A([128, 2048], BF16) for _ in range(2)]
        rt = [A([128, 32, 8], F32) for _ in range(4)]
        vst = [A([128, 1024], BF16) for _ in range(2)]
        qst = [A([128, 16, 128], BF16) for _ in range(2)]
        pend_tr = []
        for ti, (p0, n) in enumerate(TT):
            tg_i = 0 if ti == 0 else 1 + (ti - 1) // 4
            q_, qb_, v_, qs_ = qk[ti % 2], qkb[ti % 2], vst[ti % 2], qst[ti % 2]
            for blk in range(6):
                ps = kb.psum()
                mm(ps, [(hn.ap[:, kc, p0:p0 + n], wb.ap[:, kc, blk * 512:(blk + 1) * 512]) for kc in range(8)], r=wB + [hnb[tg_i]], out=ps.ap[0:n, :])
                if blk < 2:
                    op("act", "activation", out=q_.ap[0:n, blk * 512:(blk + 1) * 512], in_=ps.ap[0:n, :], func=AF.Copy, scale=0.125, r=[ps], w=[q_])
                elif blk < 4:
                    op("dve", "tensor_copy", q_.ap[0:n, blk * 512:(blk + 1) * 512], ps.ap[0:n, :], r=[ps], w=[q_])
                else:
                    op("act", "activation", out=v_.ap[0:n, (blk - 4) * 512:(blk - 3) * 512], in_=ps.ap[0:n, :], func=AF.Copy, r=[ps], w=[v_])
            dma(vtok2[p0:p0 + n, :], v_.ap[0:n, :], r=[v_], w=[DB["vtok2"]])
            qv = q_.ap.rearrange("p (g d) -> p g d", d=64)
            x1 = qv[0:n, :, 0:8]; x2 = qv[0:n, :, 8:16]
            cb_ = cosT.ap[0:n, ti, :].unsqueeze(1).to_broadcast([n, 32, 8])
            sb_ = sinT.ap[0:n, ti, :].unsqueeze(1).to_broadcast([n, 32, 8])
            a1, a2, a3, a4 = [t_.ap[0:n] for t_ in rt]
            op("pool", "tensor_tensor", a1, x1, cb_, ALU.mult, r=[q_, cosT], w=[rt[0]])
            op("pool", "tensor_tensor", a2, x2, sb_, ALU.mult, r=[q_, sinT], w=[rt[1]])
            op("dve", "tensor_tensor", a3, x2, cb_, ALU.mult, r=[q_, cosT], w=[rt[2]])
            op("dve", "tensor_tensor", a4, x1, sb_, ALU.mult, r=[q_, sinT], w=[rt[3]])
            op("pool", "tensor_tensor", x1, a1, a2, ALU.subtract, r=[rt[0], rt[1]], w=[q_])
            op("dve", "tensor_tensor", x2, a3, a4, ALU.add, r=[rt[2], rt[3]], w=[q_])
            op("act", "activation", out=qb_.ap[0:n, :], in_=q_.ap[0:n, :], func=AF.Copy, r=[q_], w=[qb_])

            def trn(ti=ti, p0=p0, n=n, qb_=qb_, qs_=qs_):
                for c4 in range(4):
                    pT = kb.psum()
                    pTb = pT.ap.bitcast(BF16)
                    for j in range(4):
                        c = c4 * 4 + j
                        op("pe", "transpose", pTb[:, j * 128:j * 128 + n], qb_.ap[0:n, c * 128:(c + 1) * 128], identb.ap[0:n, 0:n],
                           r=[qb_, identb], w=[pT], inc=(j == 3))
                    src = pTb[:, 0:512].rearrange("p (a b) -> p a b", a=4)[:, :, 0:n]
                    if c4 % 2 == 0:
                        op("act", "activation", out=qs_.ap[:, 4 * c4:4 * c4 + 4, 0:n], in_=src, func=AF.Copy, r=[pT], w=[qs_])
                    else:
                        op("dve", "tensor_copy", qs_.ap[:, 4 * c4:4 * c4 + 4, 0:n], src, r=[pT], w=[qs_])
                dma(fm(qkT)[:, :, p0:p0 + n], qs_.ap[:, :, 0:n], r=[qs_], w=[DB["qkT"]])
            pend_tr.append(trn)
            if len(pend_tr) > 1:
                pend_tr.pop(0)()
        while pend_tr:
            pend_tr.pop(0)()
        kb.reset_arena()

    if "J0" in kb.phases:
        lv = A([1, 256], F32); l2 = A([1, 4], F32); lam_bc = A([128, 1], F32)
        dma(lv.ap, c_lambda, w=[lv])
        op("dve", "tensor_tensor", lv.ap[:, 0:64], lv.ap[:, 0:64], lv.ap[:, 64:128], ALU.mult, r=[lv], w=[lv])
        op("dve", "tensor_tensor", lv.ap[:, 128:192], lv.ap[:, 128:192], lv.ap[:, 192:256], ALU.mult, r=[lv], w=[lv])
        op("dve", "reduce_sum", l2.ap[:, 0:1], lv.ap[:, 0:64], AX.X, r=[lv], w=[l2])
        op("dve", "reduce_sum", l2.ap[:, 1:2], lv.ap[:, 128:192], AX.X, r=[lv], w=[l2])
        op("act", "activation", out=l2.ap[:, 0:2], in_=l2.ap[:, 0:2], func=AF.Exp, r=[l2], w=[l2])
        op("dve", "tensor_tensor", l2.ap[:, 2:3], l2.ap[:, 1:2], l2.ap[:, 0:1], ALU.subtract, r=[l2], w=[l2])
        op("dve", "tensor_scalar", l2.ap[:, 2:3], l2.ap[:, 2:3], -LAMBDA_INIT, None, ALU.add, r=[l2], w=[l2])
        psl = kb.psum()
        mm(psl, [(ones_f.ap[0:1, :], l2.ap[0:1, 2:3])], r=[ones_f, l2], out=psl.ap[:, 0:1])
        op("dve", "tensor_copy", lam_bc.ap, psl.ap[:, 0:1], r=[psl], w=[lam_bc])
        sln = A([128, 128], F32)
        dma(sln.ap, c_subln.partition_broadcast(128), w=[sln])
        op("dve", "tensor_scalar", sln.ap, sln.ap, 1.0 - LAMBDA_INIT, None, ALU.mult, r=[sln], w=[sln])
        sel = A([128, 2], BF16)
        op("pool", "memset", sel.ap, 0.0, w=[sel])
        op("pool", "memset", sel.ap[0:64, 0:1], 1.0, w=[sel])
        op("pool", "memset", sel.ap[64:128, 1:2], 1.0, w=[sel])
        qTh = [A([128, T], BF16) for _ in range(2)]
        kTh = [A([128, T], BF16) for _ in range(2)]
        Vh = [A([128, 33, 129], BF16) for _ in range(2)]
        for v_ in Vh:
            op("pool", "memset", v_.ap[:, :, 128:129], 1.0, w=[v_])
        sqt = A([128, T], BF16)
        ssq = A([2, 2, T], F32)
        mx = A([2, 4], F32)
        shift = A([128, 2], F32)
        mxd = A([2, 2], F32)
        PT = [A([128, 512], BF16) for _ in range(4)]
        gbuf = []
        for _ in range(2):
            gbuf.append({"Os": [A([128, 512], F32) for _ in range(2)], "rbc": [A([128, 512], F32) for _ in range(2)],
                         "sq": A([128, 512], BF16), "rstd": A([128, 512], F32), "on": A([128, 512], BF16)})
        accs = [A([128, 512], F32) for _ in range(2)]
        accbs = [A([128, 512], BF16) for _ in range(2)]
        onesq = A([128, 128], BF16)
        op("pool", "memset", onesq.ap, 1.0, w=[onesq])
        qz = [[A([128, T], BF16) for _ in range(2)] for _ in range(2)]
        for qq in qz:
            op("pool", "memset", qq[0].ap[64:128, :], 0.0, w=[qq[0]])
            op("pool", "memset", qq[1].ap[0:64, :], 0.0, w=[qq[1]])
        one1 = A([128, 1], BF16)
        op("pool", "memset", one1.ap, 1.0, w=[one1])
        o128 = A([128, 128], BF16)
        op("pool", "memset", o128.ap, 1.0 / 128, w=[o128])
        slc = A([128, 1], F32)
        dma(slc.ap, c_subln.rearrange("o (p u) -> p (o u)", u=1), w=[slc])
        op("dve", "tensor_scalar", slc.ap, slc.ap, 1.0 - LAMBDA_INIT, None, ALU.mult, r=[slc], w=[slc])
        qkv = fm(qkT)

        def aload(h):
            s = h % 2
            dma(qTh[s].ap, qkT[h * 128:(h + 1) * 128, :], r=[DB["qkT"]], w=[qTh[s]])
            dma(kTh[s].ap, qkT[1024 + h * 128:1024 + (h + 1) * 128, :], r=[DB["qkT"]], w=[kTh[s]])
            dma(qz[s][0].ap[0:64, :], qkT[h * 128:h * 128 + 64, :], r=[DB["qkT"]], w=[qz[s][0]])
            dma(qz[s][1].ap[64:128, :], qkT[h * 128 + 64:(h + 1) * 128, :], r=[DB["qkT"]], w=[qz[s][1]])
            dma(Vh[s].ap[0:16, 0, 0:128], vtok2[0:16, h * 128:(h + 1) * 128], r=[DB["vtok2"]], w=[Vh[s]])
            dma(Vh[s].ap[:, 1:33, 0:128], vtok2[16:T, h * 128:(h + 1) * 128].rearrange("(j p) c -> p j c", p=128), r=[DB["vtok2"]], w=[Vh[s]])

        aload(0)
        pcast = gen_precast(1)
        pti = 0
        oi = 0
        for h in range(8):
            if h + 1 < 8:
                aload(h + 1)
            s = h % 2
            qt, kt, vh = qTh[s], kTh[s], Vh[s]
            for which, src in ((0, qt), (1, kt)):
                op("pool", "tensor_tensor", sqt.ap, src.ap, src.ap, ALU.mult, r=[src], w=[sqt])
                for gi, (p0, n) in enumerate(TG):
                    ps = kb.psum()
                    mm(ps, [(sel.ap, sqt.ap[:, p0:p0 + n])], r=[sel, sqt], out=ps.ap[0:2, 0:n])
                    op("dve", "tensor_copy", ssq.ap[:, which, p0:p0 + n], ps.ap[0:2, 0:n], r=[ps], w=[ssq])
                op("dve", "reduce_max", mx.ap[:, which:which + 1], ssq.ap[:, which, :], AX.X, r=[ssq], w=[mx])
            op("dve", "tensor_tensor", mx.ap[:, 2:3], mx.ap[:, 0:1], mx.ap[:, 1:2], ALU.mult, r=[mx], w=[mx])
            op("act", "activation", out=mx.ap[:, 3:4], in_=mx.ap[:, 2:3], func=AF.Ln, scale=1.05, r=[mx], w=[mx])
            op("act", "activation", out=mx.ap[:, 3:4], in_=mx.ap[:, 3:4], func=AF.Exp, scale=0.5, r=[mx], w=[mx])
            op("dve", "tensor_tensor", mxd.ap, mx.ap[:, 3:4].to_broadcast([2, 2]), ident.ap[0:2, 0:2], ALU.mult, r=[mx, ident], w=[mxd])
            ps = kb.psum()
            mm(ps, [(ones_f.ap[0:2, :], mxd.ap)], r=[ones_f, mxd], out=ps.ap[:, 0:2])
            op("dve", "tensor_scalar", shift.ap, ps.ap[:, 0:2], -1.0, None, ALU.mult, r=[ps], w=[shift])

            tasks = []
            for g in range(-1, 8):
                if g < 0:
                    q0, nq_tot, nblk = 0, 16, 1
                else:
                    q0, nq_tot, nblk = 16 + 512 * g, 512, 4
                for c in range(2):
                    kbl = [(0, 0, 16, 0, g < 0)]
                    if g >= 0:
                        for j in range(4 * g):
                            kbl.append((1 + j, 16 + 128 * j, 128, 0, False))
                        for i in range(4):
                            kbl.append((1 + 4 * g + i, 16 + 128 * (4 * g + i), 128, i, True))
                    for bi_, e in enumerate(kbl):
                        tasks.append(dict(g=g, c=c, q0=q0, nq_tot=nq_tot, nblk=nblk, kb=e, first=(bi_ == 0), last=(bi_ == len(kbl) - 1)))
            LA = 2
            cur = {}
            deferred = []

            def stage1(t):
                kt_i, k0, nk, qb0, masked = t["kb"]
                c = t["c"]
                qzc = qz[s][c]
                qs = t["q0"] + qb0 * 128
                ncol = t["nq_tot"] - qb0 * 128
                pS = kb.psum_from("pS", [4, 5, 6])
                mm(pS, [(kt.ap[:, k0:k0 + nk], qzc.ap[:, qs:qs + ncol])], r=[kt, qzc], out=pS.ap[0:nk, 0:ncol])
                pt = PT[cur.setdefault("pti", 0) % len(PT)]
                cur["pti"] += 1
                op("act", "activation", out=pt.ap[0:nk, 0:ncol], in_=pS.ap[0:nk, 0:ncol], func=AF.Exp, bias=shift.ap[0:nk, c:c + 1],
                   r=[pS, shift], w=[pt])
                if masked:
                    m = min(128, ncol)
                    op("pool", "tensor_tensor", pt.ap[0:nk, 0:m], pt.ap[0:nk, 0:m], tri.ap[0:nk, 0:m], ALU.mult, r=[pt, tri], w=[pt])
                t["pt"] = pt

            def stage2(t, idx):
                kt_i, k0, nk, qb0, masked = t["kb"]
                g, c, nq_tot, q0 = t["g"], t["c"], t["nq_tot"], t["q0"]
                pt = t["pt"]
                ncol = nq_tot - qb0 * 128
                qoff = qb0 * 128
                if t["first"]:
                    cur["pOT"] = kb.psum_from("pOT", [0, 1])
                    cur["pL"] = kb.psum_from("pL", [2, 3])
                    if c == 0:
                        cur["gp"] = cur.setdefault("gcount", 0) % 2
                        cur["gcount"] += 1
                pOT, pL = cur["pOT"], cur["pL"]
                op("pe", "matmul", pOT.ap[:, qoff:qoff + ncol], vh.ap[0:nk, kt_i, 0:128], pt.ap[0:nk, 0:ncol], start=t["first"], stop=t["last"],
                   r=[pt, vh], w=[pOT], inc=False)
                op("pe", "matmul", pL.ap[:, qoff:qoff + ncol], onesq.ap[0:nk, :], pt.ap[0:nk, 0:ncol], start=t["first"], stop=t["last"],
                   r=[pt, onesq], w=[pL], inc=True)
                if not t["last"]:
                    return
                B = gbuf[cur["gp"]]
                n_ = nq_tot
                Os0, rbc = B["Os"][0], B["rbc"][c]
                sq, rstd, onb_ = B["sq"], B["rstd"], B["on"]

                def stepA(n_=n_, Os0=Os0, rbc=rbc, pOT=pOT, pL=pL, c=c, sq=sq):
                    op("dve", "reciprocal", rbc.ap[:, 0:n_], pL.ap[:, 0:n_], r=[pL], w=[rbc])
                    if c == 0:
                        op("dve", "tensor_tensor", Os0.ap[:, 0:n_], pOT.ap[:, 0:n_], rbc.ap[:, 0:n_], ALU.mult, r=[pOT, rbc], w=[Os0])
                    else:
                        op("dve", "tensor_tensor", rbc.ap[:, 0:n_], pOT.ap[:, 0:n_], rbc.ap[:, 0:n_], ALU.mult, r=[pOT, rbc], w=[rbc])
                        op("dve", "scalar_tensor_tensor", Os0.ap[:, 0:n_], rbc.ap[:, 0:n_], lam_bc.ap[:, 0:1], Os0.ap[:, 0:n_], ALU.mult, ALU.add,
                           r=[rbc, lam_bc, Os0], w=[Os0])
                        op("pool", "tensor_tensor", sq.ap[:, 0:n_], Os0.ap[:, 0:n_], Os0.ap[:, 0:n_], ALU.mult, r=[Os0], w=[sq])

                deferred.append((idx + 2, stepA))
                if c == 1:
                    def stepC(n_=n_, Os0=Os0, sq=sq, rstd=rstd, onb_=onb_, q0=q0):
                        pSS = kb.ps[7]
                        mm(pSS, [(o128.ap, sq.ap[:, 0:n_])], r=[o128, sq], out=pSS.ap[:, 0:n_])
                        op("act", "activation", out=rstd.ap[:, 0:n_], in_=pSS.ap[:, 0:n_], func=AF.Ln, bias=epsc.ap, r=[pSS, epsc], w=[rstd])
                        op("act", "activation", out=rstd.ap[:, 0:n_], in_=rstd.ap[:, 0:n_], func=AF.Exp, scale=-0.5, r=[rstd], w=[rstd])
                        op("dve", "scalar_tensor_tensor", onb_.ap[:, 0:n_], Os0.ap[:, 0:n_], slc.ap[:, 0:1], rstd.ap[:, 0:n_], ALU.mult, ALU.mult,
                           r=[Os0, slc, rstd], w=[onb_])
                        dma(oT[h * 128:(h + 1) * 128, q0:q0 + n_], onb_.ap[:, 0:n_], r=[onb_], w=[DB["oT"]])
                    deferred.append((idx + 10, stepC))
                deferred.sort(key=lambda x: x[0])

            NTK = len(tasks)
            for i in range(NTK + LA):
                if i % 20 == 0:
                    next(pcast, None)
                if i < NTK:
                    stage1(tasks[i])
                j = i - LA
                if j >= 0:
                    while deferred and deferred[0][0] <= j:
                        deferred.pop(0)[1]()
                    stage2(tasks[j], j)
            while deferred:
                deferred.pop(0)[1]()
        for _ in pcast:
            pass
        kb.reset_arena()

    if "J" in kb.phases:
        out_proj(["oT"], 8, c_w_out, "hT2", "hT3")
    if "K2" in kb.phases:
        mlp(1, "hT3", "hT4")

    if "M" in kb.phases:
        xts = [A([128, 8, 512], F32) for _ in range(2)]
        sqs = [A([128, 8, 512], BF16) for _ in range(2)]
        rs = [A([128, 512], F32) for _ in range(2)]
        xn = [A([128, 8, 512], F32) for _ in range(2)]
        ostg = [A([128, D], F32) for _ in range(4)]
        src = fm(hT[4])
        oc_ = 0
        groups = TG[1:]

        def mload(gi):
            p0, n = groups[gi]
            dma(xts[gi % 2].ap, src[:, :, p0:p0 + n], r=[DB["hT4"]], w=[xts[gi % 2]])

        def mnorm(gi):
            p0, n = groups[gi]
            xn_ = xn[gi % 2]
            norm_group(xts[gi % 2], n, 4, HnOut(xn_, lambda kc, xn_=xn_: xn_.ap[:, kc, :]), sqs[gi % 2], rs[gi % 2])

        mload(0)
        mload(1)
        mnorm(0)
        for gi, (p0, n) in enumerate(groups):
            xn_ = xn[gi % 2]
            if gi + 1 < len(groups):
                mnorm(gi + 1)
            for j in range(4):
                os_ = ostg[oc_ % 4]
                oc_ += 1
                for half in range(2):
                    ps = kb.psum()
                    for kk in range(4):
                        kc = half * 4 + kk
                        op("pe", "transpose", ps.ap[:, kk * 128:(kk + 1) * 128], xn_.ap[:, kc, j * 128:(j + 1) * 128], ident.ap,
                           r=[xn_, ident], w=[ps], inc=(kk == 3))
                    if half == 0:
                        op("act", "activation", out=os_.ap[:, 0:512], in_=ps.ap, func=AF.Copy, r=[ps], w=[os_])
                    else:
                        op("dve", "tensor_copy", os_.ap[:, 512:1024], ps.ap, r=[ps], w=[os_])
                r0 = p0 - NM + j * 128
                dma(out[r0:r0 + 128, :], os_.ap, r=[os_], w=[DB["out"]])
            if gi + 2 < len(groups):
                mload(gi + 2)
        kb.reset_arena()

    S.barrier()
    S.emit()
    return kb


def _consts():
    ident = np.eye(128, dtype=np.float32)
    tri = np.triu(np.ones((128, 128), dtype=np.float32))
    pos = np.arange(T, dtype=np.float32)
    inv_freq = np.power(np.float32(500000.0), -np.arange(0, 16, 2, dtype=np.float32) / np.float32(16)).astype(np.float32)
    ang = (pos[:, None] * inv_freq[None, :]).astype(np.float32)
    cos = np.cos(ang).astype(np.float32)
    sin = np.sin(ang).astype(np.float32)

    def tile(a):
        o = np.zeros((128, 33, 8), dtype=np.float32)
        o[0:16, 0] = a[0:16]
        o[:, 1:] = a[16:].reshape(32, 128, 8).transpose(1, 0, 2)
        return o
    return {"c_ident": ident, "c_tri": tri, "c_cos": tile(cos), "c_sin": tile(sin)}


def _core_inputs(inputs, b):
    f = lambda a: np.ascontiguousarray(np.asarray(a, dtype=np.float32))
    m = {
        "x": f(inputs["x"][b]),
        "meta_tokens": f(inputs["meta_tokens"]),
        "norm_mix": f(inputs["norm_mix"]),
        "norm_mlp": f(inputs["norm_mlp"]),
        "norm_final": f(inputs["norm_final"]).reshape(1, D),
        "ab_w_in": f(inputs["ab_w_in"][0]),
        "ab_if_bias": f(inputs["ab_if_bias"][0]).reshape(8, 1),
        "mlstm_norm": f(inputs["mlstm_norm"][0]).reshape(1, D),
        "lru_conv_w": f(inputs["lru_conv_w"][0]),
        "lru_conv_b": f(inputs["lru_conv_b"][0]).reshape(1, D),
        "lru_w_r": f(inputs["lru_w_r"][0]),
        "lru_b_r": f(inputs["lru_b_r"][0]).reshape(1, D),
        "lru_w_i": f(inputs["lru_w_i"][0]),
        "lru_b_i": f(inputs["lru_b_i"][0]).reshape(1, D),
        "lru_lambda": f(inputs["lru_lambda"][0]).reshape(1, D),
        "ab_w_out": f(inputs["ab_w_out"][0]),
        "c_w_in": f(inputs["c_w_in"][0]),
        "c_lambda": f(inputs["c_lambda"][0]).reshape(1, 256),
        "c_subln": f(inputs["c_subln"][0]).reshape(1, 128),
        "c_w_out": f(inputs["c_w_out"][0]),
        "mlp_w1": f(inputs["mlp_w1"]),
        "mlp_w2": f(inputs["mlp_w2"]),
    }
    m.update(_consts())
    return m


def kernel(**inputs):
    kb = build()
    maps = []
    for b in range(8):
        m = _core_inputs(inputs, b)
        maps.append({k: m[k] for k in kb.in_names})
    res = run_bass_kernel_spmd(kb.nc, maps, core_ids=list(range(8)))
    return np.stack([np.asarray(r["out"], dtype=np.float32) for r in res.results], axis=0)
```

```python
import math
import contextlib
import numpy as np
import concourse.bass as bass
import concourse.mybir as mybir
from concourse.bass_utils import run_bass_kernel_spmd

F32 = mybir.dt.float32
BF16 = mybir.dt.bfloat16
AF = mybir.ActivationFunctionType
ALU = mybir.AluOpType
AX = mybir.AxisListType

T = 4112
NM = 16
D = 1024
EPS = 1e-6
TG = [(0, 16)] + [(16 + 512 * i, 512) for i in range(8)]
TT = [(0, 16)] + [(16 + 128 * i, 128) for i in range(32)]
TG2 = [(0, 16)] + [(16 + 256 * i, 256) for i in range(16)]
LAMBDA_INIT = 0.8 - 0.6 * math.exp(-0.3 * 1)
SEM_LIMIT = 24000
DUMMY_MM = 1
ARENA_BYTES = 192 * 1024

ENGS = ("pe", "act", "dve", "pool", "sp")


class Buf:
    __slots__ = ("w", "r")

    def __init__(self):
        self.w = None
        self.r = {}


class Sched:
    def __init__(self, nc, n_dma_sems=24):
        self.nc = nc
        self.ops = {e: [] for e in ENGS}
        self.cnt = {e: 0 for e in ENGS if e != "sp"}
        self.epoch = {e: 0 for e in ENGS if e != "sp"}
        self.seen = {e: {} for e in ENGS}
        self.n_dma = n_dma_sems
        self.dma_val = [0] * n_dma_sems
        self.dma_epoch = [0] * n_dma_sems
        self.dma_rr = 0
        self.keys = set()
        self.last = {}

    def _need(self, eng, tok, waits):
        if tok is None:
            return
        k, v = tok
        if k[0] == "pe" and eng == "pe":
            return
        if self.seen[eng].get(k, 0) >= v:
            return
        if waits.get(k, 0) < v:
            waits[k] = v

    def _deps(self, eng, reads, writes):
        waits = {}
        for b in reads:
            self._need(eng, b.w, waits)
        for b in writes:
            self._need(eng, b.w, waits)
            for t in b.r.items():
                self._need(eng, t, waits)
        for k, v in waits.items():
            self.seen[eng][k] = v
        return list(waits.items())

    def _mark(self, tok, reads, writes):
        for b in reads:
            if b.r.get(tok[0], 0) < tok[1]:
                b.r[tok[0]] = tok[1]
        for b in writes:
            b.w = tok
            b.r = {}
        self.last[tok[0]] = max(self.last.get(tok[0], 0), tok[1])

    def _next_tok(self, eng, advance):
        c, ep = self.cnt[eng], self.epoch[eng]
        if c >= SEM_LIMIT:
            c, ep = 0, ep + 1
        if advance:
            self.cnt[eng], self.epoch[eng] = c + 1, ep
        key = (eng, ep)
        self.keys.add(key)
        return (key, c + 1)

    def op(self, eng, meth, *args, reads=(), writes=(), inc=True, **kw):
        waits = self._deps(eng, reads, writes)
        tok = self._next_tok(eng, inc)
        self.ops[eng].append((waits, (meth, args, kw), tok[0] if inc else None, 1))
        self._mark(tok, reads, writes)
        return tok

    def dma(self, out, in_, reads=(), writes=(), q="sp", **kw):
        kw = dict(kw)
        kw["out"] = out
        kw["in_"] = in_
        i = self.dma_rr
        self.dma_rr = (i + 1) % self.n_dma
        waits = dict(self._deps(q, reads, writes))
        key = ("dma", i, self.dma_epoch[i])
        prev = self.dma_val[i]
        if prev and self.seen[q].get(key, 0) < prev:
            waits[key] = max(waits.get(key, 0), prev)
            self.seen[q][key] = prev
        if prev + 16 > SEM_LIMIT:
            self.dma_epoch[i] += 1
            self.dma_val[i] = 0
            key = ("dma", i, self.dma_epoch[i])
        self.dma_val[i] += 16
        self.keys.add(key)
        tok = (key, self.dma_val[i])
        self.ops[q].append((list(waits.items()), ("dma_start", (), kw), key, 16))
        self._mark(tok, reads, writes)
        return tok

    def barrier(self):
        for e in ENGS:
            waits = {}
            for k, v in self.last.items():
                self._need(e, (k, v), waits)
            for k, v in waits.items():
                self.seen[e][k] = v
            if waits:
                self.ops[e].append((list(waits.items()), None, None, 0))

    def emit(self):
        nc = self.nc
        with contextlib.ExitStack() as st:
            sems = {}
            for k in sorted(self.keys, key=str):
                sems[k] = st.enter_context(nc.semaphore("s_" + "_".join(str(x) for x in k)))
            block = st.enter_context(nc.Block())

            def run(engname):
                def body(e):
                    for waits, fn, post, amt in self.ops[engname]:
                        for k, v in waits:
                            e.wait_ge(sems[k], v)
                        if fn is None:
                            continue
                        ins = getattr(e, fn[0])(*fn[1], **fn[2])
                        if post is not None:
                            ins.then_inc(sems[post], amt)
                return body

            block.tensor(run("pe"))
            block.scalar(run("act"))
            block.vector(run("dve"))
            block.gpsimd(run("pool"))
            block.sync(run("sp"))


class Tl:
    __slots__ = ("ap", "b")

    def __init__(self, ap, b=None):
        self.ap = ap
        self.b = b if b is not None else Buf()

    def __getitem__(self, k):
        return self.ap[k]


class KB:
    def __init__(self, phases, ext_out=()):
        self.phases = set(phases)
        self.ext_out = set(ext_out)
        nc = self.nc = bass.Bass("TRN2", target_bir_lowering=False)
        self.S = Sched(nc)
        self.dram = {}
        self.meta = {}
        self.dbuf = {}
        self.in_names = []
        self.out_names = []

    def dt(self, name, shape, dtype, producer=None):
        if name in self.dram:
            return self.dram[name]
        if producer is None or producer not in self.phases:
            kind = "ExternalInput"
            self.in_names.append(name)
        elif name in self.ext_out:
            kind = "ExternalOutput"
            self.out_names.append(name)
        else:
            kind = "Internal"
        t = self.nc.dram_tensor(name, list(shape), dtype, kind=kind).ap()
        self.meta[name] = (list(shape), dtype)
        self.dram[name] = t
        self.dbuf[name] = Buf()
        return t

    def setup_sbuf(self):
        nc = self.nc
        self.persist = nc.alloc_sbuf_tensor("persist", [128, 3072], F32)
        self.poff = 0
        self.arena = nc.alloc_sbuf_tensor("arena", [128, ARENA_BYTES // 4], F32)
        self.aoff = 0
        self.ps = [Tl(nc.alloc_psum_tensor("ps%d" % i, [128, 512], F32)[:]) for i in range(8)]
        self.psi = 0
        self.psc = {}

    def _carve(self, base, off, shape, dtype):
        n = int(np.prod(shape[1:]))
        nb = n * (4 if dtype == F32 else 2)
        nb = (nb + 63) // 64 * 64
        if dtype == F32:
            v = base[0:shape[0], off // 4: off // 4 + n]
        else:
            v = base[0:shape[0], off // 4: off // 4 + (n + 1) // 2].bitcast(BF16)[:, 0:n]
        if len(shape) == 3:
            v = v.rearrange("p (a b) -> p a b", a=shape[1])
        elif len(shape) == 4:
            v = v.rearrange("p (a b c) -> p a b c", a=shape[1], b=shape[2])
        return v, nb

    def P(self, shape, dtype=F32):
        v, nb = self._carve(self.persist, self.poff, shape, dtype)
        self.poff += nb
        assert self.poff <= 3072 * 4, self.poff
        return Tl(v)

    def A(self, shape, dtype=F32):
        v, nb = self._carve(self.arena, self.aoff, shape, dtype)
        self.aoff += nb
        assert self.aoff <= ARENA_BYTES, (self.aoff, shape)
        return Tl(v)

    def reset_arena(self):
        self.S.barrier()
        self.aoff = 0

    def psum(self):
        p = self.ps[self.psi]
        self.psi = (self.psi + 1) % 8
        return p

    def psum_from(self, key, banks):
        c = self.psc.get(key, 0)
        self.psc[key] = c + 1
        return self.ps[banks[c % len(banks)]]

    def cast(self, i, out, in_, r, w):
        if i % 2 == 0:
            self.op("act", "activation", out=out, in_=in_, func=AF.Copy, r=r, w=w)
        else:
            self.op("pool", "tensor_copy", out, in_, r=r, w=w)

    def op(self, eng, meth, *args, r=(), w=(), inc=True, **kw):
        return self.S.op(eng, meth, *args, reads=[x.b if isinstance(x, Tl) else x for x in r],
                         writes=[x.b if isinstance(x, Tl) else x for x in w], inc=inc, **kw)

    def dma(self, out, in_, r=(), w=(), **kw):
        return self.S.dma(out, in_, reads=[x.b if isinstance(x, Tl) else x for x in r],
                          writes=[x.b if isinstance(x, Tl) else x for x in w], **kw)

    def mm(self, ps, pairs, r, n_out=None, out=None):
        o = out if out is not None else ps.ap
        for i, (l, rh) in enumerate(pairs):
            last = i == len(pairs) - 1
            self.op("pe", "matmul", o, l, rh, start=(i == 0), stop=last, r=r, w=[ps], inc=last)


def build(phases=None, ext_out=("out",)):
    ALL = ["A", "C", "D", "E", "F", "G", "H", "I", "J0", "J", "K2", "M"]
    if phases is None:
        phases = ALL
    kb = KB(phases, ext_out)
    nc, S = kb.nc, kb.S
    dt = kb.dt
    op, dma, mm = kb.op, kb.dma, kb.mm

    x = dt("x", [4096, D], F32)
    meta = dt("meta_tokens", [NM, D], F32)
    norm_mix = dt("norm_mix", [2, D], F32)
    norm_mlp = dt("norm_mlp", [2, D], F32)
    norm_final = dt("norm_final", [1, D], F32)
    ab_w_in = dt("ab_w_in", [D, 6152], F32)
    ab_if_bias = dt("ab_if_bias", [8, 1], F32)
    mlstm_norm = dt("mlstm_norm", [1, D], F32)
    lru_conv_w = dt("lru_conv_w", [4, D], F32)
    lru_conv_b = dt("lru_conv_b", [1, D], F32)
    lru_w_r = dt("lru_w_r", [8, 128, 128], F32)
    lru_b_r = dt("lru_b_r", [1, D], F32)
    lru_w_i = dt("lru_w_i", [8, 128, 128], F32)
    lru_b_i = dt("lru_b_i", [1, D], F32)
    lru_lambda = dt("lru_lambda", [1, D], F32)
    ab_w_out = dt("ab_w_out", [2048, D], F32)
    c_w_in = dt("c_w_in", [D, 3072], F32)
    c_lambda = dt("c_lambda", [1, 256], F32)
    c_subln = dt("c_subln", [1, 128], F32)
    c_w_out = dt("c_w_out", [D, D], F32)
    mlp_w1 = dt("mlp_w1", [2, D, 4096], F32)
    mlp_w2 = dt("mlp_w2", [2, 4096, D], F32)
    c_ident = dt("c_ident", [128, 128], F32)
    c_tri = dt("c_tri", [128, 128], F32)
    c_cos = dt("c_cos", [128, 33, 8], F32)
    c_sin = dt("c_sin", [128, 33, 8], F32)

    hT = [dt("hT%d" % i, [D, T], F32, p) for i, p in enumerate(["A", "G", "H", "J", "K2"])]
    qT = dt("qT", [D, T], BF16, "C")
    kT = dt("kT", [D, T], BF16, "C")
    ktok = dt("ktok", [T, D], BF16, "C")
    vtok = dt("vtok", [T, D], BF16, "C")
    otok = dt("otok", [T, D], BF16, "C")
    giT = dt("giT", [4, T], F32, "C")
    gfT = dt("gfT", [4, T], F32, "C")
    xbT = dt("xbT", [D, T], F32, "C")
    ggT = dt("ggT", [D, T], F32, "C")
    gprep = dt("gprep", [128, 33 * 8 + 33 * 4], F32, "D")
    yTa = dt("yTa", [D, T], BF16, "E")
    yTb = dt("yTb", [D, T], BF16, "F")
    qkT = dt("qkT", [2048, T], BF16, "I")
    vtok2 = dt("vtok2", [T, D], BF16, "I")
    oT = dt("oT", [D, T], BF16, "J0")
    w1s = [dt("w1s%d" % l, [D, 4096], BF16, p) for l, p in enumerate(["C", "J0"])]
    w2s = [dt("w2s%d" % l, [4096, D], BF16, p) for l, p in enumerate(["C", "J0"])]
    out = dt("out", [4096, D], F32, "M")
    DB = kb.dbuf

    kb.setup_sbuf()
    P, A = kb.P, kb.A

    def gen_precast(layer):
        stg = [A([128, 2048], F32) for _ in range(2)]
        stb = [A([128, 2048], BF16) for _ in range(2)]
        w1v = mlp_w1[layer].rearrange("(c p) n -> p c n", p=128)
        w2v = mlp_w2[layer].rearrange("(c p) n -> p c n", p=128)
        d1 = w1s[layer].rearrange("(c p) n -> p c n", p=128)
        d2 = w2s[layer].rearrange("(c p) n -> p c n", p=128)
        jobs = []
        for kc in range(8):
            for hf in range(2):
                jobs.append((w1v[:, kc, hf * 2048:(hf + 1) * 2048], d1[:, kc, hf * 2048:(hf + 1) * 2048], None, "w1s%d" % layer))
        for c2 in range(16):
            jobs.append((w2v[:, 2 * c2:2 * c2 + 2, :], d2[:, 2 * c2:2 * c2 + 2, :], 2, "w2s%d" % layer))
        prev = None
        for k, (src, dst, a3, dname) in enumerate(jobs):
            sg, sb_ = stg[k % 2], stb[k % 2]
            sv = sg.ap if a3 is None else sg.ap.rearrange("p (a b) -> p a b", a=a3)
            bv = sb_.ap if a3 is None else sb_.ap.rearrange("p (a b) -> p a b", a=a3)
            dma(sv, src, w=[sg])
            yield
            op("pool", "tensor_copy", sb_.ap, sg.ap, r=[sg], w=[sb_])
            yield
            if prev is not None:
                dma(prev[0], prev[1], r=[prev[2]], w=[DB[prev[3]]])
            prev = (dst, bv, sb_, dname)
            yield
        dma(prev[0], prev[1], r=[prev[2]], w=[DB[prev[3]]])
        yield

    def fm(t):
        return t.rearrange("(c p) t -> p c t", p=128)

    ident = P([128, 128], F32)
    identb = P([128, 128], BF16)
    tri = P([128, 128], F32)
    ones_b = P([128, 128], BF16)
    ones_f = P([128, 128], F32)
    epsc = P([128, 1], F32)
    gam = P([128, 8, 8], F32)
    lrup = P([128, 8, 8], F32)
    clam = P([128, 8], F32)
    shift_all = P([128, 16], F32)
    dma(ident.ap, c_ident, w=[ident])
    dma(tri.ap, c_tri, w=[tri])
    op("dve", "tensor_copy", identb.ap, ident.ap, r=[ident], w=[identb])
    op("pool", "memset", ones_b.ap, 1.0 / 1024, w=[ones_b])
    op("pool", "memset", ones_f.ap, 1.0, w=[ones_f])
    op("pool", "memset", epsc.ap, EPS, w=[epsc])

    def load_colvecs(dst, rows):
        st = A([16, 1024], F32)
        for i, rr in enumerate(rows):
            dma(st.ap[i:i + 1, :], rr, w=[st])
        for kc in range(8):
            ps = kb.psum()
            op("pe", "transpose", ps.ap[:, 0:len(rows)], st.ap[0:len(rows), kc * 128:(kc + 1) * 128],
               ident.ap[0:len(rows), 0:len(rows)], r=[st, ident], w=[ps])
            op("dve", "tensor_copy", dst.ap[:, kc, 0:len(rows)], ps.ap[:, 0:len(rows)], r=[ps], w=[dst])

    load_colvecs(gam, [norm_mix[0:1, :], norm_mix[1:2, :], norm_mlp[0:1, :], norm_mlp[1:2, :], norm_final])
    load_colvecs(lrup, [lru_conv_w[j:j + 1, :] for j in range(4)] + [lru_conv_b, lru_b_r, lru_b_i, lru_lambda])
    kb.reset_arena()

    def norm_group(xt, n, gidx, hn_out, sq, rstd, out_dtype_f32=False):
        op("act", "activation", out=sq.ap[:, :, 0:n], in_=xt.ap[:, :, 0:n], func=AF.Square, r=[xt], w=[sq])
        ps = kb.psum()
        mm(ps, [(ones_b.ap, sq.ap[:, kc, 0:n]) for kc in range(8)], r=[ones_b, sq], out=ps.ap[:, 0:n])
        op("act", "activation", out=rstd.ap[:, 0:n], in_=ps.ap[:, 0:n], func=AF.Sqrt, bias=epsc.ap, r=[ps, epsc], w=[rstd])
        op("dve", "reciprocal", rstd.ap[:, 0:n], rstd.ap[:, 0:n], r=[rstd], w=[rstd])
        for kc in range(8):
            op("dve", "scalar_tensor_tensor", hn_out(kc), xt.ap[:, kc, 0:n], gam.ap[:, kc, gidx:gidx + 1],
               rstd.ap[:, 0:n], ALU.mult, ALU.mult, r=[xt, gam, rstd], w=[hn_out.tl])

    class HnOut:
        def __init__(self, tl, fn):
            self.tl, self.fn = tl, fn

        def __call__(self, kc):
            return self.fn(kc)

    if "A" in kb.phases:
        xin = [A([128, 4, D], F32) for _ in range(2)]
        stg = [A([128, 8, 512], F32) for _ in range(2)]
        for gi, (p0, n) in enumerate(TG):
            xi, sg = xin[gi % 2], stg[gi % 2]
            nt = max(1, n // 128)
            if gi == 0:
                dma(xi.ap[0:16, 0, :], meta, w=[xi])
            else:
                r0 = p0 - NM
                dma(xi.ap, x[r0:r0 + 512, :].rearrange("(j p) d -> p j d", p=128), w=[xi])
            for kc in range(8):
                ps = kb.psum()
                for j in range(nt):
                    m = min(n, 128)
                    op("pe", "transpose", ps.ap[:, j * 128:j * 128 + m], xi.ap[0:m, j, kc * 128:(kc + 1) * 128],
                       ident.ap[0:m, 0:m], r=[xi, ident], w=[ps], inc=(j == nt - 1))
                if kc % 2 == 0:
                    op("act", "activation", out=sg.ap[:, kc, 0:n], in_=ps.ap[:, 0:n], func=AF.Copy, r=[ps], w=[sg])
                else:
                    op("dve", "tensor_copy", sg.ap[:, kc, 0:n], ps.ap[:, 0:n], r=[ps], w=[sg])
            dma(fm(hT[0])[:, :, p0:p0 + n], sg.ap[:, :, 0:n], r=[sg], w=[DB["hT0"]])
        kb.reset_arena()

    def norm_full(src_name, gidx):
        hn = A([128, 8, T], BF16)
        hnb = [Buf() for _ in TG]
        mark = kb.aoff
        xts = [A([128, 8, 512], F32) for _ in range(2)]
        sqs = [A([128, 8, 512], BF16) for _ in range(2)]
        rs = [A([128, 512], F32) for _ in range(2)]
        src = fm(kb.dram[src_name])
        for gi, (p0, n) in enumerate(TG):
            xt = xts[gi % 2]
            dma(xt.ap[:, :, 0:n], src[:, :, p0:p0 + n], r=[DB[src_name]], w=[xt])
            ho = HnOut(Tl(hn.ap, hnb[gi]), lambda kc, p0=p0, n=n: hn.ap[:, kc, p0:p0 + n])
            norm_group(xt, n, gidx, ho, sqs[gi % 2], rs[gi % 2])
        S.barrier()
        kb.aoff = mark
        return hn, hnb

    if "C" in kb.phases:
        hn, hnb = norm_full("hT0", 0)
        wst = [A([128, 8, 512], F32) for _ in range(2)]
        wbf = [A([128, 8, 512], BF16) for _ in range(2)]
        ostg = [A([128, 4, 512], F32) for _ in range(2)]
        ostg_b = [Tl(o.ap.rearrange("p a b -> p (a b)").bitcast(BF16)[:, 0:2048].rearrange("p (a b) -> p a b", a=4), o.b) for o in ostg]
        tstg = [A([128, 512], BF16) for _ in range(3)]
        win = ab_w_in.rearrange("(c p) n -> p c n", p=128)
        pcast = gen_precast(0)
        jobs = []
        for name, c0 in (("q", 0), ("k", 1024), ("v", 2048), ("o", 3072), ("xb", 4104), ("gate", 5128)):
            for hb in range(2):
                jobs.append((name, c0 + 512 * hb, 512, hb))
        jobs.append(("gates", 4096, 8, 0))
        cnt = 0
        tcnt = 0
        for ji, (name, c0, ncol, hb) in enumerate(jobs):
            ws, wb = wst[ji % 2], wbf[ji % 2]
            dma(ws.ap[:, :, 0:ncol], win[:, :, c0:c0 + ncol], w=[ws])
            kb.cast(ji, wb.ap[:, :, 0:ncol], ws.ap[:, :, 0:ncol], [ws], [wb])
            if name == "gates":
                gstgs = [A([4, 2, 512], F32) for _ in range(2)]
                for gi, (p0, n) in enumerate(TG):
                    gstg = gstgs[gi % 2]
                    for half in range(2):
                        ps = kb.psum()
                        mm(ps, [(wb.ap[:, kc, 4 * half:4 * half + 4], hn.ap[:, kc, p0:p0 + n]) for kc in range(8)],
                           r=[wb, hnb[gi]], out=ps.ap[0:4, 0:n])
                        op("dve", "tensor_copy", gstg.ap[:, half, 0:n], ps.ap[0:4, 0:n], r=[ps], w=[gstg])
                    dma(giT[:, p0:p0 + n], gstg.ap[:, 0, 0:n], r=[gstg], w=[DB["giT"]])
                    dma(gfT[:, p0:p0 + n], gstg.ap[:, 1, 0:n], r=[gstg], w=[DB["gfT"]])
                continue
            if name in ("q", "k", "xb", "gate"):
                dst_name = {"q": "qT", "k": "kT", "xb": "xbT", "gate": "ggT"}[name]
                isb = name in ("q", "k")
                for gi, (p0, n) in enumerate(TG):
                    next(pcast, None)
                    og = (ostg_b if isb else ostg)[cnt % 2]
                    cnt += 1
                    for oc in range(4):
                        ps = kb.psum()
                        mm(ps, [(wb.ap[:, kc, oc * 128:(oc + 1) * 128], hn.ap[:, kc, p0:p0 + n]) for kc in range(8)],
                           r=[wb, hnb[gi]], out=ps.ap[:, 0:n])
                        if name == "gate":
                            op("act", "activation", out=og.ap[:, oc, 0:n], in_=ps.ap[:, 0:n], func=AF.Gelu_apprx_tanh, r=[ps], w=[og])
                        elif oc % 2 == 0:
                            op("act", "activation", out=og.ap[:, oc, 0:n], in_=ps.ap[:, 0:n], func=AF.Copy, r=[ps], w=[og])
                        else:
                            op("dve", "tensor_copy", og.ap[:, oc, 0:n], ps.ap[:, 0:n], r=[ps], w=[og])
                    dma(fm(kb.dram[dst_name])[:, 4 * hb:4 * hb + 4, p0:p0 + n], og.ap[:, :, 0:n], r=[og], w=[DB[dst_name]])
            if name in ("k", "v", "o"):
                dst_name = {"k": "ktok", "v": "vtok", "o": "otok"}[name]
                for ti, (p0, n) in enumerate(TT):
                    if ti % 4 == 0:
                        next(pcast, None)
                    tg_i = 0 if ti == 0 else 1 + (ti - 1) // 4
                    ts_ = tstg[tcnt % 3]
                    tcnt += 1
                    ps = kb.psum()
                    mm(ps, [(hn.ap[:, kc, p0:p0 + n], wb.ap[:, kc, 0:512]) for kc in range(8)],
                       r=[wb, hnb[tg_i]], out=ps.ap[0:n, :])
                    if name == "o":
                        op("act", "activation", out=ts_.ap[0:n, :], in_=ps.ap[0:n, :], func=AF.Sigmoid, r=[ps], w=[ts_])
                    elif ti % 2 == 0:
                        op("act", "activation", out=ts_.ap[0:n, :], in_=ps.ap[0:n, :], func=AF.Copy, r=[ps], w=[ts_])
                    else:
                        op("dve", "tensor_copy", ts_.ap[0:n, :], ps.ap[0:n, :], r=[ps], w=[ts_])
                    dma(kb.dram[dst_name][p0:p0 + n, 512 * hb:512 * hb + 512], ts_.ap[0:n, :], r=[ts_], w=[DB[dst_name]])
        for _ in pcast:
            pass
        kb.reset_arena()

    NB = 33
    if "D" in kb.phases:
        gi_ = A([4, T], F32); gf_ = A([4, T], F32)
        t2 = A([4, T], F32); t3 = A([4, T], F32); Bc = A([4, T], F32)
        onesr = A([4, 1], F32)
        bi = A([4, 1], F32); bfb = A([4, 1], F32)
        Rb = A([4, NB + 1], F32)
        dec = A([4, NB], F32); decd = A([4, 4, NB], F32)
        gout = A([128, NB * 8 + NB * 4], F32)
        dma(gi_.ap, giT, r=[DB["giT"]], w=[gi_])
        dma(gf_.ap, gfT, r=[DB["gfT"]], w=[gf_])
        dma(bi.ap, ab_if_bias[0:4, :], w=[bi])
        dma(bfb.ap, ab_if_bias[4:8, :], w=[bfb])
        op("pool", "memset", onesr.ap, 1.0, w=[onesr])
        op("pool", "memset", Rb.ap, 0.0, w=[Rb])
        op("pool", "memset", gout.ap, 0.0, w=[gout])
        op("dve", "tensor_scalar", gf_.ap, gf_.ap, bfb.ap, None, ALU.add, r=[gf_, bfb], w=[gf_])
        op("act", "activation", out=t2.ap, in_=gf_.ap, func=AF.Abs, r=[gf_], w=[t2])
        op("act", "activation", out=t2.ap, in_=t2.ap, func=AF.Exp, scale=-1.0, r=[t2], w=[t2])
        op("act", "activation", out=t2.ap, in_=t2.ap, func=AF.Ln, bias=1.0, r=[t2], w=[t2])
        op("dve", "tensor_scalar", t3.ap, gf_.ap, 0.0, None, ALU.min, r=[gf_], w=[t3])
        op("dve", "tensor_tensor", t3.ap, t3.ap, t2.ap, ALU.subtract, r=[t3, t2], w=[t3])
        op("dve", "tensor_tensor_scan", Bc.ap, onesr.ap.to_broadcast([4, T]), t3.ap, 0.0, ALU.mult, ALU.add, r=[onesr, t3], w=[Bc])
        G = gi_; Mx = t2; Rt = t3; beta = gf_; flo = Bc
        op("dve", "scalar_tensor_tensor", G.ap, gi_.ap, bi.ap, Bc.ap, ALU.add, ALU.subtract, r=[gi_, bi, Bc], w=[G])
        op("dve", "tensor_tensor_scan", Mx.ap, G.ap, G.ap, 0.0, ALU.max, ALU.max, r=[G], w=[Mx])
        op("dve", "tensor_copy", Rb.ap[:, 1:2], Mx.ap[:, 15:16], r=[Mx], w=[Rb])
        op("dve", "tensor_copy", Rb.ap[:, 2:NB + 1], Mx.ap[:, 16:T].rearrange("p (b s) -> p b s", s=128)[:, :, 127], r=[Mx], w=[Rb])
        op("dve", "tensor_copy", Rt.ap[:, 0:16], Rb.ap[:, 1:2].to_broadcast([4, 16]), r=[Rb], w=[Rt])
        op("dve", "tensor_copy", Rt.ap[:, 16:T].rearrange("p (b s) -> p b s", s=128),
           Rb.ap[:, 2:NB + 1].unsqueeze(2).to_broadcast([4, 32, 128]), r=[Rb], w=[Rt])
        op("dve", "tensor_tensor", beta.ap, G.ap, Rt.ap, ALU.subtract, r=[G, Rt], w=[beta])
        op("act", "activation", out=beta.ap, in_=beta.ap, func=AF.Exp, r=[beta], w=[beta])
        op("dve", "tensor_scalar", beta.ap, beta.ap, 1.0 / 16, None, ALU.mult, r=[beta], w=[beta])
        op("dve", "tensor_tensor", flo.ap, Bc.ap, Rt.ap, ALU.add, r=[Bc, Rt], w=[flo])
        op("act", "activation", out=flo.ap, in_=flo.ap, func=AF.Exp, scale=-1.0, r=[flo], w=[flo])
        op("dve", "tensor_tensor", dec.ap, Rb.ap[:, 0:NB], Rb.ap[:, 1:NB + 1], ALU.subtract, r=[Rb], w=[dec])
        op("act", "activation", out=dec.ap, in_=dec.ap, func=AF.Exp, r=[dec], w=[dec])
        op("dve", "tensor_tensor", decd.ap, dec.ap.unsqueeze(1).to_broadcast([4, 4, NB]),
           ident.ap[0:4, 0:4].unsqueeze(2).to_broadcast([4, 4, NB]), ALU.mult, r=[dec, ident], w=[decd])
        ps = kb.psum()
        for ti, (p0, n) in enumerate(TT):
            op("pe", "transpose", ps.ap[0:n, ti * 8:ti * 8 + 4], beta.ap[:, p0:p0 + n], ident.ap[0:4, 0:4], r=[beta, ident], w=[ps], inc=False)
            op("pe", "transpose", ps.ap[0:n, ti * 8 + 4:ti * 8 + 8], flo.ap[:, p0:p0 + n], ident.ap[0:4, 0:4], r=[flo, ident], w=[ps], inc=(ti == NB - 1))
        op("dve", "tensor_copy", gout.ap[:, 8:NB * 8], ps.ap[:, 8:NB * 8], r=[ps], w=[gout])
        op("dve", "tensor_copy", gout.ap[0:16, 0:8], ps.ap[0:16, 0:8], r=[ps], w=[gout])
        ps2 = kb.psum()
        mm(ps2, [(ones_f.ap[0:4, :], decd.ap.rearrange("p a b -> p (a b)"))], r=[ones_f, decd], out=ps2.ap[:, 0:4 * NB])
        op("dve", "tensor_copy", gout.ap[:, NB * 8:NB * 12], ps2.ap[:, 0:4 * NB], r=[ps2], w=[gout])
        dma(gprep, gout.ap, r=[gout], w=[DB["gprep"]])
        kb.reset_arena()

    genE = genF = None
    if "E" in kb.phases:
        gp = A([128, NB * 12], F32)
        dma(gp.ap, gprep, r=[DB["gprep"]], w=[gp])
        bfv = gp.ap[:, 0:NB * 8].rearrange("p (b e) -> p b e", e=8)
        decv = gp.ap[:, NB * 8:NB * 12].rearrange("p (h b) -> p h b", h=4)
        gnb = A([128, D], F32)
        dma(gnb.ap, mlstm_norm.partition_broadcast(128), w=[gnb])
        Cst = A([128, 4, 2, 257], F32)
        Cd = A([128, 4, 2, 257], BF16)
        CstB = [Buf() for _ in range(4)]
        CdB = [Buf() for _ in range(4)]
        NBUF = 2
        qTb = [A([128, 8, 128], BF16) for _ in range(NBUF)]
        kTb = [A([128, 8, 128], BF16) for _ in range(NBUF)]
        ktb = [A([128, D], BF16) for _ in range(NBUF)]
        otb = [A([128, D], BF16) for _ in range(NBUF)]
        vab = [A([128, 4, 257], BF16) for _ in range(NBUF)]
        for v_ in vab:
            op("pool", "memset", v_.ap[:, :, 256:257], 1.0, w=[v_])
        Sm = [A([128, 128], BF16) for _ in range(4)]
        ktl = [A([128, 256], BF16) for _ in range(4)]
        ha = [A([128, 256], F32) for _ in range(2)]
        junk = A([128, 256], F32)
        sm1 = [A([128, 4], F32) for _ in range(2)]
        yst = [A([128, 8, 128], BF16) for _ in range(2)]
        yrow = fm(yTa)
        it = 0

        def loads(b):
            p0, n = TT[b]
            s = b % NBUF
            dma(qTb[s].ap[:, :, 0:n], fm(qT)[:, :, p0:p0 + n], r=[DB["qT"]], w=[qTb[s]])
            dma(kTb[s].ap[:, :, 0:n], fm(kT)[:, :, p0:p0 + n], r=[DB["kT"]], w=[kTb[s]])
            dma(ktb[s].ap[0:n, :], ktok[p0:p0 + n, :], r=[DB["ktok"]], w=[ktb[s]])
            dma(otb[s].ap[0:n, :], otok[p0:p0 + n, :], r=[DB["otok"]], w=[otb[s]])
            dma(vab[s].ap[0:n, :, 0:256], vtok[p0:p0 + n, :].rearrange("t (h v) -> t h v", h=4), r=[DB["vtok"]], w=[vab[s]])

        loads(0)
        han = [A([128, 256], BF16) for _ in range(4)]
        flat = [(b, h) for b in range(NB) for h in range(4)]
        pS_slots = [kb.ps[0]] * 4

        def s1(i):
            b, h = flat[i]
            p0, n = TT[b]
            s = b % NBUF
            i2 = i % 4
            beta_c = bfv[0:n, b, h:h + 1]
            pS = pS_slots[i % 4]
            c0 = (i % 4) * 128
            mm(pS, [(kTb[s].ap[:, 2 * h + dc, 0:n], qTb[s].ap[:, 2 * h + dc, 0:n]) for dc in range(2)],
               r=[kTb[s], qTb[s]], out=pS.ap[0:n, c0:c0 + n])
            op("dve", "scalar_tensor_tensor", Sm[i2].ap[0:n, 0:n], pS.ap[0:n, c0:c0 + n], beta_c, tri.ap[0:n, 0:n],
               ALU.mult, ALU.mult, r=[pS, gp, tri], w=[Sm[i2]])
            op("act", "activation", out=ktl[i2].ap[0:n, :], in_=ktb[s].ap[0:n, h * 256:(h + 1) * 256], func=AF.Copy,
               scale=beta_c, r=[ktb[s], gp], w=[ktl[i2]])

        def s2(i):
            b, h = flat[i]
            p0, n = TT[b]
            s = b % NBUF
            i2 = i % 2
            i4 = i % 4
            cstT = Tl(Cst.ap, CstB[h]); cdT = Tl(Cd.ap, CdB[h])
            floor_c = bfv[0:n, b, 4 + h:5 + h]
            pN = kb.ps[1 + i % 2]
            pairs = []
            if b > 0:
                pairs += [(qTb[s].ap[:, 2 * h + dc, 0:n], Cd.ap[:, h, dc, :]) for dc in range(2)]
            pairs.append((Sm[i4].ap[0:n, 0:n], vab[s].ap[0:n, h, :]))
            mm(pN, pairs, r=[qTb[s], cdT, Sm[i4], vab[s]], out=pN.ap[0:n, 0:257])
            pC = [kb.ps[3 + (i % 2) * 2 + dc] for dc in range(2)]
            for dc in range(2):
                mm(pC[dc], [(ktl[i4].ap[0:n, dc * 128:(dc + 1) * 128], vab[s].ap[0:n, h, :])], r=[ktl[i4], vab[s]],
                   out=pC[dc].ap[:, 0:257])
            yield
            for dc in range(2):
                if b == 0:
                    op("dve", "tensor_copy", Cst.ap[:, h, dc, :], pC[dc].ap[:, 0:257], r=[pC[dc]], w=[cstT])
                else:
                    op("dve", "scalar_tensor_tensor", Cst.ap[:, h, dc, :], Cst.ap[:, h, dc, :], decv[:, h, b:b + 1],
                       pC[dc].ap[:, 0:257], ALU.mult, ALU.add, r=[pC[dc], gp, cstT], w=[cstT])
            if b + 1 < NB:
                op("act", "activation", out=Cd.ap[:, h, :, :], in_=Cst.ap[:, h, :, :], func=AF.Copy,
                   scale=decv[:, h, b + 1:b + 2], r=[cstT, gp], w=[cdT])
            sm = sm1[i2]
            yield
            op("act", "activation", out=sm.ap[0:n, 0:1], in_=pN.ap[0:n, 256:257], func=AF.Abs, r=[pN], w=[sm])
            yield
            op("dve", "tensor_scalar", sm.ap[0:n, 0:1], sm.ap[0:n, 0:1], floor_c, None, ALU.max, r=[sm, gp], w=[sm])
            op("dve", "reciprocal", sm.ap[0:n, 0:1], sm.ap[0:n, 0:1], r=[sm], w=[sm])
            yield
            op("dve", "scalar_tensor_tensor", ha[i2].ap[0:n, :], pN.ap[0:n, 0:256], sm.ap[0:n, 0:1],
               otb[s].ap[0:n, h * 256:(h + 1) * 256], ALU.mult, ALU.mult, r=[pN, sm, otb[s]], w=[ha[i2]])
            yield
            op("act", "activation", out=junk.ap[0:n, :], in_=ha[i2].ap[0:n, :], func=AF.Square, accum_out=sm.ap[0:n, 1:2],
               r=[ha[i2]], w=[sm])
            op("act", "activation", out=sm.ap[0:n, 2:3], in_=sm.ap[0:n, 1:2], func=AF.Sqrt, scale=1.0 / 256, bias=epsc.ap[0:n, :],
               r=[sm, epsc], w=[sm])
            yield
            op("dve", "reciprocal", sm.ap[0:n, 2:3], sm.ap[0:n, 2:3], r=[sm], w=[sm])
            hn_ = han[i % 4]
            op("dve", "scalar_tensor_tensor", hn_.ap[0:n, :], ha[i2].ap[0:n, :], sm.ap[0:n, 2:3],
               gnb.ap[0:n, h * 256:(h + 1) * 256], ALU.mult, ALU.mult, r=[ha[i2], sm, gnb], w=[hn_])

        def s4(i):
            b, h = flat[i]
            p0, n = TT[b]
            ys = yst[b % 2]
            hn_ = han[i % 4]
            pT = kb.ps[7]
            pTb = pT.ap.bitcast(BF16)
            for vc in range(2):
                op("pe", "transpose", pTb[:, vc * 128:vc * 128 + n], hn_.ap[0:n, vc * 128:(vc + 1) * 128],
                   identb.ap[0:n, 0:n], r=[hn_, identb], w=[pT], inc=(vc == 1))
            op("act", "activation", out=ys.ap[:, 2 * h:2 * h + 2, 0:n],
               in_=pTb[:, 0:256].rearrange("p (a b) -> p a b", a=2)[:, :, 0:n], func=AF.Copy, r=[pT], w=[ys])
            if h == 3:
                dma(yrow[:, 0:8, p0:p0 + n], ys.ap[:, :, 0:n], r=[ys], w=[DB["yTa"]])

        NF = len(flat)

        def genE_():
            s1(0)
            s1(1)
            for i0_ in range(0, NF, 2):
                b, h = flat[i0_]
                if h == 0 and b + 1 < NB:
                    loads(b + 1)
                for j in (i0_ + 2, i0_ + 3):
                    if j < NF:
                        s1(j)
                gs = [s2(i0_), s2(i0_ + 1)]
                while gs:
                    for g_ in list(gs):
                        try:
                            next(g_)
                        except StopIteration:
                            gs.remove(g_)
                for j in (i0_ - 2, i0_ - 1):
                    if j >= 0:
                        s4(j)
                yield
            s4(NF - 2)
            s4(NF - 1)
            yield
        genE = genE_()

    if "F" in kb.phases:
        t8 = A([128, 8], F32); t8b = A([128, 8], F32)
        lam_v = lrup.ap[:, :, 7]
        op("act", "activation", out=t8.ap, in_=lam_v, func=AF.Abs, r=[lrup], w=[t8])
        op("act", "activation", out=t8.ap, in_=t8.ap, func=AF.Exp, scale=-1.0, r=[t8], w=[t8])
        op("act", "activation", out=t8.ap, in_=t8.ap, func=AF.Ln, bias=1.0, r=[t8], w=[t8])
        op("dve", "tensor_scalar", t8b.ap, lam_v, 0.0, None, ALU.min, r=[lrup], w=[t8b])
        op("dve", "tensor_tensor", t8b.ap, t8b.ap, t8.ap, ALU.subtract, r=[t8, t8b], w=[t8b])
        op("dve", "tensor_scalar", clam.ap, t8b.ap, 8.0, None, ALU.mult, r=[t8b], w=[clam])
        wri_s = A([128, 16, 128], F32)
        wri = A([128, 16, 128], BF16)
        dma(wri_s.ap[:, 0:8, :], lru_w_r.rearrange("n d e -> d n e"), w=[wri_s])
        dma(wri_s.ap[:, 8:16, :], lru_w_i.rearrange("n d e -> d n e"), w=[wri_s])
        op("pool", "tensor_copy", wri.ap, wri_s.ap, r=[wri_s], w=[wri])
        xb = A([128, T], F32); gg = A([128, T], F32)
        xc = A([128, T], F32); xcb = A([128, T], BF16)
        r_ = A([128, T], F32); i_ = A([128, T], F32); s_ = A([128, T], F32)
        hb_ = [A([128, T], BF16) for _ in range(2)]

        clam2 = A([128, 8], F32)
        op("dve", "tensor_scalar", clam2.ap, clam.ap, 2.0, None, ALU.mult, r=[clam], w=[clam2])
        NP = len(TG)
        pb = {nm: [Buf() for _ in range(NP)] for nm in ("xc", "xcb", "r", "i", "s", "hb0", "hb1")}

        def genF_():
            dma(xb.ap, xbT[0:128, :], r=[DB["xbT"]], w=[xb])
            dma(gg.ap, ggT[0:128, :], r=[DB["ggT"]], w=[gg])

            def tl(n_, gi):
                hb = hb_[n_ % 2]
                return (Tl(xc.ap, pb["xc"][gi]), Tl(xcb.ap, pb["xcb"][gi]), Tl(r_.ap, pb["r"][gi]),
                        Tl(i_.ap, pb["i"][gi]), Tl(s_.ap, pb["s"][gi]), Tl(hb.ap, pb["hb%d" % (n_ % 2)][gi]))

            def stage1(n_, gi):
                p0, n = TG[gi]
                pv = lambda i: lrup.ap[:, n_, i:i + 1]
                sl = slice(p0, p0 + n)
                xcT, xcbT, rT, iT, sT, hbT = tl(n_, gi)
                op("dve", "tensor_scalar", xc.ap[:, sl], xb.ap[:, sl], pv(3), pv(4), ALU.mult, ALU.add, r=[xb, lrup], w=[xcT])
                for j in (1, 2, 3):
                    lo = max(p0, j)
                    op("dve", "scalar_tensor_tensor", xc.ap[:, lo:p0 + n], xb.ap[:, lo - j:p0 + n - j], pv(3 - j), xc.ap[:, lo:p0 + n],
                       ALU.mult, ALU.add, r=[xb, lrup, xcT], w=[xcT])
                op("act", "activation", out=xcb.ap[:, sl], in_=xc.ap[:, sl], func=AF.Copy, r=[xcT], w=[xcbT])
                for which, dst, dT, bidx in ((0, r_, rT, 5), (1, i_, iT, 6)):
                    ps = kb.psum_from("F", [5, 6])
                    mm(ps, [(wri.ap[:, which * 8 + n_, :], xcb.ap[:, sl])], r=[wri, xcbT], out=ps.ap[:, 0:n])
                    op("act", "activation", out=dst.ap[:, sl], in_=ps.ap[:, 0:n], func=AF.Sigmoid, bias=pv(bidx),
                       r=[ps, lrup], w=[dT])

            def stage4(n_, gi):
                p0, n = TG[gi]
                sl = slice(p0, p0 + n)
                hb = hb_[n_ % 2]
                xcT, xcbT, rT, iT, sT, hbT = tl(n_, gi)
                op("pool", "tensor_tensor", i_.ap[:, sl], i_.ap[:, sl], xc.ap[:, sl], ALU.mult, r=[iT, xcT], w=[iT])
                op("pool", "tensor_tensor", i_.ap[:, sl], i_.ap[:, sl], s_.ap[:, sl], ALU.mult, r=[iT, sT], w=[iT])
                init = 0.0 if gi == 0 else s_.ap[:, p0 - 1:p0]
                rd = [rT, iT] + ([Tl(s_.ap, pb["s"][gi - 1])] if gi > 0 else [])
                op("dve", "tensor_tensor_scan", s_.ap[:, sl], r_.ap[:, sl], i_.ap[:, sl], init, ALU.mult, ALU.add, r=rd, w=[sT])
                op("pool", "tensor_tensor", hb.ap[:, sl], s_.ap[:, sl], gg.ap[:, sl], ALU.mult, r=[sT, gg], w=[hbT])

            for gi in range(NP):
                stage1(0, gi)
            yield
            for n_ in range(8):
                if n_ + 1 < 8:
                    dma(xb.ap, xbT[(n_ + 1) * 128:(n_ + 2) * 128, :], r=[DB["xbT"]], w=[xb])
                for gi, (p0, n) in enumerate(TG):
                    sl = slice(p0, p0 + n)
                    xcT, xcbT, rT, iT, sT, hbT = tl(n_, gi)
                    op("act", "activation", out=s_.ap[:, sl], in_=r_.ap[:, sl], func=AF.Exp, scale=clam2.ap[:, n_:n_ + 1], r=[rT, clam2], w=[sT])
                    op("act", "activation", out=r_.ap[:, sl], in_=r_.ap[:, sl], func=AF.Exp, scale=clam.ap[:, n_:n_ + 1], r=[rT, clam], w=[rT])
                for gi, (p0, n) in enumerate(TG):
                    sl = slice(p0, p0 + n)
                    xcT, xcbT, rT, iT, sT, hbT = tl(n_, gi)
                    op("act", "activation", out=s_.ap[:, sl], in_=s_.ap[:, sl], func=AF.Sqrt, scale=-1.0, bias=1.0, r=[sT], w=[sT])
                yield
                for gi in range(NP):
                    stage4(n_, gi)
                    if n_ + 1 < 8 and gi >= 1:
                        stage1(n_ + 1, gi - 1)
                if n_ + 1 < 8:
                    stage1(n_ + 1, NP - 1)
                    dma(gg.ap, ggT[(n_ + 1) * 128:(n_ + 2) * 128, :], r=[DB["ggT"]], w=[gg])
                dma(yTb[n_ * 128:(n_ + 1) * 128, :], hb_[n_ % 2].ap, r=pb["hb%d" % (n_ % 2)], w=[DB["yTb"]])
                yield
        genF = genF_()

    if genE is not None or genF is not None:
        for g_ in (genE, genF):
            if g_ is not None:
                for _ in g_:
                    pass
        kb.reset_arena()

    def out_proj(y_names, KC, w_dram, src_name, dst_name):
        wb = A([128, KC, D], BF16)
        wst = [A([128, 4, D], F32) for _ in range(2)]
        wv = w_dram.rearrange("(c p) n -> p c n", p=128)
        for c4 in range(KC // 4):
            ws = wst[c4 % 2]
            dma(ws.ap, wv[:, 4 * c4:4 * c4 + 4, :], w=[ws])
            kb.cast(c4, wb.ap[:, 4 * c4:4 * c4 + 4, :], ws.ap, [ws], [wb])
        yb = [A([128, KC, 512], BF16) for _ in range(2)]
        xt = [A([128, 8, 512], F32) for _ in range(2)]
        hs = fm(kb.dram[src_name]); hd = fm(kb.dram[dst_name])

        def ld(gi):
            p0, n = TG[gi]
            for yi, yn in enumerate(y_names):
                dma(yb[gi % 2].ap[:, 8 * yi:8 * yi + 8, 0:n], fm(kb.dram[yn])[:, :, p0:p0 + n], r=[DB[yn]], w=[yb[gi % 2]])
            dma(xt[gi % 2].ap[:, :, 0:n], hs[:, :, p0:p0 + n], r=[DB[src_name]], w=[xt[gi % 2]])

        ld(0)
        for gi, (p0, n) in enumerate(TG):
            if gi + 1 < len(TG):
                ld(gi + 1)
            y_, x_ = yb[gi % 2], xt[gi % 2]
            for oc in range(8):
                ps = kb.psum()
                mm(ps, [(wb.ap[:, kc, oc * 128:(oc + 1) * 128], y_.ap[:, kc, 0:n]) for kc in range(KC)], r=[wb, y_], out=ps.ap[:, 0:n])
                op("dve", "tensor_tensor", x_.ap[:, oc, 0:n], ps.ap[:, 0:n], x_.ap[:, oc, 0:n], ALU.add, r=[ps, x_], w=[x_])
            dma(hd[:, :, p0:p0 + n], x_.ap[:, :, 0:n], r=[x_], w=[DB[dst_name]])
        kb.reset_arena()

    if "G" in kb.phases:
        out_proj(["yTa", "yTb"], 16, ab_w_out, "hT0", "hT1")

    def mlp(layer, src_name, dst_name):
        w1b = A([128, 8, 4096], BF16)
        w2b = A([128, 32, D], BF16)
        wst = [A([128, 2048], F32) for _ in range(2)]
        w1v = mlp_w1[layer].rearrange("(c p) n -> p c n", p=128)
        w2v = mlp_w2[layer].rearrange("(c p) n -> p c n", p=128)
        w1B = [Buf() for _ in range(8)]
        w2B = [Buf() for _ in range(16)]
        pre = ("C" if layer == 0 else "J0") in kb.phases
        if pre:
            n1, n2 = "w1s%d" % layer, "w2s%d" % layer
            s1v = w1s[layer].rearrange("(c p) n -> p c n", p=128)
            s2v = w2s[layer].rearrange("(c p) n -> p c n", p=128)
            for kc in range(8):
                dma(w1b.ap[:, kc, :], s1v[:, kc, :], r=[DB[n1]], w=[w1B[kc]])
            for c2 in range(8):
                dma(w2b.ap[:, 4 * c2:4 * c2 + 4, :], s2v[:, 4 * c2:4 * c2 + 4, :], r=[DB[n2]], w=[w2B[2 * c2], w2B[2 * c2 + 1]])
        k = 0
        for kc in range(8 if not pre else 0):
            for hf in range(2):
                ws = wst[k % 2]
                dma(ws.ap, w1v[:, kc, hf * 2048:(hf + 1) * 2048], w=[ws])
                kb.cast(k, w1b.ap[:, kc, hf * 2048:(hf + 1) * 2048], ws.ap, [ws], [w1B[kc]])
                k += 1
        for c2 in range(16 if not pre else 0):
            ws = wst[k % 2]
            wsv = ws.ap.rearrange("p (a b) -> p a b", a=2)
            dma(wsv, w2v[:, 2 * c2:2 * c2 + 2, :], w=[ws])
            kb.cast(k, w2b.ap[:, 2 * c2:2 * c2 + 2, :], wsv, [ws], [w2B[c2]])
            k += 1
        NG = 256
        xts = [A([128, 8, NG], F32) for _ in range(2)]
        aT = A([128, 32, NG], BF16)
        sq = A([128, 8, NG], BF16)
        hns = [A([128, 8, NG], BF16) for _ in range(2)]
        rstds = [A([128, NG], F32) for _ in range(2)]
        tmp = [A([128, NG], F32) for _ in range(2)]
        hs = fm(kb.dram[src_name]); hd = fm(kb.dram[dst_name])
        gidx = 2 + layer
        NGR = len(TG2)

        def ld(gi):
            p0, n = TG2[gi]
            dma(xts[gi % 2].ap[:, :, 0:n], hs[:, :, p0:p0 + n], r=[DB[src_name]], w=[xts[gi % 2]])

        def nrm(gi):
            p0, n = TG2[gi]
            hn = hns[gi % 2]
            norm_group(xts[gi % 2], n, gidx, HnOut(hn, lambda kc, n=n, hn=hn: hn.ap[:, kc, 0:n]), sq, rstds[gi % 2])

        ld(0)
        ld(1)
        nrm(0)
        tc_ = 0
        for gi, (p0, n) in enumerate(TG2):
            xt = xts[gi % 2]
            hn = hns[gi % 2]
            for fc in range(32):
                ps = kb.psum()
                mm(ps, [(w1b.ap[:, kc, fc * 128:(fc + 1) * 128], hn.ap[:, kc, 0:n]) for kc in range(8)], r=w1B + [hn], out=ps.ap[:, 0:n])
                tm_ = tmp[tc_ % 2]
                op("act", "activation", out=tm_.ap[:, 0:n], in_=ps.ap[:, 0:n], func=AF.Relu, r=[ps], w=[tm_])
                op("pool" if tc_ % 2 == 0 else "dve", "tensor_tensor", aT.ap[:, fc, 0:n], tm_.ap[:, 0:n], tm_.ap[:, 0:n], ALU.mult, r=[tm_], w=[aT])
                tc_ += 1
            if gi + 1 < NGR:
                nrm(gi + 1)
            for oc in range(8):
                ps = kb.psum()
                mm(ps, [(w2b.ap[:, fc, oc * 128:(oc + 1) * 128], aT.ap[:, fc, 0:n]) for fc in range(32)], r=w2B + [aT], out=ps.ap[:, 0:n])
                op("dve", "tensor_tensor", xt.ap[:, oc, 0:n], ps.ap[:, 0:n], xt.ap[:, oc, 0:n], ALU.add, r=[ps, xt], w=[xt])
            dma(hd[:, :, p0:p0 + n], xt.ap[:, :, 0:n], r=[xt], w=[DB[dst_name]])
            if gi + 2 < NGR:
                ld(gi + 2)
        kb.reset_arena()

    if "H" in kb.phases:
        mlp(0, "hT1", "hT2")

    if "I" in kb.phases:
        hn, hnb = norm_full("hT2", 1)
        wb = A([128, 8, 3072], BF16)
        wst = [A([128, 2, 1024], F32) for _ in range(2)]
        cwv = c_w_in.rearrange("(c p) n -> p c n", p=128)
        wB = [Buf() for _ in range(12)]
        k = 0
        for cb in range(3):
            for c2 in range(4):
                ws = wst[k % 2]
                dma(ws.ap, cwv[:, 2 * c2:2 * c2 + 2, cb * 1024:(cb + 1) * 1024], w=[ws])
                kb.cast(k, wb.ap[:, 2 * c2:2 * c2 + 2, cb * 1024:(cb + 1) * 1024], ws.ap, [ws], [wB[k]])
                k += 1
        cosT = A([128, 33, 8], F32); sinT = A([128, 33, 8], F32)
        dma(cosT.ap, c_cos, w=[cosT]); dma(sinT.ap, c_sin, w=[sinT])
        qk = [A([128, 2048], F32) for _ in range(2)]
        qkb = [A([128, 2048], BF16) for _ in range(2)]
        rt = [A([128, 32, 8], F32) for _ in range(4)]
        vst = [A([128, 1024], BF16) for _ in range(2)]
        qst = [A([128, 16, 128], BF16) for _ in range(2)]
        sqq = [A([128, 2048], F32) for _ in range(1)]
        ssn = A([128, 32], F32)
        mrun = A([128, 32], F32)
        op("pool", "memset", mrun.ap, 0.0, w=[mrun])
        pend_tr = []
        for ti, (p0, n) in enumerate(TT):
            tg_i = 0 if ti == 0 else 1 + (ti - 1) // 4
            q_, qb_, v_, qs_ = qk[ti % 2], qkb[ti % 2], vst[ti % 2], qst[ti % 2]
            for blk in range(6):
                ps = kb.psum()
                mm(ps, [(hn.ap[:, kc, p0:p0 + n], wb.ap[:, kc, blk * 512:(blk + 1) * 512]) for kc in range(8)], r=wB + [hnb[tg_i]], out=ps.ap[0:n, :])
                if blk < 2:
                    op("act", "activation", out=q_.ap[0:n, blk * 512:(blk + 1) * 512], in_=ps.ap[0:n, :], func=AF.Copy, scale=0.125, r=[ps], w=[q_])
                elif blk < 4:
                    op("dve", "tensor_copy", q_.ap[0:n, blk * 512:(blk + 1) * 512], ps.ap[0:n, :], r=[ps], w=[q_])
                else:
                    op("act", "activation", out=v_.ap[0:n, (blk - 4) * 512:(blk - 3) * 512], in_=ps.ap[0:n, :], func=AF.Copy, r=[ps], w=[v_])
            dma(vtok2[p0:p0 + n, :], v_.ap[0:n, :], r=[v_], w=[DB["vtok2"]])
            qv = q_.ap.rearrange("p (g d) -> p g d", d=64)
            x1 = qv[0:n, :, 0:8]; x2 = qv[0:n, :, 8:16]
            cb_ = cosT.ap[0:n, ti, :].unsqueeze(1).to_broadcast([n, 32, 8])
            sb_ = sinT.ap[0:n, ti, :].unsqueeze(1).to_broadcast([n, 32, 8])
            a1, a2, a3, a4 = [t_.ap[0:n] for t_ in rt]
            op("pool", "tensor_tensor", a1, x1, cb_, ALU.mult, r=[q_, cosT], w=[rt[0]])
            op("pool", "tensor_tensor", a2, x2, sb_, ALU.mult, r=[q_, sinT], w=[rt[1]])
            op("dve", "tensor_tensor", a3, x2, cb_, ALU.mult, r=[q_, cosT], w=[rt[2]])
            op("dve", "tensor_tensor", a4, x1, sb_, ALU.mult, r=[q_, sinT], w=[rt[3]])
            op("pool", "tensor_tensor", x1, a1, a2, ALU.subtract, r=[rt[0], rt[1]], w=[q_])
            op("dve", "tensor_tensor", x2, a3, a4, ALU.add, r=[rt[2], rt[3]], w=[q_])
            op("act", "activation", out=qb_.ap[0:n, :], in_=q_.ap[0:n, :], func=AF.Copy, r=[q_], w=[qb_])
            op("act", "activation", out=sqq[0].ap[0:n, :], in_=q_.ap[0:n, :], func=AF.Square, r=[q_], w=[sqq[0]])
            op("dve", "tensor_reduce", ssn.ap[0:n, :], sqq[0].ap[0:n, :].rearrange("p (g d) -> p g d", d=64), AX.X, ALU.add, r=[sqq[0]], w=[ssn])
            op("dve", "tensor_tensor", mrun.ap[0:n, :], mrun.ap[0:n, :], ssn.ap[0:n, :], ALU.max, r=[mrun, ssn], w=[mrun])

            def trn(ti=ti, p0=p0, n=n, qb_=qb_, qs_=qs_):
                for c4 in range(4):
                    pT = kb.psum()
                    pTb = pT.ap.bitcast(BF16)
                    for j in range(4):
                        c = c4 * 4 + j
                        op("pe", "transpose", pTb[:, j * 128:j * 128 + n], qb_.ap[0:n, c * 128:(c + 1) * 128], identb.ap[0:n, 0:n],
                           r=[qb_, identb], w=[pT], inc=(j == 3))
                    src = pTb[:, 0:512].rearrange("p (a b) -> p a b", a=4)[:, :, 0:n]
                    if c4 % 2 == 0:
                        op("act", "activation", out=qs_.ap[:, 4 * c4:4 * c4 + 4, 0:n], in_=src, func=AF.Copy, r=[pT], w=[qs_])
                    else:
                        op("dve", "tensor_copy", qs_.ap[:, 4 * c4:4 * c4 + 4, 0:n], src, r=[pT], w=[qs_])
                dma(fm(qkT)[:, :, p0:p0 + n], qs_.ap[:, :, 0:n], r=[qs_], w=[DB["qkT"]])
            pend_tr.append(trn)
            if len(pend_tr) > 1:
                pend_tr.pop(0)()
        while pend_tr:
            pend_tr.pop(0)()
        pq = kb.psum()
        op("pe", "transpose", pq.ap[0:32, 0:128], mrun.ap, ident.ap, r=[mrun, ident], w=[pq])
        mcol = A([32, 1], F32)
        op("dve", "reduce_max", mcol.ap, pq.ap[0:32, 0:128], AX.X, r=[pq], w=[mcol])
        pq2 = kb.psum()
        op("pe", "transpose", pq2.ap[0:1, 0:32], mcol.ap, ident.ap[0:32, 0:32], r=[mcol, ident], w=[pq2])
        mrow = A([1, 32], F32)
        op("dve", "tensor_copy", mrow.ap, pq2.ap[0:1, 0:32], r=[pq2], w=[mrow])
        brow = A([1, 16], F32)
        op("dve", "tensor_tensor", brow.ap, mrow.ap[:, 0:16], mrow.ap[:, 16:32], ALU.mult, r=[mrow], w=[brow])
        op("act", "activation", out=brow.ap, in_=brow.ap, func=AF.Ln, scale=1.05, r=[brow], w=[brow])
        op("act", "activation", out=brow.ap, in_=brow.ap, func=AF.Exp, scale=0.5, r=[brow], w=[brow])
        op("dve", "tensor_scalar", brow.ap, brow.ap, -1.0, None, ALU.mult, r=[brow], w=[brow])
        pq3 = kb.psum()
        mm(pq3, [(ones_f.ap[0:1, :], brow.ap)], r=[ones_f, brow], out=pq3.ap[:, 0:16])
        op("dve", "tensor_copy", shift_all.ap, pq3.ap[:, 0:16], r=[pq3], w=[shift_all])
        kb.reset_arena()

    if "J0" in kb.phases:
        lv = A([1, 256], F32); l2 = A([1, 4], F32); lam_bc = A([128, 1], F32)
        dma(lv.ap, c_lambda, w=[lv])
        op("dve", "tensor_tensor", lv.ap[:, 0:64], lv.ap[:, 0:64], lv.ap[:, 64:128], ALU.mult, r=[lv], w=[lv])
        op("dve", "tensor_tensor", lv.ap[:, 128:192], lv.ap[:, 128:192], lv.ap[:, 192:256], ALU.mult, r=[lv], w=[lv])
        op("dve", "reduce_sum", l2.ap[:, 0:1], lv.ap[:, 0:64], AX.X, r=[lv], w=[l2])
        op("dve", "reduce_sum", l2.ap[:, 1:2], lv.ap[:, 128:192], AX.X, r=[lv], w=[l2])
        op("act", "activation", out=l2.ap[:, 0:2], in_=l2.ap[:, 0:2], func=AF.Exp, r=[l2], w=[l2])
        op("dve", "tensor_tensor", l2.ap[:, 2:3], l2.ap[:, 1:2], l2.ap[:, 0:1], ALU.subtract, r=[l2], w=[l2])
        op("dve", "tensor_scalar", l2.ap[:, 2:3], l2.ap[:, 2:3], -LAMBDA_INIT, None, ALU.add, r=[l2], w=[l2])
        psl = kb.psum()
        mm(psl, [(ones_f.ap[0:1, :], l2.ap[0:1, 2:3])], r=[ones_f, l2], out=psl.ap[:, 0:1])
        op("dve", "tensor_copy", lam_bc.ap, psl.ap[:, 0:1], r=[psl], w=[lam_bc])
        sln = A([128, 128], F32)
        dma(sln.ap, c_subln.partition_broadcast(128), w=[sln])
        op("dve", "tensor_scalar", sln.ap, sln.ap, 1.0 - LAMBDA_INIT, None, ALU.mult, r=[sln], w=[sln])
        sel = A([128, 2], BF16)
        op("pool", "memset", sel.ap, 0.0, w=[sel])
        op("pool", "memset", sel.ap[0:64, 0:1], 1.0, w=[sel])
        op("pool", "memset", sel.ap[64:128, 1:2], 1.0, w=[sel])
        qTh = [A([128, T], BF16) for _ in range(2)]
        kTh = [A([128, T], BF16) for _ in range(2)]
        Vh = [A([128, 33, 129], BF16) for _ in range(2)]
        for v_ in Vh:
            op("pool", "memset", v_.ap[:, :, 128:129], 1.0, w=[v_])
        sqt = A([128, T], BF16)
        ssq = A([2, 2, T], F32)
        mx = A([2, 4], F32)
        shift = A([128, 2], F32)
        mxd = A([2, 2], F32)
        PT = [A([128, 512], BF16) for _ in range(4)]
        gbuf = []
        for _ in range(2):
            gbuf.append({"Os": [A([128, 512], F32) for _ in range(2)], "rbc": [A([128, 512], F32) for _ in range(2)],
                         "sq": A([128, 512], BF16), "rstd": A([128, 512], F32), "on": A([128, 512], BF16)})
        accs = [A([128, 512], F32) for _ in range(2)]
        accbs = [A([128, 512], BF16) for _ in range(2)]
        onesq = A([128, 128], BF16)
        op("pool", "memset", onesq.ap, 1.0, w=[onesq])
        qz = [[A([128, T], BF16) for _ in range(2)] for _ in range(2)]
        for qq in qz:
            op("pool", "memset", qq[0].ap[64:128, :], 0.0, w=[qq[0]])
            op("pool", "memset", qq[1].ap[0:64, :], 0.0, w=[qq[1]])
        one1 = A([128, 1], BF16)
        op("pool", "memset", one1.ap, 1.0, w=[one1])
        o128 = A([128, 128], BF16)
        op("pool", "memset", o128.ap, 1.0 / 128, w=[o128])
        slc = A([128, 1], F32)
        dma(slc.ap, c_subln.rearrange("o (p u) -> p (o u)", u=1), w=[slc])
        op("dve", "tensor_scalar", slc.ap, slc.ap, 1.0 - LAMBDA_INIT, None, ALU.mult, r=[slc], w=[slc])
        qkv = fm(qkT)

        def aload(h):
            s = h % 2
            dma(qTh[s].ap, qkT[h * 128:(h + 1) * 128, :], r=[DB["qkT"]], w=[qTh[s]])
            dma(kTh[s].ap, qkT[1024 + h * 128:1024 + (h + 1) * 128, :], r=[DB["qkT"]], w=[kTh[s]])
            dma(qz[s][0].ap[0:64, :], qkT[h * 128:h * 128 + 64, :], r=[DB["qkT"]], w=[qz[s][0]])
            dma(qz[s][1].ap[64:128, :], qkT[h * 128 + 64:(h + 1) * 128, :], r=[DB["qkT"]], w=[qz[s][1]])
            dma(Vh[s].ap[0:16, 0, 0:128], vtok2[0:16, h * 128:(h + 1) * 128], r=[DB["vtok2"]], w=[Vh[s]])
            dma(Vh[s].ap[:, 1:33, 0:128], vtok2[16:T, h * 128:(h + 1) * 128].rearrange("(j p) c -> p j c", p=128), r=[DB["vtok2"]], w=[Vh[s]])

        aload(0)
        pcast = gen_precast(1)
        pti = 0
        oi = 0
        for h in range(8):
            if h + 1 < 8:
                aload(h + 1)
            s = h % 2
            qt, kt, vh = qTh[s], kTh[s], Vh[s]
            shift = Tl(shift_all.ap[:, 2 * h:2 * h + 2], shift_all.b)
            tasks = []
            for g in range(-1, 8):
                if g < 0:
                    q0, nq_tot, nblk = 0, 16, 1
                else:
                    q0, nq_tot, nblk = 16 + 512 * g, 512, 4
                for c in range(2):
                    kbl = [(0, 0, 16, 0, g < 0)]
                    if g >= 0:
                        for j in range(4 * g):
                            kbl.append((1 + j, 16 + 128 * j, 128, 0, False))
                        for i in range(4):
                            kbl.append((1 + 4 * g + i, 16 + 128 * (4 * g + i), 128, i, True))
                    for bi_, e in enumerate(kbl):
                        tasks.append(dict(g=g, c=c, q0=q0, nq_tot=nq_tot, nblk=nblk, kb=e, first=(bi_ == 0), last=(bi_ == len(kbl) - 1)))
            LA = 2
            cur = {}
            deferred = []

            def stage1(t):
                kt_i, k0, nk, qb0, masked = t["kb"]
                c = t["c"]
                qzc = qz[s][c]
                qs = t["q0"] + qb0 * 128
                ncol = t["nq_tot"] - qb0 * 128
                pS = kb.psum_from("pS", [4, 5, 6])
                mm(pS, [(kt.ap[:, k0:k0 + nk], qzc.ap[:, qs:qs + ncol])], r=[kt, qzc], out=pS.ap[0:nk, 0:ncol])
                pt = PT[cur.setdefault("pti", 0) % len(PT)]
                cur["pti"] += 1
                op("act", "activation", out=pt.ap[0:nk, 0:ncol], in_=pS.ap[0:nk, 0:ncol], func=AF.Exp, bias=shift.ap[0:nk, c:c + 1],
                   r=[pS, shift], w=[pt])
                if masked:
                    m = min(128, ncol)
                    op("pool", "tensor_tensor", pt.ap[0:nk, 0:m], pt.ap[0:nk, 0:m], tri.ap[0:nk, 0:m], ALU.mult, r=[pt, tri], w=[pt])
                t["pt"] = pt

            def stage2(t, idx):
                kt_i, k0, nk, qb0, masked = t["kb"]
                g, c, nq_tot, q0 = t["g"], t["c"], t["nq_tot"], t["q0"]
                pt = t["pt"]
                ncol = nq_tot - qb0 * 128
                qoff = qb0 * 128
                if t["first"]:
                    cur["pOT"] = kb.psum_from("pOT", [0, 1])
                    cur["pL"] = kb.psum_from("pL", [2, 3])
                    if c == 0:
                        cur["gp"] = cur.setdefault("gcount", 0) % 2
                        cur["gcount"] += 1
                pOT, pL = cur["pOT"], cur["pL"]
                op("pe", "matmul", pOT.ap[:, qoff:qoff + ncol], vh.ap[0:nk, kt_i, 0:128], pt.ap[0:nk, 0:ncol], start=t["first"], stop=t["last"],
                   r=[pt, vh], w=[pOT], inc=False)
                op("pe", "matmul", pL.ap[:, qoff:qoff + ncol], onesq.ap[0:nk, :], pt.ap[0:nk, 0:ncol], start=t["first"], stop=t["last"],
                   r=[pt, onesq], w=[pL], inc=True)
                if not t["last"]:
                    return
                B = gbuf[cur["gp"]]
                n_ = nq_tot
                Os0, rbc = B["Os"][0], B["rbc"][c]
                sq, rstd, onb_ = B["sq"], B["rstd"], B["on"]

                def stepA(n_=n_, Os0=Os0, rbc=rbc, pOT=pOT, pL=pL, c=c, sq=sq):
                    op("dve", "reciprocal", rbc.ap[:, 0:n_], pL.ap[:, 0:n_], r=[pL], w=[rbc])
                    if c == 0:
                        op("dve", "tensor_tensor", Os0.ap[:, 0:n_], pOT.ap[:, 0:n_], rbc.ap[:, 0:n_], ALU.mult, r=[pOT, rbc], w=[Os0])
                    else:
                        op("dve", "tensor_tensor", rbc.ap[:, 0:n_], pOT.ap[:, 0:n_], rbc.ap[:, 0:n_], ALU.mult, r=[pOT, rbc], w=[rbc])
                        op("dve", "scalar_tensor_tensor", Os0.ap[:, 0:n_], rbc.ap[:, 0:n_], lam_bc.ap[:, 0:1], Os0.ap[:, 0:n_], ALU.mult, ALU.add,
                           r=[rbc, lam_bc, Os0], w=[Os0])
                        op("pool", "tensor_tensor", sq.ap[:, 0:n_], Os0.ap[:, 0:n_], Os0.ap[:, 0:n_], ALU.mult, r=[Os0], w=[sq])

                deferred.append((idx + 2, stepA))
                if c == 1:
                    def stepC(n_=n_, Os0=Os0, sq=sq, rstd=rstd, onb_=onb_, q0=q0):
                        pSS = kb.ps[7]
                        mm(pSS, [(o128.ap, sq.ap[:, 0:n_])], r=[o128, sq], out=pSS.ap[:, 0:n_])
                        op("act", "activation", out=rstd.ap[:, 0:n_], in_=pSS.ap[:, 0:n_], func=AF.Ln, bias=epsc.ap, r=[pSS, epsc], w=[rstd])
                        op("act", "activation", out=rstd.ap[:, 0:n_], in_=rstd.ap[:, 0:n_], func=AF.Exp, scale=-0.5, r=[rstd], w=[rstd])
                        op("dve", "scalar_tensor_tensor", onb_.ap[:, 0:n_], Os0.ap[:, 0:n_], slc.ap[:, 0:1], rstd.ap[:, 0:n_], ALU.mult, ALU.mult,
                           r=[Os0, slc, rstd], w=[onb_])
                        dma(oT[h * 128:(h + 1) * 128, q0:q0 + n_], onb_.ap[:, 0:n_], r=[onb_], w=[DB["oT"]])
                    deferred.append((idx + 10, stepC))
                deferred.sort(key=lambda x: x[0])

            NTK = len(tasks)
            for i in range(NTK + LA):
                if i % 20 == 0:
                    next(pcast, None)
                if i < NTK:
                    stage1(tasks[i])
                j = i - LA
                if j >= 0:
                    while deferred and deferred[0][0] <= j:
                        deferred.pop(0)[1]()
                    stage2(tasks[j], j)
            while deferred:
                deferred.pop(0)[1]()
        for _ in pcast:
            pass
        kb.reset_arena()

    if "J" in kb.phases:
        out_proj(["oT"], 8, c_w_out, "hT2", "hT3")
    if "K2" in kb.phases:
        mlp(1, "hT3", "hT4")

    if "M" in kb.phases:
        xts = [A([128, 8, 512], F32) for _ in range(2)]
        sqs = [A([128, 8, 512], BF16) for _ in range(2)]
        rs = [A([128, 512], F32) for _ in range(2)]
        xn = [A([128, 8, 512], F32) for _ in range(2)]
        ostg = [A([128, D], F32) for _ in range(2)]
        src = fm(hT[4])
        oc_ = 0
        for gi, (p0, n) in enumerate(TG[1:]):
            xt = xts[gi % 2]
            dma(xt.ap, src[:, :, p0:p0 + n], r=[DB["hT4"]], w=[xt])
            xn_ = xn[gi % 2]
            norm_group(xt, n, 4, HnOut(xn_, lambda kc, xn_=xn_: xn_.ap[:, kc, :]), sqs[gi % 2], rs[gi % 2])
            for j in range(4):
                os_ = ostg[oc_ % 2]
                oc_ += 1
                for half in range(2):
                    ps = kb.psum()
                    for kk in range(4):
                        kc = half * 4 + kk
                        op("pe", "transpose", ps.ap[:, kk * 128:(kk + 1) * 128], xn_.ap[:, kc, j * 128:(j + 1) * 128], ident.ap,
                           r=[xn_, ident], w=[ps], inc=(kk == 3))
                    if half == 0:
                        op("act", "activation", out=os_.ap[:, 0:512], in_=ps.ap, func=AF.Copy, r=[ps], w=[os_])
                    else:
                        op("dve", "tensor_copy", os_.ap[:, 512:1024], ps.ap, r=[ps], w=[os_])
                r0 = p0 - NM + j * 128
                dma(out[r0:r0 + 128, :], os_.ap, r=[os_], w=[DB["out"]])
        kb.reset_arena()

    S.barrier()
    S.emit()
    return kb


def _consts():
    ident = np.eye(128, dtype=np.float32)
    tri = np.triu(np.ones((128, 128), dtype=np.float32))
    pos = np.arange(T, dtype=np.float32)
    inv_freq = np.power(np.float32(500000.0), -np.arange(0, 16, 2, dtype=np.float32) / np.float32(16)).astype(np.float32)
    ang = (pos[:, None] * inv_freq[None, :]).astype(np.float32)
    cos = np.cos(ang).astype(np.float32)
    sin = np.sin(ang).astype(np.float32)

    def tile(a):
        o = np.zeros((128, 33, 8), dtype=np.float32)
        o[0:16, 0] = a[0:16]
        o[:, 1:] = a[16:].reshape(32, 128, 8).transpose(1, 0, 2)
        return o
    return {"c_ident": ident, "c_tri": tri, "c_cos": tile(cos), "c_sin": tile(sin)}


def _core_inputs(inputs, b):
    f = lambda a: np.ascontiguousarray(np.asarray(a, dtype=np.float32))
    m = {
        "x": f(inputs["x"][b]),
        "meta_tokens": f(inputs["meta_tokens"]),
        "norm_mix": f(inputs["norm_mix"]),
        "norm_mlp": f(inputs["norm_mlp"]),
        "norm_final": f(inputs["norm_final"]).reshape(1, D),
        "ab_w_in": f(inputs["ab_w_in"][0]),
        "ab_if_bias": f(inputs["ab_if_bias"][0]).reshape(8, 1),
        "mlstm_norm": f(inputs["mlstm_norm"][0]).reshape(1, D),
        "lru_conv_w": f(inputs["lru_conv_w"][0]),
        "lru_conv_b": f(inputs["lru_conv_b"][0]).reshape(1, D),
        "lru_w_r": f(inputs["lru_w_r"][0]),
        "lru_b_r": f(inputs["lru_b_r"][0]).reshape(1, D),
        "lru_w_i": f(inputs["lru_w_i"][0]),
        "lru_b_i": f(inputs["lru_b_i"][0]).reshape(1, D),
        "lru_lambda": f(inputs["lru_lambda"][0]).reshape(1, D),
        "ab_w_out": f(inputs["ab_w_out"][0]),
        "c_w_in": f(inputs["c_w_in"][0]),
        "c_lambda": f(inputs["c_lambda"][0]).reshape(1, 256),
        "c_subln": f(inputs["c_subln"][0]).reshape(1, 128),
        "c_w_out": f(inputs["c_w_out"][0]),
        "mlp_w1": f(inputs["mlp_w1"]),
        "mlp_w2": f(inputs["mlp_w2"]),
    }
    m.update(_consts())
    return m


def kernel(**inputs):
    kb = build()
    maps = []
    for b in range(8):
        m = _core_inputs(inputs, b)
        maps.append({k: m[k] for k in kb.in_names})
    res = run_bass_kernel_spmd(kb.nc, maps, core_ids=list(range(8)))
    return np.stack([np.asarray(r["out"], dtype=np.float32) for r in res.results], axis=0)
```

```python
import math
import contextlib
import numpy as np
import concourse.bass as bass
import concourse.mybir as mybir
from concourse.bass_utils import run_bass_kernel_spmd

F32 = mybir.dt.float32
BF16 = mybir.dt.bfloat16
AF = mybir.ActivationFunctionType
ALU = mybir.AluOpType
AX = mybir.AxisListType

T = 4112
NM = 16
D = 1024
EPS = 1e-6
TG = [(0, 16)] + [(16 + 512 * i, 512) for i in range(8)]
TT = [(0, 16)] + [(16 + 128 * i, 128) for i in range(32)]
TG2 = [(0, 16)] + [(16 + 256 * i, 256) for i in range(16)]
LAMBDA_INIT = 0.8 - 0.6 * math.exp(-0.3 * 1)
SEM_LIMIT = 24000
DUMMY_MM = 1
ARENA_BYTES = 192 * 1024

ENGS = ("pe", "act", "dve", "pool", "sp")


class Buf:
    __slots__ = ("w", "r")

    def __init__(self):
        self.w = None
        self.r = {}


class Sched:
    def __init__(self, nc, n_dma_sems=24):
        self.nc = nc
        self.ops = {e: [] for e in ENGS}
        self.cnt = {e: 0 for e in ENGS if e != "sp"}
        self.epoch = {e: 0 for e in ENGS if e != "sp"}
        self.seen = {e: {} for e in ENGS}
        self.n_dma = n_dma_sems
        self.dma_val = [0] * n_dma_sems
        self.dma_epoch = [0] * n_dma_sems
        self.dma_rr = 0
        self.keys = set()
        self.last = {}

    def _need(self, eng, tok, waits):
        if tok is None:
            return
        k, v = tok
        if k[0] == "pe" and eng == "pe":
            return
        if self.seen[eng].get(k, 0) >= v:
            return
        if waits.get(k, 0) < v:
            waits[k] = v

    def _deps(self, eng, reads, writes):
        waits = {}
        for b in reads:
            self._need(eng, b.w, waits)
        for b in writes:
            self._need(eng, b.w, waits)
            for t in b.r.items():
                self._need(eng, t, waits)
        for k, v in waits.items():
            self.seen[eng][k] = v
        return list(waits.items())

    def _mark(self, tok, reads, writes):
        for b in reads:
            if b.r.get(tok[0], 0) < tok[1]:
                b.r[tok[0]] = tok[1]
        for b in writes:
            b.w = tok
            b.r = {}
        self.last[tok[0]] = max(self.last.get(tok[0], 0), tok[1])

    def _next_tok(self, eng, advance):
        c, ep = self.cnt[eng], self.epoch[eng]
        if c >= SEM_LIMIT:
            c, ep = 0, ep + 1
        if advance:
            self.cnt[eng], self.epoch[eng] = c + 1, ep
        key = (eng, ep)
        self.keys.add(key)
        return (key, c + 1)

    def op(self, eng, meth, *args, reads=(), writes=(), inc=True, **kw):
        waits = self._deps(eng, reads, writes)
        tok = self._next_tok(eng, inc)
        self.ops[eng].append((waits, (meth, args, kw), tok[0] if inc else None, 1))
        self._mark(tok, reads, writes)
        return tok

    def dma(self, out, in_, reads=(), writes=(), q="sp", **kw):
        kw = dict(kw)
        kw["out"] = out
        kw["in_"] = in_
        i = self.dma_rr
        self.dma_rr = (i + 1) % self.n_dma
        waits = dict(self._deps(q, reads, writes))
        key = ("dma", i, self.dma_epoch[i])
        prev = self.dma_val[i]
        if prev and self.seen[q].get(key, 0) < prev:
            waits[key] = max(waits.get(key, 0), prev)
            self.seen[q][key] = prev
        if prev + 16 > SEM_LIMIT:
            self.dma_epoch[i] += 1
            self.dma_val[i] = 0
            key = ("dma", i, self.dma_epoch[i])
        self.dma_val[i] += 16
        self.keys.add(key)
        tok = (key, self.dma_val[i])
        self.ops[q].append((list(waits.items()), ("dma_start", (), kw), key, 16))
        self._mark(tok, reads, writes)
        return tok

    def barrier(self):
        for e in ENGS:
            waits = {}
            for k, v in self.last.items():
                self._need(e, (k, v), waits)
            for k, v in waits.items():
                self.seen[e][k] = v
            if waits:
                self.ops[e].append((list(waits.items()), None, None, 0))

    def emit(self):
        nc = self.nc
        with contextlib.ExitStack() as st:
            sems = {}
            for k in sorted(self.keys, key=str):
                sems[k] = st.enter_context(nc.semaphore("s_" + "_".join(str(x) for x in k)))
            block = st.enter_context(nc.Block())

            def run(engname):
                def body(e):
                    for waits, fn, post, amt in self.ops[engname]:
                        for k, v in waits:
                            e.wait_ge(sems[k], v)
                        if fn is None:
                            continue
                        ins = getattr(e, fn[0])(*fn[1], **fn[2])
                        if post is not None:
                            ins.then_inc(sems[post], amt)
                return body

            block.tensor(run("pe"))
            block.scalar(run("act"))
            block.vector(run("dve"))
            block.gpsimd(run("pool"))
            block.sync(run("sp"))


class Tl:
    __slots__ = ("ap", "b")

    def __init__(self, ap, b=None):
        self.ap = ap
        self.b = b if b is not None else Buf()

    def __getitem__(self, k):
        return self.ap[k]


class KB:
    def __init__(self, phases, ext_out=()):
        self.phases = set(phases)
        self.ext_out = set(ext_out)
        nc = self.nc = bass.Bass("TRN2", target_bir_lowering=False)
        self.S = Sched(nc)
        self.dram = {}
        self.meta = {}
        self.dbuf = {}
        self.in_names = []
        self.out_names = []

    def dt(self, name, shape, dtype, producer=None):
        if name in self.dram:
            return self.dram[name]
        if producer is None or producer not in self.phases:
            kind = "ExternalInput"
            self.in_names.append(name)
        elif name in self.ext_out:
            kind = "ExternalOutput"
            self.out_names.append(name)
        else:
            kind = "Internal"
        t = self.nc.dram_tensor(name, list(shape), dtype, kind=kind).ap()
        self.meta[name] = (list(shape), dtype)
        self.dram[name] = t
        self.dbuf[name] = Buf()
        return t

    def setup_sbuf(self):
        nc = self.nc
        self.persist = nc.alloc_sbuf_tensor("persist", [128, 3072], F32)
        self.poff = 0
        self.arena = nc.alloc_sbuf_tensor("arena", [128, ARENA_BYTES // 4], F32)
        self.aoff = 0
        self.ps = [Tl(nc.alloc_psum_tensor("ps%d" % i, [128, 512], F32)[:]) for i in range(8)]
        self.psi = 0
        self.psc = {}

    def _carve(self, base, off, shape, dtype):
        n = int(np.prod(shape[1:]))
        nb = n * (4 if dtype == F32 else 2)
        nb = (nb + 63) // 64 * 64
        if dtype == F32:
            v = base[0:shape[0], off // 4: off // 4 + n]
        else:
            v = base[0:shape[0], off // 4: off // 4 + (n + 1) // 2].bitcast(BF16)[:, 0:n]
        if len(shape) == 3:
            v = v.rearrange("p (a b) -> p a b", a=shape[1])
        elif len(shape) == 4:
            v = v.rearrange("p (a b c) -> p a b c", a=shape[1], b=shape[2])
        return v, nb

    def P(self, shape, dtype=F32):
        v, nb = self._carve(self.persist, self.poff, shape, dtype)
        self.poff += nb
        assert self.poff <= 3072 * 4, self.poff
        return Tl(v)

    def A(self, shape, dtype=F32):
        v, nb = self._carve(self.arena, self.aoff, shape, dtype)
        self.aoff += nb
        assert self.aoff <= ARENA_BYTES, (self.aoff, shape)
        return Tl(v)

    def reset_arena(self):
        self.S.barrier()
        self.aoff = 0

    def psum(self):
        p = self.ps[self.psi]
        self.psi = (self.psi + 1) % 8
        return p

    def psum_from(self, key, banks):
        c = self.psc.get(key, 0)
        self.psc[key] = c + 1
        return self.ps[banks[c % len(banks)]]

    def cast(self, i, out, in_, r, w):
        if i % 2 == 0:
            self.op("act", "activation", out=out, in_=in_, func=AF.Copy, r=r, w=w)
        else:
            self.op("pool", "tensor_copy", out, in_, r=r, w=w)

    def op(self, eng, meth, *args, r=(), w=(), inc=True, **kw):
        return self.S.op(eng, meth, *args, reads=[x.b if isinstance(x, Tl) else x for x in r],
                         writes=[x.b if isinstance(x, Tl) else x for x in w], inc=inc, **kw)

    def dma(self, out, in_, r=(), w=(), **kw):
        return self.S.dma(out, in_, reads=[x.b if isinstance(x, Tl) else x for x in r],
                          writes=[x.b if isinstance(x, Tl) else x for x in w], **kw)

    def mm(self, ps, pairs, r, n_out=None, out=None):
        o = out if out is not None else ps.ap
        for i, (l, rh) in enumerate(pairs):
            last = i == len(pairs) - 1
            self.op("pe", "matmul", o, l, rh, start=(i == 0), stop=last, r=r, w=[ps], inc=last)


def build(phases=None, ext_out=("out",)):
    ALL = ["A", "C", "D", "E", "F", "G", "H", "I", "J0", "J", "K2", "M"]
    if phases is None:
        phases = ALL
    kb = KB(phases, ext_out)
    nc, S = kb.nc, kb.S
    dt = kb.dt
    op, dma, mm = kb.op, kb.dma, kb.mm

    x = dt("x", [4096, D], F32)
    meta = dt("meta_tokens", [NM, D], F32)
    norm_mix = dt("norm_mix", [2, D], F32)
    norm_mlp = dt("norm_mlp", [2, D], F32)
    norm_final = dt("norm_final", [1, D], F32)
    ab_w_in = dt("ab_w_in", [D, 6152], F32)
    ab_if_bias = dt("ab_if_bias", [8, 1], F32)
    mlstm_norm = dt("mlstm_norm", [1, D], F32)
    lru_conv_w = dt("lru_conv_w", [4, D], F32)
    lru_conv_b = dt("lru_conv_b", [1, D], F32)
    lru_w_r = dt("lru_w_r", [8, 128, 128], F32)
    lru_b_r = dt("lru_b_r", [1, D], F32)
    lru_w_i = dt("lru_w_i", [8, 128, 128], F32)
    lru_b_i = dt("lru_b_i", [1, D], F32)
    lru_lambda = dt("lru_lambda", [1, D], F32)
    ab_w_out = dt("ab_w_out", [2048, D], F32)
    c_w_in = dt("c_w_in", [D, 3072], F32)
    c_lambda = dt("c_lambda", [1, 256], F32)
    c_subln = dt("c_subln", [1, 128], F32)
    c_w_out = dt("c_w_out", [D, D], F32)
    mlp_w1 = dt("mlp_w1", [2, D, 4096], F32)
    mlp_w2 = dt("mlp_w2", [2, 4096, D], F32)
    c_ident = dt("c_ident", [128, 128], F32)
    c_tri = dt("c_tri", [128, 128], F32)
    c_cos = dt("c_cos", [128, 33, 8], F32)
    c_sin = dt("c_sin", [128, 33, 8], F32)

    hT = [dt("hT%d" % i, [D, T], F32, p) for i, p in enumerate(["A", "G", "H", "J", "K2"])]
    qT = dt("qT", [D, T], BF16, "C")
    kT = dt("kT", [D, T], BF16, "C")
    ktok = dt("ktok", [T, D], BF16, "C")
    vtok = dt("vtok", [T, D], BF16, "C")
    otok = dt("otok", [T, D], BF16, "C")
    giT = dt("giT", [4, T], F32, "C")
    gfT = dt("gfT", [4, T], F32, "C")
    xbT = dt("xbT", [D, T], F32, "C")
    ggT = dt("ggT", [D, T], F32, "C")
    gprep = dt("gprep", [128, 33 * 8 + 33 * 4], F32, "D")
    yTa = dt("yTa", [D, T], BF16, "E")
    yTb = dt("yTb", [D, T], BF16, "F")
    qkT = dt("qkT", [2048, T], BF16, "I")
    vtok2 = dt("vtok2", [T, D], BF16, "I")
    oT = dt("oT", [D, T], BF16, "J0")
    w1s = [dt("w1s%d" % l, [D, 4096], BF16, p) for l, p in enumerate(["C", "J0"])]
    w2s = [dt("w2s%d" % l, [4096, D], BF16, p) for l, p in enumerate(["C", "J0"])]
    out = dt("out", [4096, D], F32, "M")
    DB = kb.dbuf

    kb.setup_sbuf()
    P, A = kb.P, kb.A

    def gen_precast(layer):
        stg = [A([128, 2048], F32) for _ in range(2)]
        stb = [A([128, 2048], BF16) for _ in range(2)]
        w1v = mlp_w1[layer].rearrange("(c p) n -> p c n", p=128)
        w2v = mlp_w2[layer].rearrange("(c p) n -> p c n", p=128)
        d1 = w1s[layer].rearrange("(c p) n -> p c n", p=128)
        d2 = w2s[layer].rearrange("(c p) n -> p c n", p=128)
        jobs = []
        for kc in range(8):
            for hf in range(2):
                jobs.append((w1v[:, kc, hf * 2048:(hf + 1) * 2048], d1[:, kc, hf * 2048:(hf + 1) * 2048], None, "w1s%d" % layer))
        for c2 in range(16):
            jobs.append((w2v[:, 2 * c2:2 * c2 + 2, :], d2[:, 2 * c2:2 * c2 + 2, :], 2, "w2s%d" % layer))
        prev = None
        for k, (src, dst, a3, dname) in enumerate(jobs):
            sg, sb_ = stg[k % 2], stb[k % 2]
            sv = sg.ap if a3 is None else sg.ap.rearrange("p (a b) -> p a b", a=a3)
            bv = sb_.ap if a3 is None else sb_.ap.rearrange("p (a b) -> p a b", a=a3)
            dma(sv, src, w=[sg])
            yield
            op("pool", "tensor_copy", sb_.ap, sg.ap, r=[sg], w=[sb_])
            yield
            if prev is not None:
                dma(prev[0], prev[1], r=[prev[2]], w=[DB[prev[3]]])
            prev = (dst, bv, sb_, dname)
            yield
        dma(prev[0], prev[1], r=[prev[2]], w=[DB[prev[3]]])
        yield

    def fm(t):
        return t.rearrange("(c p) t -> p c t", p=128)

    ident = P([128, 128], F32)
    identb = P([128, 128], BF16)
    tri = P([128, 128], F32)
    ones_b = P([128, 128], BF16)
    ones_f = P([128, 128], F32)
    epsc = P([128, 1], F32)
    gam = P([128, 8, 8], F32)
    lrup = P([128, 8, 8], F32)
    clam = P([128, 8], F32)
    shift_all = P([128, 16], F32)
    dma(ident.ap, c_ident, w=[ident])
    dma(tri.ap, c_tri, w=[tri])
    op("dve", "tensor_copy", identb.ap, ident.ap, r=[ident], w=[identb])
    op("pool", "memset", ones_b.ap, 1.0 / 1024, w=[ones_b])
    op("pool", "memset", ones_f.ap, 1.0, w=[ones_f])
    op("pool", "memset", epsc.ap, EPS, w=[epsc])

    def load_colvecs(dst, rows):
        st = A([16, 1024], F32)
        for i, rr in enumerate(rows):
            dma(st.ap[i:i + 1, :], rr, w=[st])
        for kc in range(8):
            ps = kb.psum()
            op("pe", "transpose", ps.ap[:, 0:len(rows)], st.ap[0:len(rows), kc * 128:(kc + 1) * 128],
               ident.ap[0:len(rows), 0:len(rows)], r=[st, ident], w=[ps])
            op("dve", "tensor_copy", dst.ap[:, kc, 0:len(rows)], ps.ap[:, 0:len(rows)], r=[ps], w=[dst])

    load_colvecs(gam, [norm_mix[0:1, :], norm_mix[1:2, :], norm_mlp[0:1, :], norm_mlp[1:2, :], norm_final])
    load_colvecs(lrup, [lru_conv_w[j:j + 1, :] for j in range(4)] + [lru_conv_b, lru_b_r, lru_b_i, lru_lambda])
    kb.reset_arena()

    def norm_group(xt, n, gidx, hn_out, sq, rstd, out_dtype_f32=False):
        op("act", "activation", out=sq.ap[:, :, 0:n], in_=xt.ap[:, :, 0:n], func=AF.Square, r=[xt], w=[sq])
        ps = kb.psum()
        mm(ps, [(ones_b.ap, sq.ap[:, kc, 0:n]) for kc in range(8)], r=[ones_b, sq], out=ps.ap[:, 0:n])
        op("act", "activation", out=rstd.ap[:, 0:n], in_=ps.ap[:, 0:n], func=AF.Sqrt, bias=epsc.ap, r=[ps, epsc], w=[rstd])
        op("dve", "reciprocal", rstd.ap[:, 0:n], rstd.ap[:, 0:n], r=[rstd], w=[rstd])
        for kc in range(8):
            op("dve", "scalar_tensor_tensor", hn_out(kc), xt.ap[:, kc, 0:n], gam.ap[:, kc, gidx:gidx + 1],
               rstd.ap[:, 0:n], ALU.mult, ALU.mult, r=[xt, gam, rstd], w=[hn_out.tl])

    class HnOut:
        def __init__(self, tl, fn):
            self.tl, self.fn = tl, fn

        def __call__(self, kc):
            return self.fn(kc)

    EMBED_FUSED = ("A" in kb.phases) and ("C" in kb.phases)
    if "A" in kb.phases and not EMBED_FUSED:
        xin = [A([128, 4, D], F32) for _ in range(2)]
        stg = [A([128, 8, 512], F32) for _ in range(2)]
        for gi, (p0, n) in enumerate(TG):
            xi, sg = xin[gi % 2], stg[gi % 2]
            nt = max(1, n // 128)
            if gi == 0:
                dma(xi.ap[0:16, 0, :], meta, w=[xi])
            else:
                r0 = p0 - NM
                dma(xi.ap, x[r0:r0 + 512, :].rearrange("(j p) d -> p j d", p=128), w=[xi])
            for kc in range(8):
                ps = kb.psum()
                for j in range(nt):
                    m = min(n, 128)
                    op("pe", "transpose", ps.ap[:, j * 128:j * 128 + m], xi.ap[0:m, j, kc * 128:(kc + 1) * 128],
                       ident.ap[0:m, 0:m], r=[xi, ident], w=[ps], inc=(j == nt - 1))
                if kc % 2 == 0:
                    op("act", "activation", out=sg.ap[:, kc, 0:n], in_=ps.ap[:, 0:n], func=AF.Copy, r=[ps], w=[sg])
                else:
                    op("dve", "tensor_copy", sg.ap[:, kc, 0:n], ps.ap[:, 0:n], r=[ps], w=[sg])
            dma(fm(hT[0])[:, :, p0:p0 + n], sg.ap[:, :, 0:n], r=[sg], w=[DB["hT0"]])
        kb.reset_arena()

    def norm_full(src_name, gidx, embed=False):
        hn = A([128, 8, T], BF16)
        hnb = [Buf() for _ in TG]
        mark = kb.aoff
        xts = [A([128, 8, 512], F32) for _ in range(2)]
        sqs = [A([128, 8, 512], BF16) for _ in range(2)]
        rs = [A([128, 512], F32) for _ in range(2)]
        xin = [A([128, 4, D], F32) for _ in range(2)] if embed else None
        src = fm(kb.dram[src_name])
        for gi, (p0, n) in enumerate(TG):
            xt = xts[gi % 2]
            if embed:
                xi = xin[gi % 2]
                nt = max(1, n // 128)
                m = min(n, 128)
                if gi == 0:
                    dma(xi.ap[0:16, 0, :], meta, w=[xi])
                else:
                    r0 = p0 - NM
                    dma(xi.ap, x[r0:r0 + 512, :].rearrange("(j p) d -> p j d", p=128), w=[xi])
                for kc in range(8):
                    ps = kb.psum()
                    for j in range(nt):
                        op("pe", "transpose", ps.ap[:, j * 128:j * 128 + m], xi.ap[0:m, j, kc * 128:(kc + 1) * 128],
                           ident.ap[0:m, 0:m], r=[xi, ident], w=[ps], inc=(j == nt - 1))
                    if kc % 2 == 0:
                        op("act", "activation", out=xt.ap[:, kc, 0:n], in_=ps.ap[:, 0:n], func=AF.Copy, r=[ps], w=[xt])
                    else:
                        op("dve", "tensor_copy", xt.ap[:, kc, 0:n], ps.ap[:, 0:n], r=[ps], w=[xt])
                dma(src[:, :, p0:p0 + n], xt.ap[:, :, 0:n], r=[xt], w=[DB[src_name]])
            else:
                dma(xt.ap[:, :, 0:n], src[:, :, p0:p0 + n], r=[DB[src_name]], w=[xt])
            ho = HnOut(Tl(hn.ap, hnb[gi]), lambda kc, p0=p0, n=n: hn.ap[:, kc, p0:p0 + n])
            norm_group(xt, n, gidx, ho, sqs[gi % 2], rs[gi % 2])
        S.barrier()
        kb.aoff = mark
        return hn, hnb

    if "C" in kb.phases:
        hn, hnb = norm_full("hT0", 0, embed=EMBED_FUSED)
        wst = [A([128, 8, 512], F32) for _ in range(2)]
        wbf = [A([128, 8, 512], BF16) for _ in range(2)]
        ostg = [A([128, 4, 512], F32) for _ in range(2)]
        ostg_b = [Tl(o.ap.rearrange("p a b -> p (a b)").bitcast(BF16)[:, 0:2048].rearrange("p (a b) -> p a b", a=4), o.b) for o in ostg]
        tstg = [A([128, 512], BF16) for _ in range(3)]
        win = ab_w_in.rearrange("(c p) n -> p c n", p=128)
        pcast = gen_precast(0)
        jobs = []
        for name, c0 in (("q", 0), ("k", 1024), ("v", 2048), ("o", 3072), ("xb", 4104), ("gate", 5128)):
            for hb in range(2):
                jobs.append((name, c0 + 512 * hb, 512, hb))
        jobs.append(("gates", 4096, 8, 0))
        cnt = 0
        tcnt = 0
        for ji, (name, c0, ncol, hb) in enumerate(jobs):
            ws, wb = wst[ji % 2], wbf[ji % 2]
            dma(ws.ap[:, :, 0:ncol], win[:, :, c0:c0 + ncol], w=[ws])
            kb.cast(ji, wb.ap[:, :, 0:ncol], ws.ap[:, :, 0:ncol], [ws], [wb])
            if name == "gates":
                gstgs = [A([4, 2, 512], F32) for _ in range(2)]
                for gi, (p0, n) in enumerate(TG):
                    gstg = gstgs[gi % 2]
                    for half in range(2):
                        ps = kb.psum()
                        mm(ps, [(wb.ap[:, kc, 4 * half:4 * half + 4], hn.ap[:, kc, p0:p0 + n]) for kc in range(8)],
                           r=[wb, hnb[gi]], out=ps.ap[0:4, 0:n])
                        op("dve", "tensor_copy", gstg.ap[:, half, 0:n], ps.ap[0:4, 0:n], r=[ps], w=[gstg])
                    dma(giT[:, p0:p0 + n], gstg.ap[:, 0, 0:n], r=[gstg], w=[DB["giT"]])
                    dma(gfT[:, p0:p0 + n], gstg.ap[:, 1, 0:n], r=[gstg], w=[DB["gfT"]])
                continue
            if name in ("q", "k", "xb", "gate"):
                dst_name = {"q": "qT", "k": "kT", "xb": "xbT", "gate": "ggT"}[name]
                isb = name in ("q", "k")
                for gi, (p0, n) in enumerate(TG):
                    next(pcast, None)
                    og = (ostg_b if isb else ostg)[cnt % 2]
                    cnt += 1
                    for oc in range(4):
                        ps = kb.psum()
                        mm(ps, [(wb.ap[:, kc, oc * 128:(oc + 1) * 128], hn.ap[:, kc, p0:p0 + n]) for kc in range(8)],
                           r=[wb, hnb[gi]], out=ps.ap[:, 0:n])
                        if name == "gate":
                            op("act", "activation", out=og.ap[:, oc, 0:n], in_=ps.ap[:, 0:n], func=AF.Gelu_apprx_tanh, r=[ps], w=[og])
                        elif oc % 2 == 0:
                            op("act", "activation", out=og.ap[:, oc, 0:n], in_=ps.ap[:, 0:n], func=AF.Copy, r=[ps], w=[og])
                        else:
                            op("dve", "tensor_copy", og.ap[:, oc, 0:n], ps.ap[:, 0:n], r=[ps], w=[og])
                    dma(fm(kb.dram[dst_name])[:, 4 * hb:4 * hb + 4, p0:p0 + n], og.ap[:, :, 0:n], r=[og], w=[DB[dst_name]])
            if name in ("k", "v", "o"):
                dst_name = {"k": "ktok", "v": "vtok", "o": "otok"}[name]
                for ti, (p0, n) in enumerate(TT):
                    if ti % 4 == 0:
                        next(pcast, None)
                    tg_i = 0 if ti == 0 else 1 + (ti - 1) // 4
                    ts_ = tstg[tcnt % 3]
                    tcnt += 1
                    ps = kb.psum()
                    mm(ps, [(hn.ap[:, kc, p0:p0 + n], wb.ap[:, kc, 0:512]) for kc in range(8)],
                       r=[wb, hnb[tg_i]], out=ps.ap[0:n, :])
                    if name == "o":
                        op("act", "activation", out=ts_.ap[0:n, :], in_=ps.ap[0:n, :], func=AF.Sigmoid, r=[ps], w=[ts_])
                    elif ti % 2 == 0:
                        op("act", "activation", out=ts_.ap[0:n, :], in_=ps.ap[0:n, :], func=AF.Copy, r=[ps], w=[ts_])
                    else:
                        op("dve", "tensor_copy", ts_.ap[0:n, :], ps.ap[0:n, :], r=[ps], w=[ts_])
                    dma(kb.dram[dst_name][p0:p0 + n, 512 * hb:512 * hb + 512], ts_.ap[0:n, :], r=[ts_], w=[DB[dst_name]])
        for _ in pcast:
            pass
        kb.reset_arena()

    NB = 33
    if "D" in kb.phases:
        gi_ = A([4, T], F32); gf_ = A([4, T], F32)
        t2 = A([4, T], F32); t3 = A([4, T], F32); Bc = A([4, T], F32)
        onesr = A([4, 1], F32)
        bi = A([4, 1], F32); bfb = A([4, 1], F32)
        Rb = A([4, NB + 1], F32)
        dec = A([4, NB], F32); decd = A([4, 4, NB], F32)
        gout = A([128, NB * 8 + NB * 4], F32)
        dma(gi_.ap, giT, r=[DB["giT"]], w=[gi_])
        dma(gf_.ap, gfT, r=[DB["gfT"]], w=[gf_])
        dma(bi.ap, ab_if_bias[0:4, :], w=[bi])
        dma(bfb.ap, ab_if_bias[4:8, :], w=[bfb])
        op("pool", "memset", onesr.ap, 1.0, w=[onesr])
        op("pool", "memset", Rb.ap, 0.0, w=[Rb])
        op("pool", "memset", gout.ap, 0.0, w=[gout])
        op("dve", "tensor_scalar", gf_.ap, gf_.ap, bfb.ap, None, ALU.add, r=[gf_, bfb], w=[gf_])
        op("act", "activation", out=t2.ap, in_=gf_.ap, func=AF.Abs, r=[gf_], w=[t2])
        op("act", "activation", out=t2.ap, in_=t2.ap, func=AF.Exp, scale=-1.0, r=[t2], w=[t2])
        op("act", "activation", out=t2.ap, in_=t2.ap, func=AF.Ln, bias=1.0, r=[t2], w=[t2])
        op("dve", "tensor_scalar", t3.ap, gf_.ap, 0.0, None, ALU.min, r=[gf_], w=[t3])
        op("dve", "tensor_tensor", t3.ap, t3.ap, t2.ap, ALU.subtract, r=[t3, t2], w=[t3])
        op("dve", "tensor_tensor_scan", Bc.ap, onesr.ap.to_broadcast([4, T]), t3.ap, 0.0, ALU.mult, ALU.add, r=[onesr, t3], w=[Bc])
        G = gi_; Mx = t2; Rt = t3; beta = gf_; flo = Bc
        op("dve", "scalar_tensor_tensor", G.ap, gi_.ap, bi.ap, Bc.ap, ALU.add, ALU.subtract, r=[gi_, bi, Bc], w=[G])
        op("dve", "tensor_tensor_scan", Mx.ap, G.ap, G.ap, 0.0, ALU.max, ALU.max, r=[G], w=[Mx])
        op("dve", "tensor_copy", Rb.ap[:, 1:2], Mx.ap[:, 15:16], r=[Mx], w=[Rb])
        op("dve", "tensor_copy", Rb.ap[:, 2:NB + 1], Mx.ap[:, 16:T].rearrange("p (b s) -> p b s", s=128)[:, :, 127], r=[Mx], w=[Rb])
        op("dve", "tensor_copy", Rt.ap[:, 0:16], Rb.ap[:, 1:2].to_broadcast([4, 16]), r=[Rb], w=[Rt])
        op("dve", "tensor_copy", Rt.ap[:, 16:T].rearrange("p (b s) -> p b s", s=128),
           Rb.ap[:, 2:NB + 1].unsqueeze(2).to_broadcast([4, 32, 128]), r=[Rb], w=[Rt])
        op("dve", "tensor_tensor", beta.ap, G.ap, Rt.ap, ALU.subtract, r=[G, Rt], w=[beta])
        op("act", "activation", out=beta.ap, in_=beta.ap, func=AF.Exp, r=[beta], w=[beta])
        op("dve", "tensor_scalar", beta.ap, beta.ap, 1.0 / 16, None, ALU.mult, r=[beta], w=[beta])
        op("dve", "tensor_tensor", flo.ap, Bc.ap, Rt.ap, ALU.add, r=[Bc, Rt], w=[flo])
        op("act", "activation", out=flo.ap, in_=flo.ap, func=AF.Exp, scale=-1.0, r=[flo], w=[flo])
        op("dve", "tensor_tensor", dec.ap, Rb.ap[:, 0:NB], Rb.ap[:, 1:NB + 1], ALU.subtract, r=[Rb], w=[dec])
        op("act", "activation", out=dec.ap, in_=dec.ap, func=AF.Exp, r=[dec], w=[dec])
        op("dve", "tensor_tensor", decd.ap, dec.ap.unsqueeze(1).to_broadcast([4, 4, NB]),
           ident.ap[0:4, 0:4].unsqueeze(2).to_broadcast([4, 4, NB]), ALU.mult, r=[dec, ident], w=[decd])
        ps = kb.psum()
        for ti, (p0, n) in enumerate(TT):
            op("pe", "transpose", ps.ap[0:n, ti * 8:ti * 8 + 4], beta.ap[:, p0:p0 + n], ident.ap[0:4, 0:4], r=[beta, ident], w=[ps], inc=False)
            op("pe", "transpose", ps.ap[0:n, ti * 8 + 4:ti * 8 + 8], flo.ap[:, p0:p0 + n], ident.ap[0:4, 0:4], r=[flo, ident], w=[ps], inc=(ti == NB - 1))
        op("dve", "tensor_copy", gout.ap[:, 8:NB * 8], ps.ap[:, 8:NB * 8], r=[ps], w=[gout])
        op("dve", "tensor_copy", gout.ap[0:16, 0:8], ps.ap[0:16, 0:8], r=[ps], w=[gout])
        ps2 = kb.psum()
        mm(ps2, [(ones_f.ap[0:4, :], decd.ap.rearrange("p a b -> p (a b)"))], r=[ones_f, decd], out=ps2.ap[:, 0:4 * NB])
        op("dve", "tensor_copy", gout.ap[:, NB * 8:NB * 12], ps2.ap[:, 0:4 * NB], r=[ps2], w=[gout])
        dma(gprep, gout.ap, r=[gout], w=[DB["gprep"]])
        kb.reset_arena()

    genE = genF = None
    if "E" in kb.phases:
        gp = A([128, NB * 12], F32)
        dma(gp.ap, gprep, r=[DB["gprep"]], w=[gp])
        bfv = gp.ap[:, 0:NB * 8].rearrange("p (b e) -> p b e", e=8)
        decv = gp.ap[:, NB * 8:NB * 12].rearrange("p (h b) -> p h b", h=4)
        gnb = A([128, D], F32)
        dma(gnb.ap, mlstm_norm.partition_broadcast(128), w=[gnb])
        Cst = A([128, 4, 2, 257], F32)
        Cd = A([128, 4, 2, 257], BF16)
        CstB = [Buf() for _ in range(4)]
        CdB = [Buf() for _ in range(4)]
        NBUF = 2
        qTb = [A([128, 8, 128], BF16) for _ in range(NBUF)]
        kTb = [A([128, 8, 128], BF16) for _ in range(NBUF)]
        ktb = [A([128, D], BF16) for _ in range(NBUF)]
        otb = [A([128, D], BF16) for _ in range(NBUF)]
        vab = [A([128, 4, 257], BF16) for _ in range(NBUF)]
        for v_ in vab:
            op("pool", "memset", v_.ap[:, :, 256:257], 1.0, w=[v_])
        Sm = [A([128, 128], BF16) for _ in range(4)]
        ktl = [A([128, 256], BF16) for _ in range(4)]
        ha = [A([128, 256], F32) for _ in range(2)]
        junk = A([128, 256], F32)
        sm1 = [A([128, 4], F32) for _ in range(2)]
        yst = [A([128, 8, 128], BF16) for _ in range(2)]
        yrow = fm(yTa)
        it = 0

        def loads(b):
            p0, n = TT[b]
            s = b % NBUF
            dma(qTb[s].ap[:, :, 0:n], fm(qT)[:, :, p0:p0 + n], r=[DB["qT"]], w=[qTb[s]])
            dma(kTb[s].ap[:, :, 0:n], fm(kT)[:, :, p0:p0 + n], r=[DB["kT"]], w=[kTb[s]])
            dma(ktb[s].ap[0:n, :], ktok[p0:p0 + n, :], r=[DB["ktok"]], w=[ktb[s]])
            dma(otb[s].ap[0:n, :], otok[p0:p0 + n, :], r=[DB["otok"]], w=[otb[s]])
            dma(vab[s].ap[0:n, :, 0:256], vtok[p0:p0 + n, :].rearrange("t (h v) -> t h v", h=4), r=[DB["vtok"]], w=[vab[s]])

        loads(0)
        han = [A([128, 256], BF16) for _ in range(4)]
        flat = [(b, h) for b in range(NB) for h in range(4)]
        pS_slots = [kb.ps[0]] * 4

        def s1(i):
            b, h = flat[i]
            p0, n = TT[b]
            s = b % NBUF
            i2 = i % 4
            beta_c = bfv[0:n, b, h:h + 1]
            pS = pS_slots[i % 4]
            c0 = (i % 4) * 128
            mm(pS, [(kTb[s].ap[:, 2 * h + dc, 0:n], qTb[s].ap[:, 2 * h + dc, 0:n]) for dc in range(2)],
               r=[kTb[s], qTb[s]], out=pS.ap[0:n, c0:c0 + n])
            op("dve", "scalar_tensor_tensor", Sm[i2].ap[0:n, 0:n], pS.ap[0:n, c0:c0 + n], beta_c, tri.ap[0:n, 0:n],
               ALU.mult, ALU.mult, r=[pS, gp, tri], w=[Sm[i2]])
            op("act", "activation", out=ktl[i2].ap[0:n, :], in_=ktb[s].ap[0:n, h * 256:(h + 1) * 256], func=AF.Copy,
               scale=beta_c, r=[ktb[s], gp], w=[ktl[i2]])

        def s2(i):
            b, h = flat[i]
            p0, n = TT[b]
            s = b % NBUF
            i2 = i % 2
            i4 = i % 4
            cstT = Tl(Cst.ap, CstB[h]); cdT = Tl(Cd.ap, CdB[h])
            floor_c = bfv[0:n, b, 4 + h:5 + h]
            pN = kb.ps[1 + i % 2]
            pairs = []
            if b > 0:
                pairs += [(qTb[s].ap[:, 2 * h + dc, 0:n], Cd.ap[:, h, dc, :]) for dc in range(2)]
            pairs.append((Sm[i4].ap[0:n, 0:n], vab[s].ap[0:n, h, :]))
            mm(pN, pairs, r=[qTb[s], cdT, Sm[i4], vab[s]], out=pN.ap[0:n, 0:257])
            pC = [kb.ps[3 + (i % 2) * 2 + dc] for dc in range(2)]
            for dc in range(2):
                mm(pC[dc], [(ktl[i4].ap[0:n, dc * 128:(dc + 1) * 128], vab[s].ap[0:n, h, :])], r=[ktl[i4], vab[s]],
                   out=pC[dc].ap[:, 0:257])
            yield
            for dc in range(2):
                if b == 0:
                    op("dve", "tensor_copy", Cst.ap[:, h, dc, :], pC[dc].ap[:, 0:257], r=[pC[dc]], w=[cstT])
                else:
                    op("dve", "scalar_tensor_tensor", Cst.ap[:, h, dc, :], Cst.ap[:, h, dc, :], decv[:, h, b:b + 1],
                       pC[dc].ap[:, 0:257], ALU.mult, ALU.add, r=[pC[dc], gp, cstT], w=[cstT])
            if b + 1 < NB:
                op("act", "activation", out=Cd.ap[:, h, :, :], in_=Cst.ap[:, h, :, :], func=AF.Copy,
                   scale=decv[:, h, b + 1:b + 2], r=[cstT, gp], w=[cdT])
            sm = sm1[i2]
            yield
            op("act", "activation", out=sm.ap[0:n, 0:1], in_=pN.ap[0:n, 256:257], func=AF.Abs, r=[pN], w=[sm])
            yield
            op("dve", "tensor_scalar", sm.ap[0:n, 0:1], sm.ap[0:n, 0:1], floor_c, None, ALU.max, r=[sm, gp], w=[sm])
            op("dve", "reciprocal", sm.ap[0:n, 0:1], sm.ap[0:n, 0:1], r=[sm], w=[sm])
            yield
            op("dve", "scalar_tensor_tensor", ha[i2].ap[0:n, :], pN.ap[0:n, 0:256], sm.ap[0:n, 0:1],
               otb[s].ap[0:n, h * 256:(h + 1) * 256], ALU.mult, ALU.mult, r=[pN, sm, otb[s]], w=[ha[i2]])
            yield
            op("act", "activation", out=junk.ap[0:n, :], in_=ha[i2].ap[0:n, :], func=AF.Square, accum_out=sm.ap[0:n, 1:2],
               r=[ha[i2]], w=[sm])
            op("act", "activation", out=sm.ap[0:n, 2:3], in_=sm.ap[0:n, 1:2], func=AF.Sqrt, scale=1.0 / 256, bias=epsc.ap[0:n, :],
               r=[sm, epsc], w=[sm])
            yield
            op("dve", "reciprocal", sm.ap[0:n, 2:3], sm.ap[0:n, 2:3], r=[sm], w=[sm])
            hn_ = han[i % 4]
            op("dve", "scalar_tensor_tensor", hn_.ap[0:n, :], ha[i2].ap[0:n, :], sm.ap[0:n, 2:3],
               gnb.ap[0:n, h * 256:(h + 1) * 256], ALU.mult, ALU.mult, r=[ha[i2], sm, gnb], w=[hn_])

        def s4(i):
            b, h = flat[i]
            p0, n = TT[b]
            ys = yst[b % 2]
            hn_ = han[i % 4]
            pT = kb.ps[7]
            pTb = pT.ap.bitcast(BF16)
            for vc in range(2):
                op("pe", "transpose", pTb[:, vc * 128:vc * 128 + n], hn_.ap[0:n, vc * 128:(vc + 1) * 128],
                   identb.ap[0:n, 0:n], r=[hn_, identb], w=[pT], inc=(vc == 1))
            op("act", "activation", out=ys.ap[:, 2 * h:2 * h + 2, 0:n],
               in_=pTb[:, 0:256].rearrange("p (a b) -> p a b", a=2)[:, :, 0:n], func=AF.Copy, r=[pT], w=[ys])
            if h == 3:
                dma(yrow[:, 0:8, p0:p0 + n], ys.ap[:, :, 0:n], r=[ys], w=[DB["yTa"]])

        NF = len(flat)

        def genE_():
            s1(0)
            s1(1)
            for i0_ in range(0, NF, 2):
                b, h = flat[i0_]
                if h == 0 and b + 1 < NB:
                    loads(b + 1)
                for j in (i0_ + 2, i0_ + 3):
                    if j < NF:
                        s1(j)
                gs = [s2(i0_), s2(i0_ + 1)]
                while gs:
                    for g_ in list(gs):
                        try:
                            next(g_)
                        except StopIteration:
                            gs.remove(g_)
                for j in (i0_ - 2, i0_ - 1):
                    if j >= 0:
                        s4(j)
                yield
            s4(NF - 2)
            s4(NF - 1)
            yield
        genE = genE_()

    if "F" in kb.phases:
        t8 = A([128, 8], F32); t8b = A([128, 8], F32)
        lam_v = lrup.ap[:, :, 7]
        op("act", "activation", out=t8.ap, in_=lam_v, func=AF.Abs, r=[lrup], w=[t8])
        op("act", "activation", out=t8.ap, in_=t8.ap, func=AF.Exp, scale=-1.0, r=[t8], w=[t8])
        op("act", "activation", out=t8.ap, in_=t8.ap, func=AF.Ln, bias=1.0, r=[t8], w=[t8])
        op("dve", "tensor_scalar", t8b.ap, lam_v, 0.0, None, ALU.min, r=[lrup], w=[t8b])
        op("dve", "tensor_tensor", t8b.ap, t8b.ap, t8.ap, ALU.subtract, r=[t8, t8b], w=[t8b])
        op("dve", "tensor_scalar", clam.ap, t8b.ap, 8.0, None, ALU.mult, r=[t8b], w=[clam])
        wri_s = A([128, 16, 128], F32)
        wri = A([128, 16, 128], BF16)
        dma(wri_s.ap[:, 0:8, :], lru_w_r.rearrange("n d e -> d n e"), w=[wri_s])
        dma(wri_s.ap[:, 8:16, :], lru_w_i.rearrange("n d e -> d n e"), w=[wri_s])
        op("pool", "tensor_copy", wri.ap, wri_s.ap, r=[wri_s], w=[wri])
        xb = A([128, T], F32); gg = A([128, T], F32)
        xc = A([128, T], F32); xcb = A([128, T], BF16)
        r_ = A([128, T], F32); i_ = A([128, T], F32); s_ = A([128, T], F32)
        hb_ = [A([128, T], BF16) for _ in range(2)]

        clam2 = A([128, 8], F32)
        op("dve", "tensor_scalar", clam2.ap, clam.ap, 2.0, None, ALU.mult, r=[clam], w=[clam2])
        NP = len(TG)
        pb = {nm: [Buf() for _ in range(NP)] for nm in ("xc", "xcb", "r", "i", "s", "hb0", "hb1")}

        def genF_():
            dma(xb.ap, xbT[0:128, :], r=[DB["xbT"]], w=[xb])
            dma(gg.ap, ggT[0:128, :], r=[DB["ggT"]], w=[gg])

            def tl(n_, gi):
                hb = hb_[n_ % 2]
                return (Tl(xc.ap, pb["xc"][gi]), Tl(xcb.ap, pb["xcb"][gi]), Tl(r_.ap, pb["r"][gi]),
                        Tl(i_.ap, pb["i"][gi]), Tl(s_.ap, pb["s"][gi]), Tl(hb.ap, pb["hb%d" % (n_ % 2)][gi]))

            def stage1(n_, gi):
                p0, n = TG[gi]
                pv = lambda i: lrup.ap[:, n_, i:i + 1]
                sl = slice(p0, p0 + n)
                xcT, xcbT, rT, iT, sT, hbT = tl(n_, gi)
                op("dve", "tensor_scalar", xc.ap[:, sl], xb.ap[:, sl], pv(3), pv(4), ALU.mult, ALU.add, r=[xb, lrup], w=[xcT])
                for j in (1, 2, 3):
                    lo = max(p0, j)
                    op("dve", "scalar_tensor_tensor", xc.ap[:, lo:p0 + n], xb.ap[:, lo - j:p0 + n - j], pv(3 - j), xc.ap[:, lo:p0 + n],
                       ALU.mult, ALU.add, r=[xb, lrup, xcT], w=[xcT])
                op("act", "activation", out=xcb.ap[:, sl], in_=xc.ap[:, sl], func=AF.Copy, r=[xcT], w=[xcbT])
                for which, dst, dT, bidx in ((0, r_, rT, 5), (1, i_, iT, 6)):
                    ps = kb.psum_from("F", [5, 6])
                    mm(ps, [(wri.ap[:, which * 8 + n_, :], xcb.ap[:, sl])], r=[wri, xcbT], out=ps.ap[:, 0:n])
                    op("act", "activation", out=dst.ap[:, sl], in_=ps.ap[:, 0:n], func=AF.Sigmoid, bias=pv(bidx),
                       r=[ps, lrup], w=[dT])

            def stage4(n_, gi):
                p0, n = TG[gi]
                sl = slice(p0, p0 + n)
                hb = hb_[n_ % 2]
                xcT, xcbT, rT, iT, sT, hbT = tl(n_, gi)
                op("pool", "tensor_tensor", i_.ap[:, sl], i_.ap[:, sl], xc.ap[:, sl], ALU.mult, r=[iT, xcT], w=[iT])
                op("pool", "tensor_tensor", i_.ap[:, sl], i_.ap[:, sl], s_.ap[:, sl], ALU.mult, r=[iT, sT], w=[iT])
                init = 0.0 if gi == 0 else s_.ap[:, p0 - 1:p0]
                rd = [rT, iT] + ([Tl(s_.ap, pb["s"][gi - 1])] if gi > 0 else [])
                op("dve", "tensor_tensor_scan", s_.ap[:, sl], r_.ap[:, sl], i_.ap[:, sl], init, ALU.mult, ALU.add, r=rd, w=[sT])
                op("pool", "tensor_tensor", hb.ap[:, sl], s_.ap[:, sl], gg.ap[:, sl], ALU.mult, r=[sT, gg], w=[hbT])

            for gi in range(NP):
                stage1(0, gi)
            yield
            for n_ in range(8):
                if n_ + 1 < 8:
                    dma(xb.ap, xbT[(n_ + 1) * 128:(n_ + 2) * 128, :], r=[DB["xbT"]], w=[xb])
                for gi, (p0, n) in enumerate(TG):
                    sl = slice(p0, p0 + n)
                    xcT, xcbT, rT, iT, sT, hbT = tl(n_, gi)
                    op("act", "activation", out=s_.ap[:, sl], in_=r_.ap[:, sl], func=AF.Exp, scale=clam2.ap[:, n_:n_ + 1], r=[rT, clam2], w=[sT])
                    op("act", "activation", out=r_.ap[:, sl], in_=r_.ap[:, sl], func=AF.Exp, scale=clam.ap[:, n_:n_ + 1], r=[rT, clam], w=[rT])
                for gi, (p0, n) in enumerate(TG):
                    sl = slice(p0, p0 + n)
                    xcT, xcbT, rT, iT, sT, hbT = tl(n_, gi)
                    op("act", "activation", out=s_.ap[:, sl], in_=s_.ap[:, sl], func=AF.Sqrt, scale=-1.0, bias=1.0, r=[sT], w=[sT])
                yield
                for gi in range(NP):
                    stage4(n_, gi)
                    if n_ + 1 < 8 and gi >= 1:
                        stage1(n_ + 1, gi - 1)
                if n_ + 1 < 8:
                    stage1(n_ + 1, NP - 1)
                    dma(gg.ap, ggT[(n_ + 1) * 128:(n_ + 2) * 128, :], r=[DB["ggT"]], w=[gg])
                dma(yTb[n_ * 128:(n_ + 1) * 128, :], hb_[n_ % 2].ap, r=pb["hb%d" % (n_ % 2)], w=[DB["yTb"]])
                yield
        genF = genF_()

    if genE is not None or genF is not None:
        for g_ in (genE, genF):
            if g_ is not None:
                for _ in g_:
                    pass
        kb.reset_arena()

    def out_proj(y_names, KC, w_dram, src_name, dst_name):
        wb = A([128, KC, D], BF16)
        wst = [A([128, 4, D], F32) for _ in range(2)]
        wv = w_dram.rearrange("(c p) n -> p c n", p=128)
        for c4 in range(KC // 4):
            ws = wst[c4 % 2]
            dma(ws.ap, wv[:, 4 * c4:4 * c4 + 4, :], w=[ws])
            kb.cast(c4, wb.ap[:, 4 * c4:4 * c4 + 4, :], ws.ap, [ws], [wb])
        yb = [A([128, KC, 512], BF16) for _ in range(2)]
        xt = [A([128, 8, 512], F32) for _ in range(2)]
        hs = fm(kb.dram[src_name]); hd = fm(kb.dram[dst_name])

        def ld(gi):
            p0, n = TG[gi]
            for yi, yn in enumerate(y_names):
                dma(yb[gi % 2].ap[:, 8 * yi:8 * yi + 8, 0:n], fm(kb.dram[yn])[:, :, p0:p0 + n], r=[DB[yn]], w=[yb[gi % 2]])
            dma(xt[gi % 2].ap[:, :, 0:n], hs[:, :, p0:p0 + n], r=[DB[src_name]], w=[xt[gi % 2]])

        ld(0)
        for gi, (p0, n) in enumerate(TG):
            if gi + 1 < len(TG):
                ld(gi + 1)
            y_, x_ = yb[gi % 2], xt[gi % 2]
            for oc in range(8):
                ps = kb.psum()
                mm(ps, [(wb.ap[:, kc, oc * 128:(oc + 1) * 128], y_.ap[:, kc, 0:n]) for kc in range(KC)], r=[wb, y_], out=ps.ap[:, 0:n])
                op("dve", "tensor_tensor", x_.ap[:, oc, 0:n], ps.ap[:, 0:n], x_.ap[:, oc, 0:n], ALU.add, r=[ps, x_], w=[x_])
            dma(hd[:, :, p0:p0 + n], x_.ap[:, :, 0:n], r=[x_], w=[DB[dst_name]])
        kb.reset_arena()

    if "G" in kb.phases:
        out_proj(["yTa", "yTb"], 16, ab_w_out, "hT0", "hT1")

    def mlp(layer, src_name, dst_name):
        w1b = A([128, 8, 4096], BF16)
        w2b = A([128, 32, D], BF16)
        wst = [A([128, 2048], F32) for _ in range(2)]
        w1v = mlp_w1[layer].rearrange("(c p) n -> p c n", p=128)
        w2v = mlp_w2[layer].rearrange("(c p) n -> p c n", p=128)
        w1B = [Buf() for _ in range(8)]
        w2B = [Buf() for _ in range(16)]
        pre = ("C" if layer == 0 else "J0") in kb.phases
        if pre:
            n1, n2 = "w1s%d" % layer, "w2s%d" % layer
            s1v = w1s[layer].rearrange("(c p) n -> p c n", p=128)
            s2v = w2s[layer].rearrange("(c p) n -> p c n", p=128)
            for kc in range(8):
                dma(w1b.ap[:, kc, :], s1v[:, kc, :], r=[DB[n1]], w=[w1B[kc]])
            for c2 in range(8):
                dma(w2b.ap[:, 4 * c2:4 * c2 + 4, :], s2v[:, 4 * c2:4 * c2 + 4, :], r=[DB[n2]], w=[w2B[2 * c2], w2B[2 * c2 + 1]])
        k = 0
        for kc in range(8 if not pre else 0):
            for hf in range(2):
                ws = wst[k % 2]
                dma(ws.ap, w1v[:, kc, hf * 2048:(hf + 1) * 2048], w=[ws])
                kb.cast(k, w1b.ap[:, kc, hf * 2048:(hf + 1) * 2048], ws.ap, [ws], [w1B[kc]])
                k += 1
        for c2 in range(16 if not pre else 0):
            ws = wst[k % 2]
            wsv = ws.ap.rearrange("p (a b) -> p a b", a=2)
            dma(wsv, w2v[:, 2 * c2:2 * c2 + 2, :], w=[ws])
            kb.cast(k, w2b.ap[:, 2 * c2:2 * c2 + 2, :], wsv, [ws], [w2B[c2]])
            k += 1
        NG = 256
        xts = [A([128, 8, NG], F32) for _ in range(2)]
        aT = A([128, 32, NG], BF16)
        sq = A([128, 8, NG], BF16)
        hns = [A([128, 8, NG], BF16) for _ in range(2)]
        rstds = [A([128, NG], F32) for _ in range(2)]
        tmp = [A([128, NG], F32) for _ in range(2)]
        hs = fm(kb.dram[src_name]); hd = fm(kb.dram[dst_name])
        gidx = 2 + layer
        NGR = len(TG2)

        def ld(gi):
            p0, n = TG2[gi]
            dma(xts[gi % 2].ap[:, :, 0:n], hs[:, :, p0:p0 + n], r=[DB[src_name]], w=[xts[gi % 2]])

        def nrm(gi):
            p0, n = TG2[gi]
            hn = hns[gi % 2]
            norm_group(xts[gi % 2], n, gidx, HnOut(hn, lambda kc, n=n, hn=hn: hn.ap[:, kc, 0:n]), sq, rstds[gi % 2])

        ld(0)
        ld(1)
        nrm(0)
        tc_ = 0
        for gi, (p0, n) in enumerate(TG2):
            xt = xts[gi % 2]
            hn = hns[gi % 2]
            for fc in range(32):
                ps = kb.psum()
                mm(ps, [(w1b.ap[:, kc, fc * 128:(fc + 1) * 128], hn.ap[:, kc, 0:n]) for kc in range(8)], r=w1B + [hn], out=ps.ap[:, 0:n])
                tm_ = tmp[tc_ % 2]
                op("act", "activation", out=tm_.ap[:, 0:n], in_=ps.ap[:, 0:n], func=AF.Relu, r=[ps], w=[tm_])
                op("pool" if tc_ % 2 == 0 else "dve", "tensor_tensor", aT.ap[:, fc, 0:n], tm_.ap[:, 0:n], tm_.ap[:, 0:n], ALU.mult, r=[tm_], w=[aT])
                tc_ += 1
            if gi + 1 < NGR:
                nrm(gi + 1)
            for oc in range(8):
                ps = kb.psum()
                mm(ps, [(w2b.ap[:, fc, oc * 128:(oc + 1) * 128], aT.ap[:, fc, 0:n]) for fc in range(32)], r=w2B + [aT], out=ps.ap[:, 0:n])
                op("dve", "tensor_tensor", xt.ap[:, oc, 0:n], ps.ap[:, 0:n], xt.ap[:, oc, 0:n], ALU.add, r=[ps, xt], w=[xt])
            dma(hd[:, :, p0:p0 + n], xt.ap[:, :, 0:n], r=[xt], w=[DB[dst_name]])
            if gi + 2 < NGR:
                ld(gi + 2)
        kb.reset_arena()

    if "H" in kb.phases:
        mlp(0, "hT1", "hT2")

    if "I" in kb.phases:
        hn, hnb = norm_full("hT2", 1)
        wb = A([128, 8, 3072], BF16)
        wst = [A([128, 2, 1024], F32) for _ in range(2)]
        cwv = c_w_in.rearrange("(c p) n -> p c n", p=128)
        wB = [Buf() for _ in range(12)]
        k = 0
        for cb in range(3):
            for c2 in range(4):
                ws = wst[k % 2]
                dma(ws.ap, cwv[:, 2 * c2:2 * c2 + 2, cb * 1024:(cb + 1) * 1024], w=[ws])
                kb.cast(k, wb.ap[:, 2 * c2:2 * c2 + 2, cb * 1024:(cb + 1) * 1024], ws.ap, [ws], [wB[k]])
                k += 1
        cosT = A([128, 33, 8], F32); sinT = A([128, 33, 8], F32)
        dma(cosT.ap, c_cos, w=[cosT]); dma(sinT.ap, c_sin, w=[sinT])
        qk = [A([128, 2048], F32) for _ in range(2)]
        qkb = [A([128, 2048], BF16) for _ in range(2)]
        rt = [A([128, 32, 8], F32) for _ in range(4)]
        vst = [A([128, 1024], BF16) for _ in range(2)]
        qst = [A([128, 16, 128], BF16) for _ in range(2)]
        sqq = [A([128, 2048], F32) for _ in range(1)]
        ssn = A([128, 32], F32)
        mrun = A([128, 32], F32)
        op("pool", "memset", mrun.ap, 0.0, w=[mrun])
        pend_tr = []
        for ti, (p0, n) in enumerate(TT):
            tg_i = 0 if ti == 0 else 1 + (ti - 1) // 4
            q_, qb_, v_, qs_ = qk[ti % 2], qkb[ti % 2], vst[ti % 2], qst[ti % 2]
            for blk in range(6):
                ps = kb.psum()
                mm(ps, [(hn.ap[:, kc, p0:p0 + n], wb.ap[:, kc, blk * 512:(blk + 1) * 512]) for kc in range(8)], r=wB + [hnb[tg_i]], out=ps.ap[0:n, :])
                if blk < 2:
                    op("act", "activation", out=q_.ap[0:n, blk * 512:(blk + 1) * 512], in_=ps.ap[0:n, :], func=AF.Copy, scale=0.125, r=[ps], w=[q_])
                elif blk < 4:
                    op("dve", "tensor_copy", q_.ap[0:n, blk * 512:(blk + 1) * 512], ps.ap[0:n, :], r=[ps], w=[q_])
                else:
                    op("act", "activation", out=v_.ap[0:n, (blk - 4) * 512:(blk - 3) * 512], in_=ps.ap[0:n, :], func=AF.Copy, r=[ps], w=[v_])
            dma(vtok2[p0:p0 + n, :], v_.ap[0:n, :], r=[v_], w=[DB["vtok2"]])
            qv = q_.ap.rearrange("p (g d) -> p g d", d=64)
            x1 = qv[0:n, :, 0:8]; x2 = qv[0:n, :, 8:16]
            cb_ = cosT.ap[0:n, ti, :].unsqueeze(1).to_broadcast([n, 32, 8])
            sb_ = sinT.ap[0:n, ti, :].unsqueeze(1).to_broadcast([n, 32, 8])
            a1, a2, a3, a4 = [t_.ap[0:n] for t_ in rt]
            op("pool", "tensor_tensor", a1, x1, cb_, ALU.mult, r=[q_, cosT], w=[rt[0]])
            op("pool", "tensor_tensor", a2, x2, sb_, ALU.mult, r=[q_, sinT], w=[rt[1]])
            op("dve", "tensor_tensor", a3, x2, cb_, ALU.mult, r=[q_, cosT], w=[rt[2]])
            op("dve", "tensor_tensor", a4, x1, sb_, ALU.mult, r=[q_, sinT], w=[rt[3]])
            op("pool", "tensor_tensor", x1, a1, a2, ALU.subtract, r=[rt[0], rt[1]], w=[q_])
            op("dve", "tensor_tensor", x2, a3, a4, ALU.add, r=[rt[2], rt[3]], w=[q_])
            op("act", "activation", out=qb_.ap[0:n, :], in_=q_.ap[0:n, :], func=AF.Copy, r=[q_], w=[qb_])
            op("act", "activation", out=sqq[0].ap[0:n, :], in_=q_.ap[0:n, :], func=AF.Square, r=[q_], w=[sqq[0]])
            op("dve", "tensor_reduce", ssn.ap[0:n, :], sqq[0].ap[0:n, :].rearrange("p (g d) -> p g d", d=64), AX.X, ALU.add, r=[sqq[0]], w=[ssn])
            op("dve", "tensor_tensor", mrun.ap[0:n, :], mrun.ap[0:n, :], ssn.ap[0:n, :], ALU.max, r=[mrun, ssn], w=[mrun])

            def trn(ti=ti, p0=p0, n=n, qb_=qb_, qs_=qs_):
                for c4 in range(4):
                    pT = kb.psum()
                    pTb = pT.ap.bitcast(BF16)
                    for j in range(4):
                        c = c4 * 4 + j
                        op("pe", "transpose", pTb[:, j * 128:j * 128 + n], qb_.ap[0:n, c * 128:(c + 1) * 128], identb.ap[0:n, 0:n],
                           r=[qb_, identb], w=[pT], inc=(j == 3))
                    src = pTb[:, 0:512].rearrange("p (a b) -> p a b", a=4)[:, :, 0:n]
                    if c4 % 2 == 0:
                        op("act", "activation", out=qs_.ap[:, 4 * c4:4 * c4 + 4, 0:n], in_=src, func=AF.Copy, r=[pT], w=[qs_])
                    else:
                        op("dve", "tensor_copy", qs_.ap[:, 4 * c4:4 * c4 + 4, 0:n], src, r=[pT], w=[qs_])
                dma(fm(qkT)[:, :, p0:p0 + n], qs_.ap[:, :, 0:n], r=[qs_], w=[DB["qkT"]])
            pend_tr.append(trn)
            if len(pend_tr) > 1:
                pend_tr.pop(0)()
        while pend_tr:
            pend_tr.pop(0)()
        pq = kb.psum()
        op("pe", "transpose", pq.ap[0:32, 0:128], mrun.ap, ident.ap, r=[mrun, ident], w=[pq])
        mcol = A([32, 1], F32)
        op("dve", "reduce_max", mcol.ap, pq.ap[0:32, 0:128], AX.X, r=[pq], w=[mcol])
        pq2 = kb.psum()
        op("pe", "transpose", pq2.ap[0:1, 0:32], mcol.ap, ident.ap[0:32, 0:32], r=[mcol, ident], w=[pq2])
        mrow = A([1, 32], F32)
        op("dve", "tensor_copy", mrow.ap, pq2.ap[0:1, 0:32], r=[pq2], w=[mrow])
        brow = A([1, 16], F32)
        op("dve", "tensor_tensor", brow.ap, mrow.ap[:, 0:16], mrow.ap[:, 16:32], ALU.mult, r=[mrow], w=[brow])
        op("act", "activation", out=brow.ap, in_=brow.ap, func=AF.Ln, scale=1.05, r=[brow], w=[brow])
        op("act", "activation", out=brow.ap, in_=brow.ap, func=AF.Exp, scale=0.5, r=[brow], w=[brow])
        op("dve", "tensor_scalar", brow.ap, brow.ap, -1.0, None, ALU.mult, r=[brow], w=[brow])
        pq3 = kb.psum()
        mm(pq3, [(ones_f.ap[0:1, :], brow.ap)], r=[ones_f, brow], out=pq3.ap[:, 0:16])
        op("dve", "tensor_copy", shift_all.ap, pq3.ap[:, 0:16], r=[pq3], w=[shift_all])
        kb.reset_arena()

    if "J0" in kb.phases:
        lv = A([1, 256], F32); l2 = A([1, 4], F32); lam_bc = A([128, 1], F32)
        dma(lv.ap, c_lambda, w=[lv])
        op("dve", "tensor_tensor", lv.ap[:, 0:64], lv.ap[:, 0:64], lv.ap[:, 64:128], ALU.mult, r=[lv], w=[lv])
        op("dve", "tensor_tensor", lv.ap[:, 128:192], lv.ap[:, 128:192], lv.ap[:, 192:256], ALU.mult, r=[lv], w=[lv])
        op("dve", "reduce_sum", l2.ap[:, 0:1], lv.ap[:, 0:64], AX.X, r=[lv], w=[l2])
        op("dve", "reduce_sum", l2.ap[:, 1:2], lv.ap[:, 128:192], AX.X, r=[lv], w=[l2])
        op("act", "activation", out=l2.ap[:, 0:2], in_=l2.ap[:, 0:2], func=AF.Exp, r=[l2], w=[l2])
        op("dve", "tensor_tensor", l2.ap[:, 2:3], l2.ap[:, 1:2], l2.ap[:, 0:1], ALU.subtract, r=[l2], w=[l2])
        op("dve", "tensor_scalar", l2.ap[:, 2:3], l2.ap[:, 2:3], -LAMBDA_INIT, None, ALU.add, r=[l2], w=[l2])
        psl = kb.psum()
        mm(psl, [(ones_f.ap[0:1, :], l2.ap[0:1, 2:3])], r=[ones_f, l2], out=psl.ap[:, 0:1])
        op("dve", "tensor_copy", lam_bc.ap, psl.ap[:, 0:1], r=[psl], w=[lam_bc])
        sln = A([128, 128], F32)
        dma(sln.ap, c_subln.partition_broadcast(128), w=[sln])
        op("dve", "tensor_scalar", sln.ap, sln.ap, 1.0 - LAMBDA_INIT, None, ALU.mult, r=[sln], w=[sln])
        sel = A([128, 2], BF16)
        op("pool", "memset", sel.ap, 0.0, w=[sel])
        op("pool", "memset", sel.ap[0:64, 0:1], 1.0, w=[sel])
        op("pool", "memset", sel.ap[64:128, 1:2], 1.0, w=[sel])
        qTh = [A([128, T], BF16) for _ in range(2)]
        kTh = [A([128, T], BF16) for _ in range(2)]
        Vh = [A([128, 33, 129], BF16) for _ in range(2)]
        for v_ in Vh:
            op("pool", "memset", v_.ap[:, :, 128:129], 1.0, w=[v_])
        sqt = A([128, T], BF16)
        ssq = A([2, 2, T], F32)
        mx = A([2, 4], F32)
        shift = A([128, 2], F32)
        mxd = A([2, 2], F32)
        PT = [A([128, 512], BF16) for _ in range(4)]
        gbuf = []
        for _ in range(2):
            gbuf.append({"Os": [A([128, 512], F32) for _ in range(2)], "rbc": [A([128, 512], F32) for _ in range(2)],
                         "sq": A([128, 512], BF16), "rstd": A([128, 512], F32), "on": A([128, 512], BF16)})
        accs = [A([128, 512], F32) for _ in range(2)]
        accbs = [A([128, 512], BF16) for _ in range(2)]
        onesq = A([128, 128], BF16)
        op("pool", "memset", onesq.ap, 1.0, w=[onesq])
        qz = [[A([128, T], BF16) for _ in range(2)] for _ in range(2)]
        for qq in qz:
            op("pool", "memset", qq[0].ap[64:128, :], 0.0, w=[qq[0]])
            op("pool", "memset", qq[1].ap[0:64, :], 0.0, w=[qq[1]])
        one1 = A([128, 1], BF16)
        op("pool", "memset", one1.ap, 1.0, w=[one1])
        o128 = A([128, 128], BF16)
        op("pool", "memset", o128.ap, 1.0 / 128, w=[o128])
        slc = A([128, 1], F32)
        dma(slc.ap, c_subln.rearrange("o (p u) -> p (o u)", u=1), w=[slc])
        op("dve", "tensor_scalar", slc.ap, slc.ap, 1.0 - LAMBDA_INIT, None, ALU.mult, r=[slc], w=[slc])
        qkv = fm(qkT)

        def aload(h):
            s = h % 2
            dma(qTh[s].ap, qkT[h * 128:(h + 1) * 128, :], r=[DB["qkT"]], w=[qTh[s]])
            dma(kTh[s].ap, qkT[1024 + h * 128:1024 + (h + 1) * 128, :], r=[DB["qkT"]], w=[kTh[s]])
            dma(qz[s][0].ap[0:64, :], qkT[h * 128:h * 128 + 64, :], r=[DB["qkT"]], w=[qz[s][0]])
            dma(qz[s][1].ap[64:128, :], qkT[h * 128 + 64:(h + 1) * 128, :], r=[DB["qkT"]], w=[qz[s][1]])
            dma(Vh[s].ap[0:16, 0, 0:128], vtok2[0:16, h * 128:(h + 1) * 128], r=[DB["vtok2"]], w=[Vh[s]])
            dma(Vh[s].ap[:, 1:33, 0:128], vtok2[16:T, h * 128:(h + 1) * 128].rearrange("(j p) c -> p j c", p=128), r=[DB["vtok2"]], w=[Vh[s]])

        aload(0)
        pcast = gen_precast(1)
        pti = 0
        oi = 0
        for h in range(8):
            if h + 1 < 8:
                aload(h + 1)
            s = h % 2
            qt, kt, vh = qTh[s], kTh[s], Vh[s]
            shift = Tl(shift_all.ap[:, 2 * h:2 * h + 2], shift_all.b)
            tasks = []
            for g in range(-1, 8):
                if g < 0:
                    q0, nq_tot, nblk = 0, 16, 1
                else:
                    q0, nq_tot, nblk = 16 + 512 * g, 512, 4
                for c in range(2):
                    kbl = [(0, 0, 16, 0, g < 0)]
                    if g >= 0:
                        for j in range(4 * g):
                            kbl.append((1 + j, 16 + 128 * j, 128, 0, False))
                        for i in range(4):
                            kbl.append((1 + 4 * g + i, 16 + 128 * (4 * g + i), 128, i, True))
                    for bi_, e in enumerate(kbl):
                        tasks.append(dict(g=g, c=c, q0=q0, nq_tot=nq_tot, nblk=nblk, kb=e, first=(bi_ == 0), last=(bi_ == len(kbl) - 1)))
            LA = 2
            cur = {}
            deferred = []

            def stage1(t):
                kt_i, k0, nk, qb0, masked = t["kb"]
                c = t["c"]
                qzc = qz[s][c]
                qs = t["q0"] + qb0 * 128
                ncol = t["nq_tot"] - qb0 * 128
                pS = kb.psum_from("pS", [4, 5, 6])
                mm(pS, [(kt.ap[:, k0:k0 + nk], qzc.ap[:, qs:qs + ncol])], r=[kt, qzc], out=pS.ap[0:nk, 0:ncol])
                pt = PT[cur.setdefault("pti", 0) % len(PT)]
                cur["pti"] += 1
                op("act", "activation", out=pt.ap[0:nk, 0:ncol], in_=pS.ap[0:nk, 0:ncol], func=AF.Exp, bias=shift.ap[0:nk, c:c + 1],
                   r=[pS, shift], w=[pt])
                if masked:
                    m = min(128, ncol)
                    op("pool", "tensor_tensor", pt.ap[0:nk, 0:m], pt.ap[0:nk, 0:m], tri.ap[0:nk, 0:m], ALU.mult, r=[pt, tri], w=[pt])
                t["pt"] = pt

            def stage2(t, idx):
                kt_i, k0, nk, qb0, masked = t["kb"]
                g, c, nq_tot, q0 = t["g"], t["c"], t["nq_tot"], t["q0"]
                pt = t["pt"]
                ncol = nq_tot - qb0 * 128
                qoff = qb0 * 128
                if t["first"]:
                    cur["pOT"] = kb.psum_from("pOT", [0, 1])
                    cur["pL"] = kb.psum_from("pL", [2, 3])
                    if c == 0:
                        cur["gp"] = cur.setdefault("gcount", 0) % 2
                        cur["gcount"] += 1
                pOT, pL = cur["pOT"], cur["pL"]
                op("pe", "matmul", pOT.ap[:, qoff:qoff + ncol], vh.ap[0:nk, kt_i, 0:128], pt.ap[0:nk, 0:ncol], start=t["first"], stop=t["last"],
                   r=[pt, vh], w=[pOT], inc=False)
                op("pe", "matmul", pL.ap[:, qoff:qoff + ncol], onesq.ap[0:nk, :], pt.ap[0:nk, 0:ncol], start=t["first"], stop=t["last"],
                   r=[pt, onesq], w=[pL], inc=True)
                if not t["last"]:
                    return
                B = gbuf[cur["gp"]]
                n_ = nq_tot
                Os0, rbc = B["Os"][0], B["rbc"][c]
                sq, rstd, onb_ = B["sq"], B["rstd"], B["on"]

                def stepA(n_=n_, Os0=Os0, rbc=rbc, pOT=pOT, pL=pL, c=c, sq=sq):
                    op("dve", "reciprocal", rbc.ap[:, 0:n_], pL.ap[:, 0:n_], r=[pL], w=[rbc])
                    if c == 0:
                        op("dve", "tensor_tensor", Os0.ap[:, 0:n_], pOT.ap[:, 0:n_], rbc.ap[:, 0:n_], ALU.mult, r=[pOT, rbc], w=[Os0])
                    else:
                        op("dve", "tensor_tensor", rbc.ap[:, 0:n_], pOT.ap[:, 0:n_], rbc.ap[:, 0:n_], ALU.mult, r=[pOT, rbc], w=[rbc])
                        op("dve", "scalar_tensor_tensor", Os0.ap[:, 0:n_], rbc.ap[:, 0:n_], lam_bc.ap[:, 0:1], Os0.ap[:, 0:n_], ALU.mult, ALU.add,
                           r=[rbc, lam_bc, Os0], w=[Os0])
                        op("pool", "tensor_tensor", sq.ap[:, 0:n_], Os0.ap[:, 0:n_], Os0.ap[:, 0:n_], ALU.mult, r=[Os0], w=[sq])

                deferred.append((idx + 2, stepA))
                if c == 1:
                    def stepC(n_=n_, Os0=Os0, sq=sq, rstd=rstd, onb_=onb_, q0=q0):
                        pSS = kb.ps[7]
                        mm(pSS, [(o128.ap, sq.ap[:, 0:n_])], r=[o128, sq], out=pSS.ap[:, 0:n_])
                        op("act", "activation", out=rstd.ap[:, 0:n_], in_=pSS.ap[:, 0:n_], func=AF.Ln, bias=epsc.ap, r=[pSS, epsc], w=[rstd])
                        op("act", "activation", out=rstd.ap[:, 0:n_], in_=rstd.ap[:, 0:n_], func=AF.Exp, scale=-0.5, r=[rstd], w=[rstd])
                        op("dve", "scalar_tensor_tensor", onb_.ap[:, 0:n_], Os0.ap[:, 0:n_], slc.ap[:, 0:1], rstd.ap[:, 0:n_], ALU.mult, ALU.mult,
                           r=[Os0, slc, rstd], w=[onb_])
                        dma(oT[h * 128:(h + 1) * 128, q0:q0 + n_], onb_.ap[:, 0:n_], r=[onb_], w=[DB["oT"]])
                    deferred.append((idx + 10, stepC))
                deferred.sort(key=lambda x: x[0])

            NTK = len(tasks)
            for i in range(NTK + LA):
                if i % 20 == 0:
                    next(pcast, None)
                if i < NTK:
                    stage1(tasks[i])
                j = i - LA
                if j >= 0:
                    while deferred and deferred[0][0] <= j:
                        deferred.pop(0)[1]()
                    stage2(tasks[j], j)
            while deferred:
                deferred.pop(0)[1]()
        for _ in pcast:
            pass
        kb.reset_arena()

    if "J" in kb.phases:
        out_proj(["oT"], 8, c_w_out, "hT2", "hT3")
    if "K2" in kb.phases:
        mlp(1, "hT3", "hT4")

    if "M" in kb.phases:
        xts = [A([128, 8, 512], F32) for _ in range(2)]
        sqs = [A([128, 8, 512], BF16) for _ in range(2)]
        rs = [A([128, 512], F32) for _ in range(2)]
        xn = [A([128, 8, 512], F32) for _ in range(2)]
        ostg = [A([128, D], F32) for _ in range(2)]
        src = fm(hT[4])
        oc_ = 0
        for gi, (p0, n) in enumerate(TG[1:]):
            xt = xts[gi % 2]
            dma(xt.ap, src[:, :, p0:p0 + n], r=[DB["hT4"]], w=[xt])
            xn_ = xn[gi % 2]
            norm_group(xt, n, 4, HnOut(xn_, lambda kc, xn_=xn_: xn_.ap[:, kc, :]), sqs[gi % 2], rs[gi % 2])
            for j in range(4):
                os_ = ostg[oc_ % 2]
                oc_ += 1
                for half in range(2):
                    ps = kb.psum()
                    for kk in range(4):
                        kc = half * 4 + kk
                        op("pe", "transpose", ps.ap[:, kk * 128:(kk + 1) * 128], xn_.ap[:, kc, j * 128:(j + 1) * 128], ident.ap,
                           r=[xn_, ident], w=[ps], inc=(kk == 3))
                    if half == 0:
                        op("act", "activation", out=os_.ap[:, 0:512], in_=ps.ap, func=AF.Copy, r=[ps], w=[os_])
                    else:
                        op("dve", "tensor_copy", os_.ap[:, 512:1024], ps.ap, r=[ps], w=[os_])
                r0 = p0 - NM + j * 128
                dma(out[r0:r0 + 128, :], os_.ap, r=[os_], w=[DB["out"]])
        kb.reset_arena()

    S.barrier()
    S.emit()
    return kb


def _consts():
    ident = np.eye(128, dtype=np.float32)
    tri = np.triu(np.ones((128, 128), dtype=np.float32))
    pos = np.arange(T, dtype=np.float32)
    inv_freq = np.power(np.float32(500000.0), -np.arange(0, 16, 2, dtype=np.float32) / np.float32(16)).astype(np.float32)
    ang = (pos[:, None] * inv_freq[None, :]).astype(np.float32)
    cos = np.cos(ang).astype(np.float32)
    sin = np.sin(ang).astype(np.float32)

    def tile(a):
        o = np.zeros((128, 33, 8), dtype=np.float32)
        o[0:16, 0] = a[0:16]
        o[:, 1:] = a[16:].reshape(32, 128, 8).transpose(1, 0, 2)
        return o
    return {"c_ident": ident, "c_tri": tri, "c_cos": tile(cos), "c_sin": tile(sin)}


def _core_inputs(inputs, b):
    f = lambda a: np.ascontiguousarray(np.asarray(a, dtype=np.float32))
    m = {
        "x": f(inputs["x"][b]),
        "meta_tokens": f(inputs["meta_tokens"]),
        "norm_mix": f(inputs["norm_mix"]),
        "norm_mlp": f(inputs["norm_mlp"]),
        "norm_final": f(inputs["norm_final"]).reshape(1, D),
        "ab_w_in": f(inputs["ab_w_in"][0]),
        "ab_if_bias": f(inputs["ab_if_bias"][0]).reshape(8, 1),
        "mlstm_norm": f(inputs["mlstm_norm"][0]).reshape(1, D),
        "lru_conv_w": f(inputs["lru_conv_w"][0]),
        "lru_conv_b": f(inputs["lru_conv_b"][0]).reshape(1, D),
        "lru_w_r": f(inputs["lru_w_r"][0]),
        "lru_b_r": f(inputs["lru_b_r"][0]).reshape(1, D),
        "lru_w_i": f(inputs["lru_w_i"][0]),
        "lru_b_i": f(inputs["lru_b_i"][0]).reshape(1, D),
        "lru_lambda": f(inputs["lru_lambda"][0]).reshape(1, D),
        "ab_w_out": f(inputs["ab_w_out"][0]),
        "c_w_in": f(inputs["c_w_in"][0]),
        "c_lambda": f(inputs["c_lambda"][0]).reshape(1, 256),
        "c_subln": f(inputs["c_subln"][0]).reshape(1, 128),
        "c_w_out": f(inputs["c_w_out"][0]),
        "mlp_w1": f(inputs["mlp_w1"]),
        "mlp_w2": f(inputs["mlp_w2"]),
    }
    m.update(_consts())
    return m


def kernel(**inputs):
    kb = build()
    maps = []
    for b in range(8):
        m = _core_inputs(inputs, b)
        maps.append({k: m[k] for k in kb.in_names})
    res = run_bass_kernel_spmd(kb.nc, maps, core_ids=list(range(8)))
    return np.stack([np.asarray(r["out"], dtype=np.float32) for r in res.results], axis=0)
```
